# Optimizing a Trainium2 kernel written in Bass

```python
import jax
import jax.numpy as jnp
from jax import lax
import numpy as np

D_MODEL = 4096
BATCH = 4
SEQ = 2048
DEPTH = 2
DEC_BATCH = 8
DEC_SEQ = 8
PAST_LEN = 16384
PAGE_SIZE = 128

N_DN_LAYERS = (DEPTH + 1) // 2
N_NSA_LAYERS = DEPTH // 2
EPS = 1e-6

DN_HEADS = 32
DN_DK = 128
DN_DV = 128
DN_CONV = 4
DN_CHUNK = 64
DN_QK = DN_HEADS * DN_DK
DN_V = DN_HEADS * DN_DV
DN_CONV_DIM = 2 * DN_QK + DN_V
DN_IN = DN_CONV_DIM + DN_V + 2 * DN_HEADS

NSA_HEADS = 32
NSA_KV = 4
NSA_REP = NSA_HEADS // NSA_KV
NSA_DK = 192
NSA_DV = 128
CMP_BLOCK = 32
CMP_STRIDE = 16
CMP_RATIO = CMP_BLOCK // CMP_STRIDE
CMP_HIDDEN = 256
SLC_BLOCK = 64
SLC_TOPN = 16
WINDOW = 512
NSA_QBLOCK = 32
NSA_Q = NSA_HEADS * NSA_DK
NSA_KW = NSA_KV * NSA_DK
NSA_VW = NSA_KV * NSA_DV
NSA_O = NSA_HEADS * NSA_DV
NSA_SPLITS = (NSA_Q, NSA_KW, NSA_VW, NSA_KW, NSA_VW, NSA_KW, NSA_VW, 3 * NSA_HEADS, NSA_O)
NSA_IN = sum(NSA_SPLITS)

kernel_name = 'hybrid_gdn_nsa_decoder_step'


def rmsnorm(x, w):
    xf = x.astype(jnp.float32)
    y = xf * lax.rsqrt(jnp.mean(xf * xf, axis=-1, keepdims=True) + EPS)
    return (y * w.astype(jnp.float32)).astype(x.dtype)


def l2norm(x):
    return x * lax.rsqrt(jnp.sum(x * x, axis=-1, keepdims=True) + EPS)


def masked_softmax(s, mask):
    s = jnp.where(mask, s, -jnp.inf)
    m = jnp.max(s, axis=-1, keepdims=True)
    m = jnp.where(jnp.isfinite(m), m, 0.0)
    p = jnp.exp(s - m)
    return p / jnp.maximum(jnp.sum(p, axis=-1, keepdims=True), 1e-30)


def causal_conv(buf, x, w):
    L = x.shape[1]
    xp = jnp.concatenate([buf.astype(x.dtype), x], axis=1)
    y = sum(xp[:, j:j + L] * w[j] for j in range(DN_CONV))
    return jax.nn.silu(y), xp[:, L:]


def gated_delta_chunked(q, k, v, g, beta, s0):
    B, T, H, Dk = q.shape
    Dv = v.shape[-1]
    C = min(DN_CHUNK, T)
    pad = (-T) % C
    if pad:
        pw = ((0, 0), (0, pad), (0, 0), (0, 0))
        q, k, v = [jnp.pad(t, pw) for t in (q, k, v)]
        g, beta = [jnp.pad(t, pw[:3]) for t in (g, beta)]
    N = (T + pad) // C

    def chunks(t):
        t = t.reshape((B, N, C, H) + t.shape[3:])
        return jnp.moveaxis(t, (1, 3), (0, 2))

    qc, kc, vc, bc = chunks(q), chunks(k), chunks(v), chunks(beta)
    gc = jnp.cumsum(chunks(g), axis=-1)
    idx = jnp.arange(C)
    lower = idx[:, None] >= idx[None, :]
    strict = idx[:, None] > idx[None, :]
    decay = jnp.exp(jnp.where(lower, gc[..., :, None] - gc[..., None, :], -jnp.inf))
    kb = kc * bc[..., None]
    lmat = jnp.where(strict, jnp.einsum('nbhcd,nbhsd->nbhcs', kb, kc) * decay, 0.0)
    amat = lmat + jnp.eye(C, dtype=lmat.dtype)
    rhs = jnp.concatenate([vc * bc[..., None], kb * jnp.exp(gc)[..., None]], axis=-1)
    sol = lax.linalg.triangular_solve(amat, rhs, left_side=True, lower=True, unit_diagonal=True)
    u, w = sol[..., :Dv], sol[..., Dv:]
    attn = jnp.einsum('nbhcd,nbhsd->nbhcs', qc, kc) * decay

    def step(s, xs):
        q_n, k_n, u_n, w_n, g_n, a_n = xs
        v_new = u_n - jnp.einsum('bhcd,bhde->bhce', w_n, s)
        o_n = (jnp.einsum('bhcd,bhde->bhce', q_n * jnp.exp(g_n)[..., None], s)
               + jnp.einsum('bhcs,bhse->bhce', a_n, v_new))
        g_last = g_n[..., -1]
        s = (s * jnp.exp(g_last)[..., None, None]
             + jnp.einsum('bhcd,bhce->bhde', k_n * jnp.exp(g_last[..., None] - g_n)[..., None], v_new))
        return s, o_n

    s_fin, o = lax.scan(step, s0, (qc, kc, u, w, gc, attn))
    o = jnp.moveaxis(o, (0, 2), (1, 3)).reshape(B, N * C, H, Dv)[:, :T]
    return o, s_fin


def deltanet_mixer(h, conv_buf, s0, w_in, conv_w, a_log, dt_bias, out_norm, w_out):
    f32 = jnp.float32
    B, L, _ = h.shape
    proj = h @ w_in
    qkv, z, b, a = jnp.split(proj, [DN_CONV_DIM, DN_CONV_DIM + DN_V, DN_CONV_DIM + DN_V + DN_HEADS], axis=-1)
    qkv, new_buf = causal_conv(conv_buf, qkv, conv_w)
    q, k, v = jnp.split(qkv.astype(f32), [DN_QK, 2 * DN_QK], axis=-1)
    q = l2norm(q.reshape(B, L, DN_HEADS, DN_DK)) * (DN_DK ** -0.5)
    k = l2norm(k.reshape(B, L, DN_HEADS, DN_DK))
    v = v.reshape(B, L, DN_HEADS, DN_DV)
    beta = jax.nn.sigmoid(b.astype(f32))
    g = -jnp.exp(a_log.astype(f32)) * jax.nn.softplus(a.astype(f32) + dt_bias.astype(f32))
    o, s_new = gated_delta_chunked(q, k, v, g, beta, s0.astype(f32))
    o = rmsnorm(o, out_norm) * jax.nn.silu(z.reshape(B, L, DN_HEADS, DN_DV).astype(f32))
    y = o.reshape(B, L, DN_V).astype(h.dtype) @ w_out
    return y, (s_new, new_buf)


def alibi_slopes():
    hh = jnp.arange(1, NSA_HEADS + 1, dtype=jnp.float32)
    return jnp.exp2(-8.0 * hh / NSA_HEADS).reshape(NSA_KV, NSA_REP)


def compress(rows, pe, w1, w2):
    B, L = rows.shape[:2]
    pad = (-L) % CMP_STRIDE
    rows = jnp.pad(rows, ((0, 0), (0, pad), (0, 0), (0, 0)))
    nseg = (L + pad) // CMP_STRIDE
    seg = rows.reshape((B, nseg, CMP_STRIDE) + rows.shape[2:])
    nc = nseg - CMP_RATIO + 1
    blocks = jnp.concatenate([seg[:, r:r + nc] for r in range(CMP_RATIO)], axis=2)
    hid = jax.nn.silu(jnp.einsum('bnlgd,ldh->bngh', blocks + pe[:, None, :], w1))
    return jnp.einsum('bngh,hd->bngd', hid, w2)


def cmp_to_slc(nc, nsb):
    i = jnp.arange(nc)[:, None] * CMP_STRIDE
    j = jnp.arange(nsb)[None, :] * SLC_BLOCK
    ov = jnp.minimum(i + CMP_BLOCK, j + SLC_BLOCK) - jnp.maximum(i, j)
    return jnp.maximum(ov, 0).astype(jnp.float32) / CMP_STRIDE


def slc_blocks(rows):
    B, L = rows.shape[:2]
    pad = (-L) % SLC_BLOCK
    rows = jnp.pad(rows, ((0, 0), (0, pad), (0, 0), (0, 0)))
    return rows.reshape((B, (L + pad) // SLC_BLOCK, SLC_BLOCK) + rows.shape[2:]).transpose(0, 3, 1, 2, 4)


def nsa_core(q, gates, tq, kc, vc, ec, ks, vs, kw, vw, tw):
    f32 = jnp.float32
    slopes = alibi_slopes()
    qf = q.astype(f32) * (NSA_DK ** -0.5)
    s = jnp.einsum('bqgrd,bngd->bqgrn', qf, kc.astype(f32))
    dist = (tq[:, None] - ec[None, :]).astype(f32)
    s = s - slopes[None, None, :, :, None] * dist[None, :, None, None, :]
    p_c = masked_softmax(s, (ec[None, :] <= tq[:, None])[None, :, None, None, :])
    o_c = jnp.einsum('bqgrn,bngd->bqgrd', p_c, vc.astype(f32))
    nsb = ks.shape[2]
    imp = jnp.einsum('bqgrn,nj->bqgj', p_c, cmp_to_slc(kc.shape[1], nsb))
    jb = jnp.arange(nsb)[None, :]
    cur = (tq // SLC_BLOCK)[:, None]
    forced = (jb == 0) | (jb == cur) | (jb == cur - 1)
    valid = jb * SLC_BLOCK <= tq[:, None]
    imp = jnp.where(forced[None, :, None, :], jnp.inf, jnp.where(valid[None, :, None, :], imp, -jnp.inf))
    _, idx = lax.top_k(imp, min(SLC_TOPN, nsb))
    B, Q, G = idx.shape[:3]
    bi = jnp.arange(B)[:, None, None, None]
    gi = jnp.arange(G)[None, None, :, None]
    ksel = ks[bi, gi, idx].astype(f32)
    vsel = vs[bi, gi, idx].astype(f32)
    pos = idx[..., None] * SLC_BLOCK + jnp.arange(SLC_BLOCK)
    s = jnp.einsum('bqgrd,bqgnkd->bqgrnk', qf, ksel)
    dsel = (tq[None, :, None, None, None] - pos).astype(f32)[:, :, :, None]
    s = s - slopes[None, None, :, :, None, None] * dsel
    smask = (pos <= tq[None, :, None, None, None]).reshape(B, Q, G, 1, -1)
    p_s = masked_softmax(s.reshape(B, Q, G, NSA_REP, -1), smask)
    o_s = jnp.einsum('bqgrm,bqgmd->bqgrd', p_s, vsel.reshape(B, Q, G, -1, vsel.shape[-1]))
    s = jnp.einsum('bqgrd,bwgd->bqgrw', qf, kw.astype(f32))
    dw = tq[:, None] - tw[None, :]
    s = s - slopes[None, None, :, :, None] * dw.astype(f32)[None, :, None, None, :]
    wmask = (dw >= 0) & (dw < WINDOW) & (tw[None, :] >= 0)
    p_w = masked_softmax(s, wmask[None, :, None, None, :])
    o_w = jnp.einsum('bqgrw,bwgd->bqgrd', p_w, vw.astype(f32))
    gf = gates.astype(f32)
    return gf[..., 0:1] * o_c + gf[..., 1:2] * o_s + gf[..., 2:3] * o_w


def nsa_project(h, w_in, q_norm, kn_s, kn_w):
    B, L, _ = h.shape
    parts = jnp.split(h @ w_in, np.cumsum(NSA_SPLITS)[:-1].tolist(), axis=-1)
    q, kc, vc, ks, vs, kw, vw, gl, z = parts
    q = rmsnorm(q.reshape(B, L, NSA_KV, NSA_REP, NSA_DK), q_norm)
    kc = kc.reshape(B, L, NSA_KV, NSA_DK)
    vc = vc.reshape(B, L, NSA_KV, NSA_DV)
    ks = rmsnorm(ks.reshape(B, L, NSA_KV, NSA_DK), kn_s)
    vs = vs.reshape(B, L, NSA_KV, NSA_DV)
    kw = rmsnorm(kw.reshape(B, L, NSA_KV, NSA_DK), kn_w)
    vw = vw.reshape(B, L, NSA_KV, NSA_DV)
    gates = jax.nn.sigmoid(gl.astype(jnp.float32)).reshape(B, L, NSA_KV, NSA_REP, 3)
    return q, kc, vc, ks, vs, kw, vw, gates, z


def nsa_compressed(kc_rows, vc_rows, kn_c, pe_k, w1_k, w2_k, pe_v, w1_v, w2_v):
    kc = rmsnorm(compress(kc_rows, pe_k, w1_k, w2_k), kn_c)
    vc = compress(vc_rows, pe_v, w1_v, w2_v)
    ec = jnp.arange(kc.shape[1]) * CMP_STRIDE + CMP_BLOCK - 1
    return kc, vc, ec


def nsa_prompt(h, w_in, q_norm, kn_c, kn_s, kn_w, pe_k, w1_k, w2_k, pe_v, w1_v, w2_v, w_out):
    B, T, _ = h.shape
    q, kc_r, vc_r, ks_r, vs_r, kw_r, vw_r, gates, z = nsa_project(h, w_in, q_norm, kn_s, kn_w)
    kc, vc, ec = nsa_compressed(kc_r, vc_r, kn_c, pe_k, w1_k, w2_k, pe_v, w1_v, w2_v)
    ks, vs = slc_blocks(ks_r), slc_blocks(vs_r)
    pw = ((0, 0), (WINDOW, 0), (0, 0), (0, 0))
    kw_pad, vw_pad = jnp.pad(kw_r, pw), jnp.pad(vw_r, pw)
    qb = min(NSA_QBLOCK, T)
    nb = T // qb

    def block(xs):
        i, q_i, g_i = xs
        start = i * qb
        tq = start + jnp.arange(qb)
        kw_i = lax.dynamic_slice_in_dim(kw_pad, start, WINDOW + qb, axis=1)
        vw_i = lax.dynamic_slice_in_dim(vw_pad, start, WINDOW + qb, axis=1)
        tw = start - WINDOW + jnp.arange(WINDOW + qb)
        return nsa_core(q_i, g_i, tq, kc, vc, ec, ks, vs, kw_i, vw_i, tw)

    qs = q.reshape(B, nb, qb, NSA_KV, NSA_REP, NSA_DK).swapaxes(0, 1)
    gs = gates.reshape(B, nb, qb, NSA_KV, NSA_REP, 3).swapaxes(0, 1)
    o = lax.map(block, (jnp.arange(nb), qs, gs))
    o = o.swapaxes(0, 1).reshape(B, T, NSA_O)
    y = (o * jax.nn.silu(z.astype(jnp.float32))).astype(h.dtype) @ w_out
    wl = min(WINDOW, T)
    return y, (kc_r, vc_r, ks_r, vs_r, kw_r[:, T - wl:], vw_r[:, T - wl:])


def gather_past(pool, page_table):
    g = pool[page_table]
    return g.reshape((g.shape[0], g.shape[1] * g.shape[2]) + g.shape[3:])


def nsa_sample(h, ck, cv, sk, sv, wk_buf, wv_buf, page_table,
               w_in, q_norm, kn_c, kn_s, kn_w, pe_k, w1_k, w2_k, pe_v, w1_v, w2_v, w_out):
    B, S, _ = h.shape
    P = page_table.shape[1] * ck.shape[1]
    q, kc_r, vc_r, ks_r, vs_r, kw_r, vw_r, gates, z = nsa_project(h, w_in, q_norm, kn_s, kn_w)
    kc_all = jnp.concatenate([gather_past(ck, page_table), kc_r], axis=1)
    vc_all = jnp.concatenate([gather_past(cv, page_table), vc_r], axis=1)
    kc, vc, ec = nsa_compressed(kc_all, vc_all, kn_c, pe_k, w1_k, w2_k, pe_v, w1_v, w2_v)
    ks = slc_blocks(jnp.concatenate([gather_past(sk, page_table), ks_r], axis=1))
    vs = slc_blocks(jnp.concatenate([gather_past(sv, page_table), vs_r], axis=1))
    kw = jnp.concatenate([wk_buf, kw_r], axis=1)
    vw = jnp.concatenate([wv_buf, vw_r], axis=1)
    wl = wk_buf.shape[1]
    tw = P - wl + jnp.arange(wl + S)
    tq = P + jnp.arange(S)
    o = nsa_core(q, gates, tq, kc, vc, ec, ks, vs, kw, vw, tw).reshape(B, S, NSA_O)
    y = (o * jax.nn.silu(z.astype(jnp.float32))).astype(h.dtype) @ w_out
    return y, (kc_r, vc_r, ks_r, vs_r, kw[:, S:], vw[:, S:])


def stack_field(group, n):
    return jnp.stack([st[n] for st in group])


def setup_inputs(seed: int = 0) -> dict:
    key = jax.random.key(seed)
    keys = jax.random.split(key, 40)
    counter = iter(range(40))
    f32 = jnp.float32

    def nk():
        return keys[next(counter)]

    def nrm(shape, scale=1.0):
        return jax.random.normal(nk(), shape, f32) * scale

    def gain(shape):
        return 1.0 + 0.02 * jax.random.normal(nk(), shape, f32)

    n_pages = PAST_LEN // PAGE_SIZE
    n_used = DEC_BATCH * n_pages
    n_pool = n_used + max(1, n_used // 4)
    win_len = min(WINDOW, PAST_LEN)
    LA, LB = N_DN_LAYERS, N_NSA_LAYERS
    return {
        'x_prompt': nrm((BATCH, SEQ, D_MODEL)),
        'x_sample': nrm((DEC_BATCH, DEC_SEQ, D_MODEL)),
        'state_delta': nrm((LA, DEC_BATCH, DN_HEADS, DN_DK, DN_DV), 0.1),
        'state_conv': nrm((LA, DEC_BATCH, DN_CONV - 1, DN_CONV_DIM)),
        'cache_cmp_k': nrm((LB, n_pool, PAGE_SIZE, NSA_KV, NSA_DK)),
        'cache_cmp_v': nrm((LB, n_pool, PAGE_SIZE, NSA_KV, NSA_DV)),
        'cache_slc_k': nrm((LB, n_pool, PAGE_SIZE, NSA_KV, NSA_DK)),
        'cache_slc_v': nrm((LB, n_pool, PAGE_SIZE, NSA_KV, NSA_DV)),
        'cache_win_k': nrm((LB, DEC_BATCH, win_len, NSA_KV, NSA_DK)),
        'cache_win_v': nrm((LB, DEC_BATCH, win_len, NSA_KV, NSA_DV)),
        'page_table': jax.random.permutation(nk(), n_pool)[:n_used].reshape(DEC_BATCH, n_pages).astype(jnp.int32),
        'norm_dn': gain((LA, D_MODEL)),
        'w_in_dn': nrm((LA, D_MODEL, DN_IN), D_MODEL ** -0.5),
        'conv_w_dn': nrm((LA, DN_CONV, DN_CONV_DIM), DN_CONV ** -0.5),
        'a_log_dn': jnp.log(jax.random.uniform(nk(), (LA, DN_HEADS), f32, minval=1.0, maxval=16.0)),
        'dt_bias_dn': gain((LA, DN_HEADS)),
        'out_norm_dn': gain((LA, DN_DV)),
        'w_out_dn': nrm((LA, DN_V, D_MODEL), DN_V ** -0.5),
        'norm_nsa': gain((LB, D_MODEL)),
        'w_in_nsa': nrm((LB, D_MODEL, NSA_IN), D_MODEL ** -0.5),
        'q_norm_nsa': gain((LB, NSA_DK)),
        'k_norm_cmp': gain((LB, NSA_DK)),
        'k_norm_slc': gain((LB, NSA_DK)),
        'k_norm_win': gain((LB, NSA_DK)),
        'cmp_pe_k': nrm((LB, CMP_BLOCK, NSA_DK), 0.1),
        'cmp_w1_k': nrm((LB, CMP_BLOCK, NSA_DK, CMP_HIDDEN), (CMP_BLOCK * NSA_DK) ** -0.5),
        'cmp_w2_k': nrm((LB, CMP_HIDDEN, NSA_DK), CMP_HIDDEN ** -0.5),
        'cmp_pe_v': nrm((LB, CMP_BLOCK, NSA_DV), 0.1),
        'cmp_w1_v': nrm((LB, CMP_BLOCK, NSA_DV, CMP_HIDDEN), (CMP_BLOCK * NSA_DV) ** -0.5),
        'cmp_w2_v': nrm((LB, CMP_HIDDEN, NSA_DV), CMP_HIDDEN ** -0.5),
        'w_out_nsa': nrm((LB, NSA_O, D_MODEL), NSA_O ** -0.5),
    }


def reference(x_prompt, x_sample, state_delta, state_conv, cache_cmp_k, cache_cmp_v,
              cache_slc_k, cache_slc_v, cache_win_k, cache_win_v, page_table,
              norm_dn, w_in_dn, conv_w_dn, a_log_dn, dt_bias_dn, out_norm_dn, w_out_dn,
              norm_nsa, w_in_nsa, q_norm_nsa, k_norm_cmp, k_norm_slc, k_norm_win,
              cmp_pe_k, cmp_w1_k, cmp_w2_k, cmp_pe_v, cmp_w1_v, cmp_w2_v, w_out_nsa):
    xp, xs = x_prompt, x_sample
    B = xp.shape[0]
    p_dn, s_dn, p_nsa, s_nsa = [], [], [], []
    for i in range(DEPTH):
        j = i // 2
        if i % 2 == 0:
            dw = (w_in_dn[j], conv_w_dn[j], a_log_dn[j], dt_bias_dn[j], out_norm_dn[j], w_out_dn[j])
            buf0 = jnp.zeros((B, DN_CONV - 1, DN_CONV_DIM), xp.dtype)
            st0 = jnp.zeros((B, DN_HEADS, DN_DK, DN_DV), jnp.float32)
            yp, pst = deltanet_mixer(rmsnorm(xp, norm_dn[j]), buf0, st0, *dw)
            ys, sst = deltanet_mixer(rmsnorm(xs, norm_dn[j]), state_conv[j], state_delta[j], *dw)
            p_dn.append(pst)
            s_dn.append(sst)
        else:
            nw = (w_in_nsa[j], q_norm_nsa[j], k_norm_cmp[j], k_norm_slc[j], k_norm_win[j],
                  cmp_pe_k[j], cmp_w1_k[j], cmp_w2_k[j], cmp_pe_v[j], cmp_w1_v[j], cmp_w2_v[j], w_out_nsa[j])
            yp, pst = nsa_prompt(rmsnorm(xp, norm_nsa[j]), *nw)
            ys, sst = nsa_sample(rmsnorm(xs, norm_nsa[j]), cache_cmp_k[j], cache_cmp_v[j],
                                 cache_slc_k[j], cache_slc_v[j], cache_win_k[j], cache_win_v[j],
                                 page_table, *nw)
            p_nsa.append(pst)
            s_nsa.append(sst)
        xp = xp + yp.astype(xp.dtype)
        xs = xs + ys.astype(xs.dtype)
    return (xp, xs,
            stack_field(p_dn, 0), stack_field(p_dn, 1),
            stack_field(p_nsa, 0), stack_field(p_nsa, 1), stack_field(p_nsa, 2),
            stack_field(p_nsa, 3), stack_field(p_nsa, 4), stack_field(p_nsa, 5),
            stack_field(s_dn, 0), stack_field(s_dn, 1),
            stack_field(s_nsa, 0), stack_field(s_nsa, 1), stack_field(s_nsa, 2),
            stack_field(s_nsa, 3), stack_field(s_nsa, 4), stack_field(s_nsa, 5))
```

```python
import math
from contextlib import ExitStack
import numpy as np
import ml_dtypes
import concourse.bass as bass
import concourse.mybir as mybir
from concourse.bass_utils import run_bass_kernel_spmd

F32 = mybir.dt.float32
BF16 = mybir.dt.bfloat16
AF = mybir.ActivationFunctionType
ALU = mybir.AluOpType
AX = mybir.AxisListType
EPS = 1e-6
ENGS = ("pe", "act", "dve", "pool", "sp")


class Prog:
    def __init__(self, ndma=40):
        self.streams = {e: [] for e in ENGS}
        self.cnt = {e: 0 for e in ("pe", "act", "dve", "pool")}
        self.waited = {e: {} for e in ENGS}
        self.lastw = {}
        self.readers = {}
        self.ndma = ndma
        self.dma_i = 0
        self.dma_final = {}
        self.floor = {e: {} for e in ENGS}
        import threading
        self._tl = threading.local()

    def barrier(self):
        snap = {k: v for k, v in self.cnt.items() if v}
        snap.update(self.dma_final)
        for e in ENGS:
            self.floor[e] = dict(snap)

    def _need(self, eng, reads, writes):
        need = dict(self.floor[eng])
        self.floor[eng] = {}

        def add(tok):
            k, v = tok
            if need.get(k, 0) < v:
                need[k] = v

        for r in reads:
            t = self.lastw.get(r)
            if t:
                add(t)
        for w in writes:
            t = self.lastw.get(w)
            if t:
                add(t)
            for k, v in self.readers.get(w, {}).items():
                add((k, v))
        out = []
        for k, v in need.items():
            if eng == "pe" and k == "pe":
                continue
            if self.waited[eng].get(k, 0) >= v:
                continue
            self.waited[eng][k] = v
            out.append((k, v))
        return out

    def _commit(self, tok, reads, writes):
        for w in writes:
            self.lastw[w] = tok
            self.readers[w] = {}
        for r in reads:
            if r not in writes:
                d = self.readers.setdefault(r, {})
                if d.get(tok[0], 0) < tok[1]:
                    d[tok[0]] = tok[1]

    def op(self, eng, fn, reads=(), writes=()):
        writes = list(writes) + [r for r in reads if r.startswith("ps") and r not in writes]
        waits = self._need(eng, reads, writes)
        self.cnt[eng] += 1
        tok = (eng, self.cnt[eng])
        self.streams[eng].append((waits, fn, eng, 1))
        self._commit(tok, reads, writes)
        self._yield()

    def dma(self, eng, fn, reads=(), writes=()):
        i = self.dma_i
        self.dma_i += 1
        slot, gen = i % self.ndma, i // self.ndma
        key = ("d", slot)
        waits = self._need(eng, reads, writes)
        if gen > 0 and self.waited[eng].get(key, 0) < 16 * gen:
            waits.append((key, 16 * gen))
            self.waited[eng][key] = 16 * gen
        tok = (key, 16 * (gen + 1))
        self.dma_final[key] = 16 * (gen + 1)
        self.streams[eng].append((waits, fn, key, 16))
        self._commit(tok, reads, writes)
        self._yield()

    def _yield(self):
        h = getattr(self._tl, "hook", None)
        if h is not None:
            h()

    def interleave(self, fa, fb):
        import threading
        sems = {"a": threading.Semaphore(0), "b": threading.Semaphore(0)}
        done = {"a": False, "b": False}
        errs = []

        def runner(f, me, other):
            sems[me].acquire()

            def hook():
                if not done[other]:
                    sems[other].release()
                    sems[me].acquire()
            self._tl.hook = hook
            try:
                f()
            except BaseException as ex:
                errs.append(ex)
            finally:
                done[me] = True
                self._tl.hook = None
                sems[other].release()

        ta = threading.Thread(target=runner, args=(fa, "a", "b"))
        tb = threading.Thread(target=runner, args=(fb, "b", "a"))
        ta.start()
        tb.start()
        sems["a"].release()
        ta.join()
        tb.join()
        if errs:
            raise errs[0]

    def emit(self, nc, stack):
        sems = {}
        for k in list(self.cnt.keys()) + list(self.dma_final.keys()):
            nm = k if isinstance(k, str) else "d%d" % k[1]
            sems[k] = stack.enter_context(nc.semaphore("s_" + nm))
        block = stack.enter_context(nc.Block())

        def run(name, e):
            for waits, fn, sk, inc in self.streams[name]:
                for k, v in waits:
                    e.wait_ge(sems[k], v)
                fn(e).then_inc(sems[sk], inc)
            if name == "sp":
                for k, v in self.dma_final.items():
                    e.wait_ge(sems[k], v)
                for k, v in self.cnt.items():
                    if v:
                        e.wait_ge(sems[k], v)

        @block.tensor
        def _(e):
            run("pe", e)

        @block.scalar
        def _(e):
            run("act", e)

        @block.vector
        def _(e):
            run("dve", e)

        @block.gpsimd
        def _(e):
            run("pool", e)

        @block.sync
        def _(e):
            run("sp", e)


def _bf(x):
    return np.asarray(x, dtype=np.float64).astype(ml_dtypes.bfloat16)


def _split_bf(x, n):
    x = np.asarray(x, dtype=np.float64)
    out = []
    for _ in range(n):
        h = x.astype(ml_dtypes.bfloat16)
        out.append(h)
        x = x - h.astype(np.float64)
    return out


def _kaug(pos):
    pos = np.asarray(pos, dtype=np.int64)
    ph, pl = 128 * (pos // 128), pos % 128
    one = np.ones_like(pos)
    return np.stack([ph, ph, pl, pl, one, one, one]).astype(np.float64).astype(ml_dtypes.bfloat16)


def _qaug(slopes, tq):
    HN, Tq = len(slopes), len(tq)
    sh, sl = _split_bf(slopes, 2)
    sp = sh.astype(np.float64) + sl.astype(np.float64)
    n3 = _split_bf(-sp[:, None] * np.asarray(tq, np.float64)[None, :], 3)
    b = lambda v: np.broadcast_to(v[:, None], (HN, Tq))
    return np.stack([b(sh), b(sl), b(sh), b(sl), n3[0], n3[1], n3[2]]).astype(ml_dtypes.bfloat16)


def nsa_consts_sample(HN, NPG, TS=8):
    c = {}
    hh = np.arange(1, HN + 1, dtype=np.float32)
    slopes = np.exp2(np.float32(-8.0) * hh / np.float32(HN)).astype(np.float32)
    P0 = NPG * 128
    LS, NCS, NBP = P0 + 16, NPG * 8, 2 * NPG
    NSB = NBP + 1
    c["kaugS"] = np.ascontiguousarray(_kaug(np.arange(LS)))
    c["kaugCS"] = np.ascontiguousarray(_kaug(16 * np.arange(NCS) + 31))
    c["qaugS"] = np.ascontiguousarray(_qaug(slopes, P0 + np.arange(TS)))
    kl = np.arange(128)[:, None]
    q = (np.arange(64) % TS)[None, :]
    m = np.zeros((128, 3, 64), np.float32)
    m[:, 0, :] = np.where(kl <= q, 0.0, -30000.0)
    m[:, 1, :] = np.where(kl > q, 0.0, -30000.0)
    m[:, 2, :] = np.where(16 * (NCS - 128 + kl) + 31 <= P0 + q, 0.0, -30000.0)
    c["maskS"] = m.astype(ml_dtypes.bfloat16)
    i_ = np.arange(NCS)[:, None] * 16
    jb = np.arange(NSB)[None, :] * 64
    ov = (np.maximum(np.minimum(i_ + 32, jb + 64) - np.maximum(i_, jb), 0) / 16.0).astype(np.float32)
    c["movS"] = np.ascontiguousarray(ov.reshape(NCS // 128, 128, NSB).transpose(1, 0, 2)).astype(ml_dtypes.bfloat16)
    tq = (P0 + np.arange(TS))[:, None]
    jb = np.arange(NSB)[None, :]
    cur = tq // 64
    forced = (jb == 0) | (jb == cur) | (jb == cur - 1)
    valid = jb * 64 <= tq
    c["fmS"] = np.where(forced, 1e30, -1e30).astype(np.float32)
    c["vmS"] = np.where((~valid) & (~forced), -1e30, 1e30).astype(np.float32)
    c["iota_c"] = np.arange(128, dtype=np.float32).reshape(128, 1)
    return c


def nsa_consts(T, HN):
    c = {}
    hh = np.arange(1, HN + 1, dtype=np.float32)
    slopes = np.exp2(np.float32(-8.0) * hh / np.float32(HN)).astype(np.float32)
    c["kaugP"] = np.ascontiguousarray(_kaug(np.arange(T)))
    c["kaugC"] = np.ascontiguousarray(_kaug(16 * np.arange(128) + 31))
    c["qaugP"] = np.ascontiguousarray(_qaug(slopes, np.arange(T)))
    NQT = T // 512
    kl = np.arange(128)[:, None]
    ql = np.arange(512)[None, :]
    m = np.zeros((128, 8 + NQT, 512), np.float32)
    for d in range(4):
        m[:, d, :] = np.where(128 * d + kl <= ql, 0.0, -30000.0)
    for i, d in enumerate(range(-4, 0)):
        m[:, 4 + i, :] = np.where(ql - (128 * d + kl) < 512, 0.0, -30000.0)
    for qt in range(NQT):
        m[:, 8 + qt, :] = np.where((16 * kl + 31 <= 512 * qt + ql) & (kl < T // 16 - 1), 0.0, -30000.0)
    c["maskP"] = m.astype(ml_dtypes.bfloat16)
    j = np.arange(128)[:, None, None]
    mm_ = np.arange(64)[None, :, None]
    key = np.arange(128)[None, None, :]
    c["eone"] = (j == 2 * mm_ + key // 64).astype(np.float32).astype(ml_dtypes.bfloat16)
    ncp, nsb = T // 16 - 1, T // 64
    i_ = np.arange(128)[:, None] * 16
    jb = np.arange(nsb)[None, :] * 64
    ov = np.maximum(np.minimum(i_ + 32, jb + 64) - np.maximum(i_, jb), 0) / 16.0
    ov[ncp:] = 0
    c["movP"] = ov.astype(np.float32).astype(ml_dtypes.bfloat16)
    t = np.arange(T)[:, None]
    jb = np.arange(nsb)[None, :]
    cur = t // 64
    forced = (jb == 0) | (jb == cur) | (jb == cur - 1)
    valid = jb * 64 <= t
    fm = np.where(forced, 1e30, -1e30).astype(np.float32)
    vm = np.where((~valid) & (~forced), -1e30, 1e30).astype(np.float32)
    lay = lambda a: np.ascontiguousarray(a.reshape(T // 128, 128, nsb).transpose(1, 0, 2))
    c["fmP"], c["vmP"] = lay(fm), lay(vm)
    return c


def host_consts():
    i = np.arange(128)
    same = (i[:, None] // 64) == (i[None, :] // 64)
    c = {}
    c["ident"] = np.eye(128, dtype=np.float32)
    c["tri"] = (same & (i[:, None] <= i[None, :])).astype(np.float32)
    c["stri"] = (same & (i[:, None] < i[None, :])).astype(np.float32)
    c["last"] = (same & ((i[:, None] % 64) == 63)).astype(np.float32)
    lastS = np.zeros((128, 128), np.float32)
    lastS[7, :] = 1.0
    c["lastS"] = lastS
    c["ones"] = np.ones((128, 128), np.float32)
    return c


class _Stop(Exception):
    pass


class V:
    def __init__(self, ap, name):
        self.ap, self.name = ap, name

    def __getitem__(self, k):
        return self.ap[k]


class Carver:
    fallback = None
    nfb = 0

    def __init__(self, regions):
        self.regions = regions
        self.reset()

    def reset(self):
        self.off = [0] * len(self.regions)

    def alloc(self, name, shape, dt=F32):
        n = 1
        for d_ in shape[1:]:
            n *= d_
        n32 = (n + 1) // 2 if dt == BF16 else n
        for i, r in enumerate(self.regions):
            if self.off[i] + n32 <= r.shape[1]:
                v = r[0:shape[0], self.off[i]:self.off[i] + n32]
                self.off[i] += n32
                if dt != F32:
                    v = v.bitcast(dt)[:, 0:n]

                if len(shape) == 3:
                    v = v.rearrange("p (a b) -> p a b", b=shape[2])
                elif len(shape) == 4:
                    v = v.rearrange("p (a b c) -> p a b c", b=shape[2], c=shape[3])
                return V(v, name)
        if Carver.fallback is not None:
            Carver.nfb += 1
            return V(Carver.fallback("%s_fb%d" % (name, Carver.nfb), shape, dt)[:], name)
        raise RuntimeError("carver out of space for %s %s" % (name, shape))


def build_program(cfg):
    nc, P, st = bass.Bass("TRN2", target_bir_lowering=False), Prog(), ExitStack()
    try:
        _build(cfg, nc, P, st)
    except _Stop:
        pass
    P.emit(nc, st)
    st.close()
    return nc


def _build(cfg, nc, P, st):
    D, H, T = cfg["D"], cfg["H"], cfg["T"]
    stop = cfg.get("stop", 99)
    TS = 8
    TT = T + TS
    KC = D // 128
    QK = H * 128
    CONV = 3 * QK
    DN_IN = CONV + QK + 2 * H
    TB = cfg.get("TB", 256)
    assert T % 128 == 0 and T % TB == 0
    NT = T // 128

    def din(name, shape, dt=F32):
        return nc.dram_tensor(name, list(shape), dt, kind="ExternalInput").ap()

    def dout(name, shape, dt=F32):
        return nc.dram_tensor(name, list(shape), dt, kind="ExternalOutput").ap()

    def dscr(name, shape, dt):
        return nc.dram_tensor(name, list(shape), dt, kind="Internal").ap()

    def sb(name, shape, dt=F32):
        return st.enter_context(nc.sbuf_tensor(name, list(shape), dt))

    def ps(name, shape, dt=F32):
        return st.enter_context(nc.psum_tensor(name, list(shape), dt))

    x_d = din("x", [TT, D])
    wnb_d = din("norm_dn_b", [128, D])
    win_d = din("w_in_dn", [D, DN_IN])
    convw_d = din("conv_w", [128, 3 * H, 4])
    conv0_d = din("conv0_in", [128, 3 * H, 3])
    s0_d = din("s0", [H, 128, 128])
    alog_d = din("a_log_b", [128, H])
    dtb_d = din("dt_bias_b", [128, H])
    onw_d = din("out_norm_c", [128, 1])
    wout_d = din("w_out_dn", [QK, D])
    cst = {k: din("c_" + k, [128, 128]) for k in ("ident", "tri", "stri", "last", "lastS", "ones")}

    o_sdP = dout("o_sdP", [H, 128, 128])
    o_sdS = dout("o_sdS", [H, 128, 128])
    o_scP = dout("o_scP", [128, 3 * H, 3])
    o_scS = dout("o_scS", [128, 3 * H, 3])
    o_x1 = dout("o_x1", [TT, D])

    o_dbg = dout("o_dbg", [10, 128, T]) if cfg.get("dbg") else None
    G = cfg.get("G", 4)
    HN = G * 8
    DKN, DVN = 192, 128
    NQ, KW_, VW_ = HN * DKN, G * DKN, G * DVN
    c_kc = NQ
    c_vc = c_kc + KW_
    c_ks = c_vc + VW_
    c_vs = c_ks + KW_
    c_kw = c_vs + VW_
    c_vw = c_kw + KW_
    c_gl = c_vw + VW_
    c_z = c_gl + 3 * HN
    NSA_IN = c_z + HN * DVN
    WL = min(512, T)
    WLC = 512
    wnb2_d = din("norm_nsa_b", [128, D])
    win2_d = din("w_in_nsa", [D, NSA_IN])
    knb_d = din("knorm_b", [128, 2, DKN])
    cwk_d = din("cwin_k", [WLC, KW_])
    cwv_d = din("cwin_v", [WLC, VW_])
    o_pck = dout("o_pck", [T, KW_]); o_pcv = dout("o_pcv", [T, VW_])
    o_psk = dout("o_psk", [T, KW_]); o_psv = dout("o_psv", [T, VW_])
    o_pwk = dout("o_pwk", [WL, KW_]); o_pwv = dout("o_pwv", [WL, VW_])
    o_sck = dout("o_sck", [TS, KW_]); o_scv = dout("o_scv", [TS, VW_])
    o_ssk = dout("o_ssk", [TS, KW_]); o_ssv = dout("o_ssv", [TS, VW_])
    o_swk = dout("o_swk", [WLC, KW_]); o_swv = dout("o_swv", [WLC, VW_])
    h1T_d = dscr("h1T_scr", [(TT + 127) // 128, 128, KC * 128], BF16)
    kv_s = {nm: dscr(nm + "_scr", [TT, w_], BF16) for nm, w_ in (("kc", KW_), ("vc", VW_), ("ks", KW_), ("vs", VW_), ("kw", KW_), ("vw", VW_))}
    gate_s = dscr("gate_scr", [TT, 3 * HN], F32)
    qnw_d = din("qnorm_c", [128, 2])
    NQT = T // 512
    NCP = T // 16 - 1
    w1k_d = din("w1k_t", [DKN, 32, 256])
    w1v_d = din("w1v_t", [DVN, 32, 256])
    w2k_d = din("w2k", [256, DKN])
    w2v_d = din("w2v", [256, DVN])
    pek_d = din("pek_t", [DKN, 32])
    pev_d = din("pev_t", [DVN, 32])
    kncb_d = din("knc_b", [128, DKN])
    kaugP_d = din("kaugP", [7, T], BF16)
    kaugC_d = din("kaugC", [7, 128], BF16)
    qaugP_d = din("qaugP", [7, HN, T], BF16)
    maskP_d = din("maskP", [128, 8 + NQT, 512], BF16)
    eone_d = din("eone", [128, 64, 128], BF16)
    movP_d = din("movP", [128, 32], BF16)
    fmP_d = din("fmP", [128, T // 128, 32])
    vmP_d = din("vmP", [128, T // 128, 32])
    wout2_d = din("w_out_nsa", [HN * DVN, D])
    o_y = dout("o_y", [TT, D])
    fmK = {nm: dscr(nm + "T_scr", [G, DKN, T], BF16) for nm in ("kc", "ks", "kw")}
    fmV = dscr("vcT_scr", [G, DVN, T], BF16)
    ckT_s = dscr("ckT_scr", [G, DKN, 128], BF16)
    cv_s = dscr("cv_scr", [G, 128, DVN], BF16)
    NPG = cfg.get("NPG", 128)
    NPOOL = cfg.get("NPOOL", 1280)
    P0 = NPG * 128
    LS = P0 + 16
    NCS = NPG * 8
    NBP = 2 * NPG
    NSB = NBP + 1
    NCK = (NBP + 127) // 128
    pt_d = din("page_tab", [1, NPG], mybir.dt.int32)
    iota_d = din("iota_c", [128, 1])
    pool_d = {"kc": din("pool_ck", [NPOOL * 128, KW_]), "vc": din("pool_cv", [NPOOL * 128, VW_]),
              "ks": din("pool_sk", [NPOOL * 128, KW_]), "vs": din("pool_sv", [NPOOL * 128, VW_])}
    kaugS_d = din("kaugS", [7, LS], BF16)
    kaugCS_d = din("kaugCS", [7, NCS], BF16)
    qaugS_d = din("qaugS", [7, HN, TS], BF16)
    maskS_d = din("maskS", [128, 3, 64], BF16)
    movS_d = din("movS", [128, NCS // 128, NSB], BF16)
    fmS_d = din("fmS", [TS, NSB])
    vmS_d = din("vmS", [TS, NSB])
    sK = {nm: dscr("s" + nm + "T_scr", [G, DKN, LS], BF16) for nm in ("kc", "ks")}
    sKw = dscr("skwT_scr", [G, DKN, WLC + TS], BF16)
    sVcT = dscr("svcT_scr", [G, DVN, LS], BF16)
    sVs = dscr("svs_scr", [LS, VW_], BF16)
    sVw = dscr("svw_scr", [WLC + TS, VW_], BF16)
    ckTS_s = dscr("ckTS_scr", [G, DKN, NCS], BF16)
    cvS_s = dscr("cvS_scr", [G, NCS, DVN], BF16)
    qT_s = dscr("qT_scr", [HN, DKN, TT], BF16)
    zs_s = dscr("zs_scr", [HN, DVN, TT], BF16)
    gT_s = dscr("gT_scr", [3 * HN, TT], F32)
    NTA_ = (TT + 127) // 128
    hT_d = dscr("hT_scr", [NTA_, 128, KC * 128], BF16)
    hblk = lambda d_, it_: d_[it_].rearrange("p (k t) -> p k t", t=128)
    og_d = dscr("og_scr", [NTA_, 128, H * 128], BF16)

    ident_f = sb("ident_f", [128, 128])
    ident_b = sb("ident_b", [128, 128], BF16)
    tri_f = sb("tri_f", [128, 128])
    stri_f = sb("stri_f", [128, 128])
    last_f = sb("last_f", [128, 128])
    lastS_f = sb("lastS_f", [128, 128])
    ones_f = sb("ones_f", [128, 128])
    ones_b = sb("ones_b", [128, 128], BF16)
    convw = sb("convw", [128, 3 * H, 4])
    conv0 = sb("conv0", [128, 3 * H, 3])
    convoP = sb("convoP", [128, 3 * H, 3])
    convoS = sb("convoS", [128, 3 * H, 3])
    alog = sb("alog", [128, H])
    dtb = sb("dtb", [128, H])
    onw = sb("onw", [128, 1])

    for t_, nm in ((ident_f, "ident"), (tri_f, "tri"), (stri_f, "stri"), (last_f, "last"), (lastS_f, "lastS"), (ones_f, "ones")):
        P.dma("sp", (lambda e, a=t_, b=cst[nm]: e.dma_start(out=a[:], in_=b[:, :])), writes=[t_.name])
    P.dma("pool", lambda e: e.dma_start(out=ident_b[:], in_=cst["ident"][:, :]), writes=["ident_b"])
    P.dma("pool", lambda e: e.dma_start(out=ones_b[:], in_=cst["ones"][:, :]), writes=["ones_b"])
    tri_b = sb("tri_b", [128, 128], BF16)
    last_b = sb("last_b", [128, 128], BF16)
    lastS_b = sb("lastS_b", [128, 128], BF16)
    P.dma("pool", lambda e: e.dma_start(out=tri_b[:], in_=cst["tri"][:, :]), writes=["tri_b"])
    P.dma("pool", lambda e: e.dma_start(out=last_b[:], in_=cst["last"][:, :]), writes=["last_b"])
    P.dma("pool", lambda e: e.dma_start(out=lastS_b[:], in_=cst["lastS"][:, :]), writes=["lastS_b"])
    for t_, d_ in ((alog, alog_d), (dtb, dtb_d), (onw, onw_d)):
        P.dma("sp", (lambda e, a=t_, b=d_: e.dma_start(out=a[:], in_=b[:, :])), writes=[t_.name])
    for t_, d_ in ((convw, convw_d), (conv0, conv0_d)):
        P.dma("sp", (lambda e, a=t_, b=d_: e.dma_start(out=a[:], in_=b[:, :, :])), writes=[t_.name])

    psG = [ps("psG%d" % i, [128, 512]) for i in range(2)]
    psT = ps("psT", [128, 1024], BF16)
    psM = [ps("psM%d" % i, [128, 512]) for i in range(4)]
    psS = ps("psS", [128, 512])

    AW = max(2 * D + D // 2 + KC * 64, 3 * (T + 3) + 3 * T)
    arena = sb("arena", [128, AW])
    wnb = arena[:, 0:D]
    xt = [arena[:, D:2 * D]] * 2
    xb = arena[:, 2 * D:2 * D + D // 2].bitcast(BF16)
    junk = xb
    hTt = arena[:, 2 * D + D // 2:2 * D + D // 2 + KC * 64].bitcast(BF16).rearrange("p (k t) -> p k t", t=128)
    ntile_all = (TT + 127) // 128
    ssq = sb("ssq", [128, ntile_all])

    def rms_phase(src_d, w_d, dst_d, dkey):
        P.dma("sp", (lambda e: e.dma_start(out=wnb, in_=w_d[:, :])), writes=["wnb"])
        P.op("dve", lambda e: e.memset(ssq[:], 0.0), writes=["ssq"])
        for it in range(ntile_all):
            r0 = it * 128
            nr = min(128, TT - r0)
            xx = xt[it % 2]
            P.dma("sp", (lambda e, xx=xx, r0=r0, nr=nr: e.dma_start(out=xx[:nr], in_=src_d[r0:r0 + nr, :])), reads=["x1src"], writes=["xt"])
            sl = ssq[:nr, it:it + 1]
            P.op("act", (lambda e, xx=xx, nr=nr, sl=sl: e.activation(out=junk[:nr], in_=xx[:nr], func=AF.Square, accum_out=sl)),
                 reads=["xt", "ssq"], writes=["xb", "ssq"])
            P.op("act", (lambda e, sl=sl: e.activation(out=sl, in_=sl, func=AF.Sqrt, scale=1.0 / D, bias=EPS)), reads=["ssq"], writes=["ssq"])
            P.op("dve", (lambda e, sl=sl: e.reciprocal(out=sl, in_=sl)), reads=["ssq"], writes=["ssq"])
            P.op("dve", (lambda e, xx=xx, nr=nr, sl=sl: e.scalar_tensor_tensor(out=xb[:nr], in0=xx[:nr], scalar=sl, in1=wnb[:nr], op0=ALU.mult, op1=ALU.mult)),
                 reads=["xt", "ssq", "wnb"], writes=["xb"])
            for k0 in range(0, KC, 8):
                kn = min(8, KC - k0)
                for k in range(kn):
                    P.op("pe", (lambda e, k=k, k0=k0, nr=nr: e.transpose(out=psT[:, k * 128:k * 128 + nr], in_=xb[:nr, (k0 + k) * 128:(k0 + k + 1) * 128], identity=ident_b[:nr, :nr])),
                         reads=["xb", "ident_b"], writes=["psT"])
                P.op("act", (lambda e, k0=k0, kn=kn, nr=nr: e.activation(out=hTt[:, k0:k0 + kn, :nr], in_=psT[:, 0:kn * 128].rearrange("p (k t) -> p k t", t=128)[:, :, :nr], func=AF.Copy)),
                     reads=["psT"], writes=["hTt"])
            P.dma("sp", (lambda e, r0=r0, nr=nr: e.dma_start(out=hblk(dst_d, r0 // 128)[:, :, :nr], in_=hTt[:, :, :nr])), reads=["hTt"], writes=[dkey])

    rms_phase(x_d, wnb_d, hT_d, "hT_d")

    if stop <= 1:
        raise _Stop()
    hTb = [sb("hTb%d" % i, [128, KC, TB], BF16) for i in range(2)]
    wba = sb("wba", [128, KC, 2 * H], BF16)
    win_r = win_d.rearrange("(kc p) n -> p kc n", p=128)
    P.dma("pool", lambda e: e.dma_start(out=wba[:], in_=win_r[:, :, CONV + QK:CONV + QK + 2 * H]), writes=["wba"])
    NTA = ntile_all
    beta_tm = sb("beta_tm", [128, NTA, H])
    g_tm = sb("g_tm", [128, NTA, H])
    gc_tm = sb("gc_tm", [128, NTA, H])
    ngc_tm = sb("ngc_tm", [128, NTA, H])
    bg_tm = sb("bg_tm", [128, NTA, H])
    ed_tm = sb("ed_tm", [128, NTA, H])
    tmpH = sb("tmpH", [128, H])
    tmpR = sb("tmpR", [128, H])
    gk_b = [sb("gk_b%d" % i, [128, NTA, H], BF16) for i in range(3)]
    gk_f = [sb("gk_f%d" % i, [128, NTA, H]) for i in range(3)]
    bk_f = [sb("bk_f%d" % i, [128, NTA, H]) for i in range(2)]
    ck_b = [sb("ck_b%d" % i, [128, H], BF16) for i in range(3)]
    for t_ in gk_b + gk_f + bk_f:
        P.op("pool", (lambda e, t_=t_: e.memset(t_[:], 0.0)), writes=[t_.name])

    def split3(src_ap, outs_b, outs_f, nr, key_src, keys_b, keys_f, n=3):
        cur = src_ap
        curk = key_src
        for k in range(n):
            ob = outs_b[k]
            P.op("dve", (lambda e, ob=ob, cur=cur: e.tensor_copy(out=ob, in_=cur)), reads=[curk], writes=[keys_b[k]])
            if outs_f is not None:
                of = outs_f[k]
                P.op("dve", (lambda e, ob=ob, of=of: e.tensor_copy(out=of, in_=ob)), reads=[keys_b[k]], writes=[keys_f[k]])
            if k < n - 1:
                P.op("dve", (lambda e, ob=ob, cur=cur: e.tensor_tensor(out=tmpR[:nr], in0=cur, in1=ob, op=ALU.subtract)), reads=[curk, keys_b[k]], writes=["tmpR"])
                cur = tmpR[:nr]
                curk = "tmpR"

    nea = sb("nea", [128, H])
    for t_ in (beta_tm, g_tm, gc_tm, ngc_tm, bg_tm, ed_tm):
        P.op("pool", (lambda e, t_=t_: e.memset(t_[:], 0.0)), writes=[t_.name])
    P.op("act", lambda e: e.activation(out=nea[:], in_=alog[:], func=AF.Exp), reads=["alog"], writes=["nea"])
    P.op("dve", lambda e: e.tensor_scalar(out=nea[:], in0=nea[:], scalar1=-1.0, scalar2=None, op0=ALU.mult), reads=["nea"], writes=["nea"])

    blocks = [(b * TB, TB) for b in range(T // TB)] + [(T, TS)]
    bi = [0]
    for (c0, n) in blocks:
        hb = hTb[bi[0] % 2]
        bi[0] += 1
        P.dma("sp", (lambda e, hb=hb, c0=c0, n=n: e.dma_start(out=hb[:, :, :n], in_=hblk(hT_d, c0 // 128)[:, :, :n])), reads=["hT_d"], writes=[hb.name])
        for t0 in range(0, n, 128):
            nr = min(128, n - t0)
            it = (c0 + t0) // 128
            pm = psM[it % 2]
            for k in range(KC):
                P.op("pe", (lambda e, hb=hb, k=k, t0=t0, nr=nr, pm=pm: e.matmul(pm[:nr, 0:2 * H], lhsT=hb[:, k, t0:t0 + nr], rhs=wba[:, k, :], start=(k == 0), stop=(k == KC - 1))),
                     reads=[hb.name, "wba"], writes=[pm.name])
            P.op("act", (lambda e, pm=pm, nr=nr, it=it: e.activation(out=beta_tm[:nr, it, :], in_=pm[:nr, 0:H], func=AF.Sigmoid)), reads=[pm.name], writes=["beta_tm"])
            P.op("dve", (lambda e, pm=pm, nr=nr: e.tensor_tensor(out=tmpH[:nr], in0=pm[:nr, H:2 * H], in1=dtb[:nr], op=ALU.add)), reads=[pm.name, "dtb"], writes=["tmpH"])
            P.op("act", (lambda e, nr=nr: e.activation(out=tmpH[:nr], in_=tmpH[:nr], func=AF.Exp)), reads=["tmpH"], writes=["tmpH"])
            P.op("act", (lambda e, nr=nr: e.activation(out=tmpH[:nr], in_=tmpH[:nr], func=AF.Ln, bias=1.0)), reads=["tmpH"], writes=["tmpH"])
            P.op("dve", (lambda e, nr=nr, it=it: e.tensor_tensor(out=g_tm[:nr, it, :], in0=tmpH[:nr], in1=nea[:nr], op=ALU.mult)), reads=["tmpH", "nea"], writes=["g_tm"])
            pc = psM[2 + it % 2]
            split3(g_tm[:nr, it, :], [t_[:nr, it, :] for t_ in gk_b], [t_[:nr, it, :] for t_ in gk_f], nr, "g_tm",
                   [t_.name for t_ in gk_b], [t_.name for t_ in gk_f])
            split3(beta_tm[:nr, it, :], [ck_b[0][:nr, :], ck_b[1][:nr, :]], [t_[:nr, it, :] for t_ in bk_f], nr, "beta_tm",
                   [ck_b[0].name, ck_b[1].name], [t_.name for t_ in bk_f], n=2)
            for k3 in range(3):
                P.op("pe", (lambda e, nr=nr, it=it, pc=pc, k3=k3: e.matmul(pc[:nr, 0:H], lhsT=tri_b[:nr, :nr], rhs=gk_b[k3][:nr, it, :], start=(k3 == 0), stop=(k3 == 2))),
                     reads=["tri_b", gk_b[k3].name], writes=[pc.name])
            P.op("act", (lambda e, nr=nr, it=it, pc=pc: e.activation(out=gc_tm[:nr, it, :], in_=pc[:nr, 0:H], func=AF.Copy)), reads=[pc.name], writes=["gc_tm"])
            P.op("dve", (lambda e, nr=nr, it=it, pc=pc: e.tensor_scalar(out=ngc_tm[:nr, it, :], in0=pc[:nr, 0:H], scalar1=-1.0, scalar2=None, op0=ALU.mult)), reads=[pc.name], writes=["ngc_tm"])
            P.op("act", (lambda e, nr=nr, it=it, pc=pc: e.activation(out=bg_tm[:nr, it, :], in_=pc[:nr, 0:H], func=AF.Exp)), reads=[pc.name], writes=["bg_tm"])
            P.op("dve", (lambda e, nr=nr, it=it: e.tensor_tensor(out=bg_tm[:nr, it, :], in0=bg_tm[:nr, it, :], in1=beta_tm[:nr, it, :], op=ALU.mult)), reads=["bg_tm", "beta_tm"], writes=["bg_tm"])
            lastm = last_b if nr == 128 else lastS_b
            split3(gc_tm[:nr, it, :], [t_[:nr, :] for t_ in ck_b], None, nr, "gc_tm", [t_.name for t_ in ck_b], None)
            for k3 in range(3):
                P.op("pe", (lambda e, nr=nr, it=it, pc=pc, lastm=lastm, k3=k3: e.matmul(pc[:nr, H:2 * H], lhsT=lastm[:nr, :nr], rhs=ck_b[k3][:nr, :], start=(k3 == 0), stop=(k3 == 2))),
                     reads=[lastm.name, ck_b[k3].name], writes=[pc.name])
            P.op("dve", (lambda e, nr=nr, it=it, pc=pc: e.tensor_tensor(out=ed_tm[:nr, it, :], in0=pc[:nr, H:2 * H], in1=gc_tm[:nr, it, :], op=ALU.subtract)), reads=[pc.name, "gc_tm"], writes=["ed_tm"])
            P.op("act", (lambda e, nr=nr, it=it: e.activation(out=ed_tm[:nr, it, :], in_=ed_tm[:nr, it, :], func=AF.Exp)), reads=["ed_tm"], writes=["ed_tm"])

    if stop <= 2:
        raise _Stop()
    wh = sb("wh", [128, KC, 4, 128], BF16)
    P.barrier()
    raw = [arena[:, j * (T + 3):(j + 1) * (T + 3)] for j in range(3)]
    rawn = ["raw0", "raw1", "raw2"]
    o3 = 3 * (T + 3)
    zT = arena[:, o3:o3 + T]
    cacc = arena[:, o3 + T:o3 + 2 * T]
    QT = sb("QT", [128, T], BF16)
    KT = sb("KT", [128, T], BF16)
    VT = sb("VT", [128, T], BF16)
    sqb = sb("sqb", [128, T], BF16)
    Gbc = sb("Gbc", [128, T])
    Ebc = sb("Ebc", [128, T])
    Bbc = sb("Bbc", [128, T])
    QgT = sb("QgT", [128, T], BF16)
    KbT = sb("KbT", [128, T], BF16)
    oT = arena[:, o3 + 2 * T:o3 + 3 * T]
    ogT = sb("ogT", [128, T], BF16)
    rhsk = [sb("rhsk%d" % i, [128, 128], BF16) for i in range(5)]
    S32 = sb("S32", [128, 128])
    Sbf = sb("Sbf", [128, 128], BF16)
    dbl = lambda nm, shape, dt=F32: [sb("%s%d" % (nm, i), shape, dt) for i in range(2)]
    decT = dbl("decT", [128, 128])
    decTs = dbl("decTs", [128, 128])
    Am = dbl("Am", [128, 128], BF16)
    Atm = dbl("Atm", [128, 128], BF16)
    Pm = dbl("Pm", [128, 128], BF16)
    Ptm = dbl("Ptm", [128, 128], BF16)
    X32 = dbl("X32", [128, 128])
    Xb = dbl("Xb", [128, 128], BF16)
    attnT = dbl("attnT", [128, 128], BF16)
    Vb = dbl("Vb", [128, 128], BF16)
    Kbg = dbl("Kbg", [128, 128], BF16)
    Kd = dbl("Kd", [128, 128], BF16)
    u32 = dbl("u32", [128, 128])
    WT = dbl("WT", [128, 128], BF16)
    vnew = sb("vnew", [128, 128], BF16)
    egl = sb("egl", [128, 1])

    def R(*names):
        return list(names)

    KH = KC // 2
    hv = lambda t_, k0, k1: t_[:, k0:k1, :]
    sv = lambda t_: t_[:].rearrange("p (k t) -> p k t", t=128)
    abufs = [(hv(hTb[0], 0, KH), hv(hTb[0], KH, KC), "hTb0", "hTb0"), (hv(hTb[1], 0, KH), hv(hTb[1], KH, KC), "hTb1", "hTb1")]
    if TB == 128 and T == KH * 128:
        abufs += [(sv(QT), sv(KT), "QT", "KT"), (sv(VT), sv(sqb), "VT", "sqb"), (sv(QgT), sv(KbT), "QgT", "KbT")]

    def head_body(h):
        for j, cbase in enumerate((h * 128, QK + h * 128, 2 * QK + h * 128, CONV + h * 128)):
            P.dma("pool", (lambda e, j=j, cbase=cbase: e.dma_start(out=wh[:, :, j, :], in_=win_r[:, :, cbase:cbase + 128])), writes=["wh"])
        def seq_body(seq, Tn):
            c_base = 0 if seq == "P" else T
            it_base = 0 if seq == "P" else NT
            for j in range(3):
                if seq == "P":
                    P.op("pool", (lambda e, j=j: e.memset(raw[j][:, 0:3], 0.0)), writes=[rawn[j]])
                else:
                    P.op("pool", (lambda e, j=j: e.tensor_copy(out=raw[j][:, 0:3], in_=conv0[:, j * H + h, :])), reads=["conv0"], writes=[rawn[j]])
            for b0 in range(0, Tn, TB):
                n = min(TB, Tn - b0)
                lo_, hi_, lk_, hk_ = abufs[bi[0] % len(abufs)]
                bi[0] += 1
                src_ = hblk(hT_d, (c_base + b0) // 128)
                P.dma("sp", (lambda e, lo_=lo_, src_=src_, n=n: e.dma_start(out=lo_[:, :, :n], in_=src_[:, 0:KH, :n])), reads=["hT_d"], writes=[lk_])
                P.dma("sp", (lambda e, hi_=hi_, src_=src_, n=n: e.dma_start(out=hi_[:, :, :n], in_=src_[:, KH:KC, :n])), reads=["hT_d"], writes=[hk_])
                for j in range(4):
                    pg = psG[j % 2]
                    for k in range(KC):
                        bt_, bk_, kk_ = (lo_, lk_, k) if k < KH else (hi_, hk_, k - KH)
                        P.op("pe", (lambda e, bt_=bt_, kk_=kk_, k=k, j=j, n=n, pg=pg: e.matmul(pg[:, :n], lhsT=wh[:, k, j, :], rhs=bt_[:, kk_, :n], start=(k == 0), stop=(k == KC - 1))),
                             reads=[bk_, "wh"], writes=[pg.name])
                    dst = raw[j][:, 3 + b0:3 + b0 + n] if j < 3 else zT[:, b0:b0 + n]
                    dname = rawn[j] if j < 3 else "zT"
                    P.op("act", (lambda e, pg=pg, n=n, dst=dst: e.activation(out=dst, in_=pg[:, :n], func=AF.Copy)), reads=[pg.name], writes=[dname])
            if stop <= 3:
                raise _Stop()
            co = convoP if seq == "P" else convoS
            for j in range(3):
                P.op("pool", (lambda e, j=j, co=co, Tn=Tn: e.tensor_copy(out=co[:, j * H + h, :], in_=raw[j][:, Tn:Tn + 3])), reads=[rawn[j]], writes=[co.name])
            for j, dstT in enumerate((QT, KT, VT)):
                ch = j * H + h
                P.op("dve", (lambda e, j=j, ch=ch, Tn=Tn: e.tensor_scalar(out=cacc[:, :Tn], in0=raw[j][:, 0:Tn], scalar1=convw[:, ch, 0:1], scalar2=None, op0=ALU.mult)),
                     reads=[rawn[j], "convw"], writes=["cacc"])
                for jj in range(1, 4):
                    P.op("dve", (lambda e, j=j, ch=ch, jj=jj, Tn=Tn: e.scalar_tensor_tensor(out=cacc[:, :Tn], in0=raw[j][:, jj:jj + Tn], scalar=convw[:, ch, jj:jj + 1], in1=cacc[:, :Tn], op0=ALU.mult, op1=ALU.add)),
                         reads=[rawn[j], "convw", "cacc"], writes=["cacc"])
                if j == 2:
                    P.op("act", (lambda e, Tn=Tn: e.activation(out=VT[:, :Tn], in_=cacc[:, :Tn], func=AF.Silu)), reads=["cacc"], writes=["VT"])
                else:
                    P.op("act", (lambda e, Tn=Tn: e.activation(out=cacc[:, :Tn], in_=cacc[:, :Tn], func=AF.Silu)), reads=["cacc"], writes=["cacc"])
                    P.op("act", (lambda e, Tn=Tn: e.activation(out=sqb[:, :Tn], in_=cacc[:, :Tn], func=AF.Square)), reads=["cacc"], writes=["sqb"])
                    for c0 in range(0, Tn, 512):
                        n = min(512, Tn - c0)
                        pm = psM[(c0 // 512) % 2]
                        P.op("pe", (lambda e, pm=pm, c0=c0, n=n: e.matmul(pm[:, :n], lhsT=ones_b[:, :], rhs=sqb[:, c0:c0 + n], start=True, stop=True)),
                             reads=["ones_b", "sqb"], writes=[pm.name])
                        P.op("act", (lambda e, pm=pm, c0=c0, n=n: e.activation(out=oT[:, c0:c0 + n], in_=pm[:, :n], func=AF.Sqrt, bias=EPS)), reads=[pm.name], writes=["oT"])
                    P.op("dve", (lambda e, Tn=Tn: e.reciprocal(out=oT[:, :Tn], in_=oT[:, :Tn])), reads=["oT"], writes=["oT"])
                    scl = (128.0 ** -0.5) if j == 0 else 1.0
                    P.op("dve", (lambda e, Tn=Tn, dstT=dstT, scl=scl: e.scalar_tensor_tensor(out=dstT[:, :Tn], in0=cacc[:, :Tn], scalar=scl, in1=oT[:, :Tn], op0=ALU.mult, op1=ALU.mult)),
                         reads=["cacc", "oT"], writes=[dstT.name])
            if stop <= 4:
                raise _Stop()
            ntl = (Tn + 127) // 128
            for tl in cfg.get("tls", range(ntl)):
                nr = min(128, Tn - tl * 128)
                it = it_base + tl
                cs = slice(tl * 128, tl * 128 + nr)
                pm = psM[tl % 2]
                for k3 in range(3):
                    rk = rhsk[k3]
                    P.op("dve", (lambda e, nr=nr, it=it, rk=rk, k3=k3: e.tensor_scalar(out=rk[:nr, :nr], in0=tri_f[:nr, :nr], scalar1=gk_f[k3][:nr, it, h:h + 1], scalar2=None, op0=ALU.mult)),
                         reads=["tri_f", gk_f[k3].name], writes=[rk.name])
                    P.op("pe", (lambda e, pm=pm, nr=nr, rk=rk, k3=k3: e.matmul(pm[:, 0:nr], lhsT=ones_b[:nr, :], rhs=rk[:nr, :nr], start=(k3 == 0), stop=(k3 == 2))), reads=["ones_b", rk.name], writes=[pm.name])
                for k2 in range(2):
                    rk = rhsk[3 + k2]
                    P.op("dve", (lambda e, nr=nr, it=it, rk=rk, k2=k2: e.tensor_scalar(out=rk[:nr, :nr], in0=ident_f[:nr, :nr], scalar1=bk_f[k2][:nr, it, h:h + 1], scalar2=None, op0=ALU.mult)),
                         reads=["ident_f", bk_f[k2].name], writes=[rk.name])
                    P.op("pe", (lambda e, pm=pm, nr=nr, rk=rk, k2=k2: e.matmul(pm[:, 128:128 + nr], lhsT=ones_b[:nr, :], rhs=rk[:nr, :nr], start=(k2 == 0), stop=(k2 == 1))), reads=["ones_b", rk.name], writes=[pm.name])
                if stop <= 4.2:
                    raise _Stop()
                if not cfg.get("skipE"):
                    P.op("act", (lambda e, pm=pm, nr=nr, cs=cs: e.activation(out=Ebc[:, cs], in_=pm[:, 0:nr], func=AF.Exp)), reads=[pm.name], writes=["Ebc"])
                if stop <= 4.21:
                    raise _Stop()
                if not cfg.get("skipG"):
                    P.op("dve", (lambda e, pm=pm, nr=nr, cs=cs: e.tensor_copy(out=Gbc[:, cs], in_=pm[:, 0:nr])), reads=[pm.name], writes=["Gbc"])
                if stop <= 4.22:
                    raise _Stop()
                if not cfg.get("skipB"):
                    P.op("act", (lambda e, pm=pm, nr=nr, cs=cs: e.activation(out=Bbc[:, cs], in_=pm[:, 128:128 + nr], func=AF.Copy)), reads=[pm.name], writes=["Bbc"])
                if stop <= 4.25:
                    raise _Stop()
            if stop <= 4.3:
                raise _Stop()
            P.op("dve", (lambda e, Tn=Tn: e.tensor_tensor(out=QgT[:, :Tn], in0=QT[:, :Tn], in1=Ebc[:, :Tn], op=ALU.mult)), reads=["QT", "Ebc"], writes=["QgT"])
            P.op("pool", (lambda e, Tn=Tn: e.tensor_tensor(out=KbT[:, :Tn], in0=KT[:, :Tn], in1=Bbc[:, :Tn], op=ALU.mult)), reads=["KT", "Bbc"], writes=["KbT"])
            if stop <= 4.6:
                raise _Stop()
            if seq == "P":
                P.op("dve", lambda e: e.memset(S32[:], 0.0), writes=["S32"])
            else:
                P.dma("sp", (lambda e: e.dma_start(out=S32[:], in_=s0_d[h, :, :])), writes=["S32"])
            P.op("act", lambda e: e.activation(out=Sbf[:], in_=S32[:], func=AF.Copy), reads=["S32"], writes=["Sbf"])

            def precompute(tl):
                nr = min(128, Tn - tl * 128)
                it = it_base + tl
                q = tl % 2
                cs = slice(tl * 128, tl * 128 + nr)
                nm = lambda t_: t_[q].name
                P.op("dve", (lambda e: e.tensor_scalar(out=decT[q][:nr, :nr], in0=Gbc[:nr, cs], scalar1=ngc_tm[:nr, it, h:h + 1], scalar2=0.0, op0=ALU.add, op1=ALU.min)),
                     reads=["Gbc", "ngc_tm"], writes=[nm(decT)])
                P.op("act", (lambda e: e.activation(out=decT[q][:nr, :nr], in_=decT[q][:nr, :nr], func=AF.Exp)), reads=[nm(decT)], writes=[nm(decT)])
                P.op("pool", (lambda e: e.tensor_tensor(out=decTs[q][:nr, :nr], in0=decT[q][:nr, :nr], in1=stri_f[:nr, :nr], op=ALU.mult)), reads=[nm(decT), "stri_f"], writes=[nm(decTs)])
                P.op("pool", (lambda e: e.tensor_tensor(out=decT[q][:nr, :nr], in0=decT[q][:nr, :nr], in1=tri_f[:nr, :nr], op=ALU.mult)), reads=[nm(decT), "tri_f"], writes=[nm(decT)])
                pa = psM[2]
                P.op("pe", (lambda e: e.matmul(pa[:nr, 0:nr], lhsT=KT[:, cs], rhs=KbT[:, cs], start=True, stop=True)), reads=["KT", "KbT"], writes=[pa.name])
                P.op("pe", (lambda e: e.matmul(pa[:nr, 128:128 + nr], lhsT=KT[:, cs], rhs=QT[:, cs], start=True, stop=True)), reads=["KT", "QT"], writes=[pa.name])
                P.op("dve", (lambda e: e.scalar_tensor_tensor(out=Am[q][:nr, :nr], in0=pa[:nr, 0:nr], scalar=-1.0, in1=decTs[q][:nr, :nr], op0=ALU.mult, op1=ALU.mult)),
                     reads=[pa.name, nm(decTs)], writes=[nm(Am)])
                P.op("dve", (lambda e: e.tensor_tensor(out=attnT[q][:nr, :nr], in0=pa[:nr, 128:128 + nr], in1=decT[q][:nr, :nr], op=ALU.mult)),
                     reads=[pa.name, nm(decT)], writes=[nm(attnT)])
                P.op("pe", (lambda e: e.transpose(out=psT[:nr, 0:128], in_=KT[:, cs], identity=ident_b[:, :])), reads=["KT", "ident_b"], writes=["psT"])
                P.op("pe", (lambda e: e.transpose(out=psT[:nr, 128:256], in_=VT[:, cs], identity=ident_b[:, :])), reads=["VT", "ident_b"], writes=["psT"])
                P.op("pe", (lambda e: e.transpose(out=psT[:nr, 256:256 + nr], in_=Am[q][:nr, :nr], identity=ident_b[:nr, :nr])), reads=[nm(Am), "ident_b"], writes=["psT"])
                P.op("act", (lambda e: e.activation(out=Kbg[q][:nr, :], in_=psT[:nr, 0:128], func=AF.Copy, scale=bg_tm[:nr, it, h:h + 1])), reads=["psT", "bg_tm"], writes=[nm(Kbg)])
                P.op("dve", (lambda e: e.tensor_scalar(out=Kd[q][:nr, :], in0=psT[:nr, 0:128], scalar1=ed_tm[:nr, it, h:h + 1], scalar2=None, op0=ALU.mult)), reads=["psT", "ed_tm"], writes=[nm(Kd)])
                P.op("dve", (lambda e: e.tensor_scalar(out=Vb[q][:nr, :], in0=psT[:nr, 128:256], scalar1=beta_tm[:nr, it, h:h + 1], scalar2=None, op0=ALU.mult)), reads=["psT", "beta_tm"], writes=[nm(Vb)])
                P.op("act", (lambda e: e.activation(out=Atm[q][:nr, :nr], in_=psT[:nr, 256:256 + nr], func=AF.Copy)), reads=["psT"], writes=[nm(Atm)])
                P.op("dve", (lambda e: e.tensor_tensor(out=X32[q][:nr, :nr], in0=Am[q][:nr, :nr], in1=ident_f[:nr, :nr], op=ALU.add)), reads=[nm(Am), "ident_f"], writes=[nm(X32)])
                P.op("act", (lambda e: e.activation(out=Xb[q][:nr, :nr], in_=X32[q][:nr, :nr], func=AF.Copy)), reads=[nm(X32)], writes=[nm(Xb)])
                cur, curT = Am[q], Atm[q]
                for kk in range(1, 6):
                    pb = psM[3]
                    nxt, nxtT = (Pm[q], Ptm[q]) if (kk % 2 == 1) else (Am[q], Atm[q])
                    P.op("pe", (lambda e, cur=cur, curT=curT: e.matmul(pb[:nr, 0:nr], lhsT=cur[:nr, :nr], rhs=curT[:nr, :nr], start=True, stop=True)), reads=[cur.name, curT.name], writes=[pb.name])
                    if kk < 5:
                        P.op("pe", (lambda e, cur=cur, curT=curT: e.matmul(pb[:nr, 128:128 + nr], lhsT=curT[:nr, :nr], rhs=cur[:nr, :nr], start=True, stop=True)), reads=[cur.name, curT.name], writes=[pb.name])
                    P.op("act", (lambda e, nxtT=nxtT: e.activation(out=nxtT[:nr, :nr], in_=pb[:nr, 0:nr], func=AF.Copy)), reads=[pb.name], writes=[nxtT.name])
                    if kk < 5:
                        P.op("dve", (lambda e, nxt=nxt: e.tensor_copy(out=nxt[:nr, :nr], in_=pb[:nr, 128:128 + nr])), reads=[pb.name], writes=[nxt.name])
                    P.op("pe", (lambda e, nxtT=nxtT: e.matmul(pb[:nr, 256:256 + nr], lhsT=nxtT[:nr, :nr], rhs=Xb[q][:nr, :nr], start=True, stop=True)), reads=[nxtT.name, nm(Xb)], writes=[pb.name])
                    P.op("dve", (lambda e: e.tensor_tensor(out=X32[q][:nr, :nr], in0=X32[q][:nr, :nr], in1=pb[:nr, 256:256 + nr], op=ALU.add)), reads=[pb.name, nm(X32)], writes=[nm(X32)])
                    P.op("act", (lambda e: e.activation(out=Xb[q][:nr, :nr], in_=X32[q][:nr, :nr], func=AF.Copy)), reads=[nm(X32)], writes=[nm(Xb)])
                    cur, curT = nxt, nxtT
                pc = psM[2]
                P.op("pe", (lambda e: e.matmul(pc[:nr, 256:384], lhsT=Xb[q][:nr, :nr], rhs=Vb[q][:nr, :], start=True, stop=True)), reads=[nm(Xb), nm(Vb)], writes=[pc.name])
                P.op("pe", (lambda e: e.matmul(pc[:, 384:384 + nr], lhsT=Kbg[q][:nr, :], rhs=Xb[q][:nr, :nr], start=True, stop=True)), reads=[nm(Xb), nm(Kbg)], writes=[pc.name])
                P.op("act", (lambda e: e.activation(out=u32[q][:nr, :], in_=pc[:nr, 256:384], func=AF.Copy)), reads=[pc.name], writes=[nm(u32)])
                P.op("dve", (lambda e: e.tensor_copy(out=WT[q][:, :nr], in_=pc[:, 384:384 + nr])), reads=[pc.name], writes=[nm(WT)])

            def chain(tl):
                nr = min(128, Tn - tl * 128)
                q = tl % 2
                nm = lambda t_: t_[q].name
                for p0 in range(0, nr, 64):
                    n = min(64, nr - p0)
                    rs = slice(p0, p0 + n)
                    c0 = tl * 128 + p0
                    P.op("pe", (lambda e: e.matmul(psS[:nr, 0:128], lhsT=WT[q][:, :nr], rhs=Sbf[:, :], start=True, stop=True)), reads=[nm(WT), "Sbf"], writes=["psS"])
                    P.op("dve", (lambda e, rs=rs: e.tensor_tensor(out=vnew[rs, :], in0=u32[q][rs, :], in1=psS[rs, 0:128], op=ALU.subtract)), reads=[nm(u32), "psS"], writes=["vnew"])
                    P.op("pe", (lambda e, c0=c0, n=n: e.matmul(psS[:, 128:128 + n], lhsT=Sbf[:, :], rhs=QgT[:, c0:c0 + n], start=True, stop=False)), reads=["Sbf", "QgT"], writes=["psS"])
                    P.op("pe", (lambda e, rs=rs, p0=p0, n=n: e.matmul(psS[:, 128:128 + n], lhsT=vnew[rs, :], rhs=attnT[q][rs, p0:p0 + n], start=False, stop=True)), reads=["vnew", nm(attnT)], writes=["psS"])
                    P.op("pe", (lambda e, rs=rs: e.matmul(psS[:, 256:384], lhsT=Kd[q][rs, :], rhs=vnew[rs, :], start=True, stop=True)), reads=["vnew", nm(Kd)], writes=["psS"])
                    P.op("act", (lambda e, c0=c0, n=n: e.activation(out=oT[:, c0:c0 + n], in_=psS[:, 128:128 + n], func=AF.Copy)), reads=["psS"], writes=["oT"])
                    lc = c0 + n - 1
                    P.op("dve", (lambda e, lc=lc: e.scalar_tensor_tensor(out=S32[:, :], in0=S32[:, :], scalar=Ebc[:, lc:lc + 1], in1=psS[:, 256:384], op0=ALU.mult, op1=ALU.add)),
                         reads=["S32", "Ebc", "psS"], writes=["S32"])
                    P.op("act", (lambda e: e.activation(out=Sbf[:], in_=S32[:], func=AF.Copy)), reads=["S32"], writes=["Sbf"])

            if stop <= 5:
                raise _Stop()
            precompute(0)
            if stop <= 6:
                raise _Stop()
            for tl in range(ntl):
                if tl + 1 < ntl:
                    P.interleave((lambda tl=tl: precompute(tl + 1)), (lambda tl=tl: chain(tl)))
                else:
                    chain(tl)
            if o_dbg is not None and h == cfg["dbg"] - 1 and seq == "P":
                for i_, (t_, k_) in enumerate(((QT, "QT"), (KT, "KT"), (VT, "VT"), (Gbc, "Gbc"), (Bbc, "Bbc"), (Ebc, "Ebc"), (oT, "oT"), (zT, "zT"), (raw[0][:, 3:3 + T], "raw0"), (QgT, "QgT"))):
                    P.dma("pool", (lambda e, i_=i_, t_=t_: e.dma_start(out=o_dbg[i_, :, :], in_=t_[:, :T])), reads=[k_], writes=["o_dbg%d" % i_])
            so = o_sdP if seq == "P" else o_sdS
            P.dma("sp", (lambda e, so=so: e.dma_start(out=so[h, :, :], in_=S32[:])), reads=["S32"], writes=["o_sd"])
            P.op("act", (lambda e, Tn=Tn: e.activation(out=sqb[:, :Tn], in_=oT[:, :Tn], func=AF.Square)), reads=["oT"], writes=["sqb"])
            for c0 in range(0, Tn, 512):
                n = min(512, Tn - c0)
                pm = psM[(c0 // 512) % 2]
                P.op("pe", (lambda e, pm=pm, c0=c0, n=n: e.matmul(pm[:, :n], lhsT=ones_b[:, :], rhs=sqb[:, c0:c0 + n], start=True, stop=True)), reads=["ones_b", "sqb"], writes=[pm.name])
                P.op("act", (lambda e, pm=pm, c0=c0, n=n: e.activation(out=cacc[:, c0:c0 + n], in_=pm[:, :n], func=AF.Sqrt, scale=1.0 / 128, bias=EPS)), reads=[pm.name], writes=["cacc"])
            P.op("dve", (lambda e, Tn=Tn: e.reciprocal(out=cacc[:, :Tn], in_=cacc[:, :Tn])), reads=["cacc"], writes=["cacc"])
            P.op("dve", (lambda e, Tn=Tn: e.scalar_tensor_tensor(out=oT[:, :Tn], in0=oT[:, :Tn], scalar=onw[:, 0:1], in1=cacc[:, :Tn], op0=ALU.mult, op1=ALU.mult)), reads=["oT", "onw", "cacc"], writes=["oT"])
            P.op("act", (lambda e, Tn=Tn: e.activation(out=zT[:, :Tn], in_=zT[:, :Tn], func=AF.Silu)), reads=["zT"], writes=["zT"])
            P.op("dve", (lambda e, Tn=Tn: e.tensor_tensor(out=ogT[:, :Tn], in0=oT[:, :Tn], in1=zT[:, :Tn], op=ALU.mult)), reads=["oT", "zT"], writes=["ogT"])
            if seq == "P":
                P.dma("sp", (lambda e: e.dma_start(out=og_d[0:NT].rearrange("n p (h t) -> p n h t", t=128)[:, :, h, :], in_=ogT[:, :T].rearrange("p (n t) -> p n t", t=128))), reads=["ogT"], writes=["og_d"])
            else:
                P.dma("sp", (lambda e: e.dma_start(out=og_d[NT].rearrange("p (h t) -> p h t", t=128)[:, h, 0:TS], in_=ogT[:, :TS])), reads=["ogT"], writes=["og_d"])

        for sq_ in (("P", T), ("S", TS)):
            seq_body(*sq_)

    for h_ in range(H):
        head_body(h_)
    P.dma("sp", lambda e: e.dma_start(out=o_scP[:, :, :], in_=convoP[:]), reads=["convoP"], writes=["o_scP"])
    P.dma("sp", lambda e: e.dma_start(out=o_scS[:, :, :], in_=convoS[:]), reads=["convoS"], writes=["o_scS"])

    P.barrier()
    CB = min(512, D)
    whf0 = wh[:].rearrange("p a b c -> p (a b c)")
    if H * CB <= KC * 512 and H * 128 <= 2 * T and 2 * CB <= T:
        wo = whf0[:, 0:H * CB].rearrange("p (h n) -> p h n", n=CB)
        ogt = [t_[:].bitcast(BF16)[:, 0:H * 128].rearrange("p (h n) -> p h n", n=128) for t_ in (Gbc, Ebc)]
        x1t = [V(Bbc[:, i * CB:(i + 1) * CB], "x1t%d" % i) for i in range(2)]
    else:
        wo = sb("wo_t", [128, H, CB], BF16)[:]
        ogt = [sb("ogt_t%d" % i, [128, H, 128], BF16)[:] for i in range(2)]
        x1t = [sb("x1t%d" % i, [128, CB]) for i in range(2)]
    ogn = ["ogt0", "ogt1"]

    def outproj(w_dram, src_d, skey, dst_d, dkey):
        w_r = w_dram.rearrange("(h p) n -> p h n", p=128)
        ti = 0
        for cb in range(0, D, CB):
            P.dma("pool", (lambda e, cb=cb: e.dma_start(out=wo, in_=w_r[:, :, cb:cb + CB])), writes=["wo"])
            for it in range(ntile_all):
                r0 = it * 128
                nr = min(128, TT - r0)
                og = ogt[ti % 2]
                ogk = ogn[ti % 2]
                xo = x1t[ti % 2]
                pg = psG[ti % 2]
                ti += 1
                P.dma("sp", (lambda e, og=og, r0=r0, nr=nr: e.dma_start(out=og[:, :, :nr], in_=og_d[r0 // 128].rearrange("p (h t) -> p h t", t=128)[:, :, :nr])), reads=["og_d"], writes=[ogk])
                P.dma("sp", (lambda e, xo=xo, r0=r0, nr=nr, cb=cb: e.dma_start(out=xo[:nr, :], in_=src_d[r0:r0 + nr, cb:cb + CB])), reads=[skey], writes=[xo.name])
                for hh in range(H):
                    P.op("pe", (lambda e, og=og, hh=hh, nr=nr, pg=pg: e.matmul(pg[:nr, :CB], lhsT=og[:, hh, :nr], rhs=wo[:, hh, :], start=(hh == 0), stop=(hh == H - 1))),
                         reads=[ogk, "wo"], writes=[pg.name])
                P.op("dve", (lambda e, xo=xo, nr=nr, pg=pg: e.tensor_tensor(out=xo[:nr, :], in0=xo[:nr, :], in1=pg[:nr, :CB], op=ALU.add)), reads=[xo.name, pg.name], writes=[xo.name])
                P.dma("sp", (lambda e, xo=xo, r0=r0, nr=nr, cb=cb: e.dma_start(out=dst_d[r0:r0 + nr, cb:cb + CB], in_=xo[:nr, :])), reads=[xo.name], writes=[dkey])

    outproj(wout_d, x_d, "xin", o_x1, "x1src")

    if stop <= 10:
        raise _Stop()
    P.barrier()
    rms_phase(o_x1, wnb2_d, h1T_d, "h1T_d")

    P.barrier()
    whf = wh[:].rearrange("p a b c -> p (a b c)")
    wkv = whf[:, 0:KC * 512].rearrange("p (k n) -> p k n", n=512)
    Carver.fallback = sb if D < 1024 else None
    f32v = lambda t_: t_[:].bitcast(F32)
    CV = Carver([arena[:, :], Gbc[:], Ebc[:], Bbc[:], f32v(QT), f32v(KT), f32v(VT), f32v(sqb), f32v(QgT), f32v(KbT), f32v(ogT)])
    h1x = [CV.alloc("h1x%d" % i, [128, KC, 128], BF16) for i in range(2)]
    h1t = [hTb[0][:, :, 0:128], hTb[1][:, :, 0:128], h1x[0][:, :, :], h1x[1][:, :, :]]
    h1n = ["hTb0", "hTb1", "h1x0", "h1x1"]
    kvt = [CV.alloc("kvt%d" % i, [128, 512]) for i in range(2)]
    knb = CV.alloc("knb", [128, 2, DKN])
    ss2 = CV.alloc("ss2", [128, 4])
    sqj = CV.alloc("sqj", [128, DKN])
    P.dma("sp", lambda e: e.dma_start(out=knb[:], in_=knb_d[:, :, :]), writes=["knb"])
    win2_r = win2_d.rearrange("(kc p) n -> p kc n", p=128)
    P.dma("sp", lambda e: e.dma_start(out=o_swk[0:WLC - TS, :], in_=cwk_d[TS:WLC, :]), writes=["o_swk"])
    P.dma("sp", lambda e: e.dma_start(out=o_swv[0:WLC - TS, :], in_=cwv_d[TS:WLC, :]), writes=["o_swv"])
    specs = [("kc", c_kc, DKN, None, o_pck, o_sck, False), ("vc", c_vc, DVN, None, o_pcv, o_scv, False),
             ("ks", c_ks, DKN, 0, o_psk, o_ssk, False), ("vs", c_vs, DVN, None, o_psv, o_ssv, False),
             ("kw", c_kw, DKN, 1, o_pwk, o_swk, True), ("vw", c_vw, DVN, None, o_pwv, o_swv, True)]
    tix = [0]
    for (nm_, cbase, gw, nidx, outP, outS, iswin) in specs:
        gpc = max(1, min(G, 512 // gw))
        for g0 in range(0, G, gpc):
            ng = min(gpc, G - g0)
            ncol = ng * gw
            co_ = g0 * gw
            P.dma("pool", (lambda e, cbase=cbase, co_=co_, ncol=ncol: e.dma_start(out=wkv[:, :, :ncol], in_=win2_r[:, :, cbase + co_:cbase + co_ + ncol])), writes=["wh"])
            for it in range(ntile_all):
                r0 = it * 128
                nr = min(128, TT - r0)
                i2 = tix[0] % 2
                ih = tix[0] % 4
                tix[0] += 1
                ht, hk, kt, pg = h1t[ih], h1n[ih], kvt[i2], psG[i2]
                P.dma("sp", (lambda e, ht=ht, r0=r0, nr=nr: e.dma_start(out=ht[:, :, :nr], in_=hblk(h1T_d, r0 // 128)[:, :, :nr])), reads=["h1T_d"], writes=[hk])
                for k in range(KC):
                    P.op("pe", (lambda e, ht=ht, k=k, nr=nr, pg=pg, ncol=ncol: e.matmul(pg[:nr, :ncol], lhsT=ht[:, k, :nr], rhs=wkv[:, k, :ncol], start=(k == 0), stop=(k == KC - 1))),
                         reads=[hk, "wh"], writes=[pg.name])
                if nidx is None:
                    P.op("act", (lambda e, kt=kt, pg=pg, nr=nr, ncol=ncol: e.activation(out=kt[:nr, :ncol], in_=pg[:nr, :ncol], func=AF.Copy)), reads=[pg.name], writes=[kt.name])
                else:
                    for gi in range(ng):
                        cs_ = slice(gi * gw, (gi + 1) * gw)
                        P.op("act", (lambda e, pg=pg, nr=nr, cs_=cs_, gi=gi: e.activation(out=sqj[:nr, :], in_=pg[:nr, cs_], func=AF.Square, accum_out=ss2[:nr, gi:gi + 1])),
                             reads=[pg.name], writes=["sqj", "ss2"])
                    P.op("act", (lambda e, nr=nr, ng=ng: e.activation(out=ss2[:nr, :ng], in_=ss2[:nr, :ng], func=AF.Sqrt, scale=1.0 / DKN, bias=EPS)), reads=["ss2"], writes=["ss2"])
                    P.op("dve", (lambda e, nr=nr, ng=ng: e.reciprocal(out=ss2[:nr, :ng], in_=ss2[:nr, :ng])), reads=["ss2"], writes=["ss2"])
                    for gi in range(ng):
                        cs_ = slice(gi * gw, (gi + 1) * gw)
                        P.op("dve", (lambda e, kt=kt, pg=pg, nr=nr, cs_=cs_, gi=gi, nidx=nidx: e.scalar_tensor_tensor(out=kt[:nr, cs_], in0=pg[:nr, cs_], scalar=ss2[:nr, gi:gi + 1], in1=knb[:nr, nidx, :], op0=ALU.mult, op1=ALU.mult)),
                             reads=[pg.name, "ss2", "knb"], writes=[kt.name])
                if it < NT:
                    if not iswin:
                        P.dma("sp", (lambda e, kt=kt, outP=outP, r0=r0, co_=co_, ncol=ncol: e.dma_start(out=outP[r0:r0 + 128, co_:co_ + ncol], in_=kt[:, :ncol])), reads=[kt.name], writes=["o_" + nm_])
                    elif r0 >= T - WL:
                        P.dma("sp", (lambda e, kt=kt, outP=outP, r0=r0, co_=co_, ncol=ncol: e.dma_start(out=outP[r0 - (T - WL):r0 - (T - WL) + 128, co_:co_ + ncol], in_=kt[:, :ncol])), reads=[kt.name], writes=["o_" + nm_])
                else:
                    ro = WLC - TS if iswin else 0
                    P.dma("sp", (lambda e, kt=kt, outS=outS, ro=ro, co_=co_, ncol=ncol: e.dma_start(out=outS[ro:ro + TS, co_:co_ + ncol], in_=kt[:TS, :ncol])), reads=[kt.name], writes=["o_s" + nm_])
                P.dma("pool", (lambda e, kt=kt, nm_=nm_, r0=r0, nr=nr, co_=co_, ncol=ncol: e.dma_start(out=kv_s[nm_][r0:r0 + nr, co_:co_ + ncol], in_=kt[:nr, :ncol])), reads=[kt.name], writes=["s_" + nm_])
    if stop <= 11:
        raise _Stop()

    P.barrier()
    CV.reset()
    HW_ = DKN + DVN
    wq2 = CV.alloc("wq2", [128, KC, 2 * HW_], BF16)
    wq = wh[:].rearrange("p a b c -> p (a b c)")[:, 0:KC * 512].rearrange("p (k n) -> p k n", n=512)
    h1y = [CV.alloc("h1y%d" % i, [128, KC, 128], BF16) for i in range(2)]
    h1t3 = [hTb[0][:, :, 0:128], hTb[1][:, :, 0:128], h1y[0][:, :, :], h1y[1][:, :, :]]
    qnw = CV.alloc("qnw", [128, 2])
    sqa = [CV.alloc("sqa%d" % i, [128, 128], BF16) for i in range(2)]
    sqb2 = [CV.alloc("sqb2%d" % i, [128, 128], BF16) for i in range(2)]
    rq = [CV.alloc("rq%d" % i, [128, 128]) for i in range(2)]
    qo = [CV.alloc("qo%d" % i, [128, 128], BF16) for i in range(2)]
    qo2 = [CV.alloc("qo2%d" % i, [128, 128], BF16) for i in range(2)]
    zo = [CV.alloc("zo%d" % i, [128, 128], BF16) for i in range(2)]
    go = [CV.alloc("go%d" % i, [128, 128]) for i in range(2)]
    P.dma("sp", lambda e: e.dma_start(out=qnw[:], in_=qnw_d[:, :]), writes=["qnw"])

    def b3_pair(h0):
        for u in range(2):
            hn = h0 + u
            P.dma("pool", (lambda e, u=u, hn=hn: e.dma_start(out=wq2[:, :, u * HW_:u * HW_ + DKN], in_=win2_r[:, :, hn * DKN:(hn + 1) * DKN])), writes=["wq2"])
            P.dma("pool", (lambda e, u=u, hn=hn: e.dma_start(out=wq2[:, :, u * HW_ + DKN:(u + 1) * HW_], in_=win2_r[:, :, c_z + hn * DVN:c_z + (hn + 1) * DVN])), writes=["wq2"])
        for it in range(ntile_all):
            r0 = it * 128
            n = min(128, TT - r0)
            ih = tix[0] % 4
            tix[0] += 1
            ht, hk = h1t3[ih], h1n[ih]
            P.dma("sp", (lambda e, ht=ht, r0=r0, n=n: e.dma_start(out=ht[:, :, :n], in_=hblk(h1T_d, r0 // 128)[:, :, :n])), reads=["h1T_d"], writes=[hk])
            for u in range(2):
                hn = h0 + u
                pq, pz, po = psG[u], psM[u], psM[2 + u]
                c0_ = u * HW_
                for k in range(KC):
                    P.op("pe", (lambda e, ht=ht, k=k, n=n, pq=pq, c0_=c0_: e.matmul(pq[:, 0:n], lhsT=wq2[:, k, c0_:c0_ + 128], rhs=ht[:, k, :n], start=(k == 0), stop=(k == KC - 1))), reads=[hk, "wq2"], writes=[pq.name])
                for k in range(KC):
                    P.op("pe", (lambda e, ht=ht, k=k, n=n, pq=pq, c0_=c0_: e.matmul(pq[:64, 128:128 + n], lhsT=wq2[:, k, c0_ + 128:c0_ + 192], rhs=ht[:, k, :n], start=(k == 0), stop=(k == KC - 1))), reads=[hk, "wq2"], writes=[pq.name])
                for k in range(KC):
                    P.op("pe", (lambda e, ht=ht, k=k, n=n, pz=pz, c0_=c0_: e.matmul(pz[:, 0:n], lhsT=wq2[:, k, c0_ + DKN:c0_ + HW_], rhs=ht[:, k, :n], start=(k == 0), stop=(k == KC - 1))), reads=[hk, "wq2"], writes=[pz.name])
                sa_, sb_, rq_ = sqa[u], sqb2[u], rq[u]
                P.op("act", (lambda e, pq=pq, n=n, sa_=sa_: e.activation(out=sa_[:, :n], in_=pq[:, 0:n], func=AF.Square)), reads=[pq.name], writes=[sa_.name])
                P.op("act", (lambda e, pq=pq, n=n, sb_=sb_: e.activation(out=sb_[:64, :n], in_=pq[:64, 128:128 + n], func=AF.Square)), reads=[pq.name], writes=[sb_.name])
                P.op("pe", (lambda e, n=n, po=po, sa_=sa_: e.matmul(po[:, 0:n], lhsT=ones_b[:, :], rhs=sa_[:, :n], start=True, stop=False)), reads=["ones_b", sa_.name], writes=[po.name])
                P.op("pe", (lambda e, n=n, po=po, sb_=sb_: e.matmul(po[:, 0:n], lhsT=ones_b[:64, :], rhs=sb_[:64, :n], start=False, stop=True)), reads=["ones_b", sb_.name], writes=[po.name])
                P.op("act", (lambda e, n=n, po=po, rq_=rq_: e.activation(out=rq_[:, :n], in_=po[:, 0:n], func=AF.Sqrt, scale=1.0, bias=DKN * EPS)), reads=[po.name], writes=[rq_.name])
                P.op("dve", (lambda e, n=n, rq_=rq_: e.reciprocal(out=rq_[:, :n], in_=rq_[:, :n])), reads=[rq_.name], writes=[rq_.name])
                qa_, qb_, zz_ = qo[u], qo2[u], zo[u]
                P.op("dve", (lambda e, pq=pq, n=n, qa_=qa_, rq_=rq_: e.scalar_tensor_tensor(out=qa_[:, :n], in0=pq[:, 0:n], scalar=qnw[:, 0:1], in1=rq_[:, :n], op0=ALU.mult, op1=ALU.mult)), reads=[pq.name, "qnw", rq_.name], writes=[qa_.name])
                P.op("dve", (lambda e, pq=pq, n=n, qb_=qb_, rq_=rq_: e.scalar_tensor_tensor(out=qb_[:64, :n], in0=pq[:64, 128:128 + n], scalar=qnw[:64, 1:2], in1=rq_[:64, :n], op0=ALU.mult, op1=ALU.mult)), reads=[pq.name, "qnw", rq_.name], writes=[qb_.name])
                P.op("act", (lambda e, pz=pz, n=n, zz_=zz_: e.activation(out=zz_[:, :n], in_=pz[:, 0:n], func=AF.Silu)), reads=[pz.name], writes=[zz_.name])
                P.dma("sp", (lambda e, qa_=qa_, r0=r0, n=n, hn=hn: e.dma_start(out=qT_s[hn, 0:128, r0:r0 + n], in_=qa_[:, :n])), reads=[qa_.name], writes=["qT_s"])
                P.dma("sp", (lambda e, qb_=qb_, r0=r0, n=n, hn=hn: e.dma_start(out=qT_s[hn, 128:192, r0:r0 + n], in_=qb_[:64, :n])), reads=[qb_.name], writes=["qT_s"])
                P.dma("sp", (lambda e, zz_=zz_, r0=r0, n=n, hn=hn: e.dma_start(out=zs_s[hn, :, r0:r0 + n], in_=zz_[:, :n])), reads=[zz_.name], writes=["zs_s"])

    for hn_ in range(0, HN, 2):
        b3_pair(hn_)
    NGL = 3 * HN
    P.dma("pool", (lambda e: e.dma_start(out=wq[:, :, 0:NGL], in_=win2_r[:, :, c_gl:c_gl + NGL])), writes=["wh"])
    for it in range(ntile_all):
        r0 = it * 128
        n = min(128, TT - r0)
        i2 = tix[0] % 2
        tix[0] += 1
        ht, hk, pq, gg = h1t3[i2], h1n[i2], psG[i2], go[i2]
        P.dma("sp", (lambda e, ht=ht, r0=r0, n=n: e.dma_start(out=ht[:, :, :n], in_=hblk(h1T_d, r0 // 128)[:, :, :n])), reads=["h1T_d"], writes=[hk])
        for k in range(KC):
            P.op("pe", (lambda e, ht=ht, k=k, n=n, pq=pq: e.matmul(pq[:NGL, 0:n], lhsT=wq[:, k, 0:NGL], rhs=ht[:, k, :n], start=(k == 0), stop=(k == KC - 1))), reads=[hk, "wh"], writes=[pq.name])
        P.op("act", (lambda e, pq=pq, n=n, gg=gg: e.activation(out=gg[:NGL, :n], in_=pq[:NGL, 0:n], func=AF.Sigmoid)), reads=[pq.name], writes=[gg.name])
        P.dma("sp", (lambda e, gg=gg, r0=r0, n=n: e.dma_start(out=gT_s[:, r0:r0 + n], in_=gg[:NGL, :n])), reads=[gg.name], writes=["gT_s"])
    if stop <= 12:
        raise _Stop()

    P.barrier()
    flat32 = lambda t_, pat: t_[:].rearrange(pat).bitcast(F32)
    CV = Carver([arena[:, :], flat32(wh, "p a b c -> p (a b c)"), Gbc[:], Ebc[:], Bbc[:], flat32(hTb[0], "p a b -> p (a b)"), flat32(hTb[1], "p a b -> p (a b)"),
                 f32v(QT), f32v(KT), f32v(VT), f32v(sqb), f32v(QgT), f32v(KbT), f32v(ogT)])
    tmt = [CV.alloc("tmt%d" % i, [128, G * DKN], BF16) for i in range(2)]
    fmt = [CV.alloc("fmt%d" % i, [128, 2 * G, 128], BF16) for i in range(2)]

    def fm_tile(loader, gw, dst, col0, nr, wkey):
        i2 = tix[0] % 2
        tix[0] += 1
        tt_, ff_ = tmt[i2], fmt[i2]
        loader(tt_)
        nch = 2 if gw == DKN else 1
        for g in range(G):
            for c in range(nch):
                w_ = 128 if c == 0 else gw - 128
                j = g * nch + c
                P.op("pe", (lambda e, tt_=tt_, g=g, c=c, w_=w_, j=j: e.transpose(out=psT[:w_, j * 128:j * 128 + nr], in_=tt_[:nr, g * gw + c * 128:g * gw + c * 128 + w_], identity=ident_b[:nr, :nr])),
                     reads=[tt_.name, "ident_b"], writes=["psT"])
        nj = G * nch
        P.op("act", (lambda e, ff_=ff_, nj=nj: e.activation(out=ff_[:, 0:nj, :nr], in_=psT[:, 0:nj * 128].rearrange("p (j t) -> p j t", t=128)[:, :, :nr], func=AF.Copy)), reads=["psT"], writes=[ff_.name])
        for c in range(nch):
            w_ = 128 if c == 0 else gw - 128
            P.dma("sp", (lambda e, ff_=ff_, c=c, w_=w_: e.dma_start(out=dst[:, c * 128:c * 128 + w_, col0:col0 + nr].rearrange("g p t -> p g t"),
                                                                  in_=ff_[:w_, 0:nj, :nr].rearrange("p (g c) t -> p g c t", c=nch)[:, :, c, :])), reads=[ff_.name], writes=[wkey])
        return tt_

    def plain_loader(src_ap, rkey, gw, nr, eng="sp"):
        def ld(tt_):
            P.dma(eng, (lambda e: e.dma_start(out=tt_[:nr, 0:G * gw], in_=src_ap)), reads=[rkey], writes=[tt_.name])
        return ld

    for nm_ in ("kc", "ks", "kw"):
        for it in range(NT):
            fm_tile(plain_loader(kv_s[nm_][it * 128:(it + 1) * 128, :], "s_" + nm_, DKN, 128), DKN, fmK[nm_], it * 128, 128, "fm_" + nm_)
    for it in range(NT):
        fm_tile(plain_loader(kv_s["vc"][it * 128:(it + 1) * 128, :], "s_vc", DVN, 128), DVN, fmV, it * 128, 128, "fm_vc")

    I32 = mybir.dt.int32
    pti = CV.alloc("pti", [128, NPG], I32)
    ptf = CV.alloc("ptf", [128, NPG])
    idx = CV.alloc("idx", [128, NPG], I32)
    iot = CV.alloc("iot", [128, 1])
    zpad = CV.alloc("zpad", [128, 8], BF16)
    P.dma("sp", lambda e: e.dma_start(out=pti[:], in_=pt_d[0:1, :].partition_broadcast(128)), writes=["pti"])
    P.dma("sp", lambda e: e.dma_start(out=iot[:], in_=iota_d[:, :]), writes=["iot"])
    P.op("dve", lambda e: e.memset(zpad[:], 0.0), writes=["zpad"])
    P.op("dve", lambda e: e.tensor_copy(out=ptf[:], in_=pti[:]), reads=["pti"], writes=["ptf"])
    P.op("dve", lambda e: e.tensor_scalar(out=ptf[:], in0=ptf[:], scalar1=128.0, scalar2=None, op0=ALU.mult), reads=["ptf"], writes=["ptf"])
    P.op("dve", lambda e: e.tensor_scalar(out=ptf[:], in0=ptf[:], scalar1=iot[:, 0:1], scalar2=None, op0=ALU.add), reads=["ptf", "iot"], writes=["ptf"])
    P.op("dve", lambda e: e.tensor_copy(out=idx[:], in_=ptf[:]), reads=["ptf"], writes=["idx"])

    def gather_loader(pool_ap, gw, L):
        def ld(tt_):
            P.dma("pool", (lambda e: e.indirect_dma_start(out=tt_[:, 0:G * gw], out_offset=None, in_=pool_ap[:, :],
                                                          in_offset=bass.IndirectOffsetOnAxis(ap=idx[:, L:L + 1], axis=0))), reads=["idx"], writes=[tt_.name])
        return ld

    for L_ in range(NPG):
        fm_tile(gather_loader(pool_d["kc"], DKN, L_), DKN, sK["kc"], L_ * 128, 128, "sfm_kc")
        fm_tile(gather_loader(pool_d["ks"], DKN, L_), DKN, sK["ks"], L_ * 128, 128, "sfm_ks")
        fm_tile(gather_loader(pool_d["vc"], DVN, L_), DVN, sVcT, L_ * 128, 128, "sfm_vc")
        i2 = tix[0] % 2
        tix[0] += 1
        tt_ = tmt[i2]
        gather_loader(pool_d["vs"], DVN, L_)(tt_)
        P.dma("sp", (lambda e, tt_=tt_, L_=L_: e.dma_start(out=sVs[L_ * 128:(L_ + 1) * 128, :], in_=tt_[:, 0:VW_])), reads=[tt_.name], writes=["s_svs"])
    fm_tile(plain_loader(kv_s["kc"][T:T + TS, :], "s_kc", DKN, TS), DKN, sK["kc"], P0, TS, "sfm_kc")
    fm_tile(plain_loader(kv_s["ks"][T:T + TS, :], "s_ks", DKN, TS), DKN, sK["ks"], P0, TS, "sfm_ks")
    fm_tile(plain_loader(kv_s["vc"][T:T + TS, :], "s_vc", DVN, TS), DVN, sVcT, P0, TS, "sfm_vc")
    P.dma("sp", lambda e: e.dma_start(out=sVs[P0:P0 + TS, :], in_=kv_s["vs"][T:T + TS, :]), reads=["s_vs"], writes=["s_svs"])
    for g in range(G):
        P.dma("sp", (lambda e, g=g: e.dma_start(out=sK["kc"][g, 0:128, P0 + TS:P0 + 16], in_=zpad[:, :])), reads=["zpad"], writes=["sfm_kc"])
        P.dma("sp", (lambda e, g=g: e.dma_start(out=sK["kc"][g, 128:192, P0 + TS:P0 + 16], in_=zpad[:64, :])), reads=["zpad"], writes=["sfm_kc"])
        P.dma("sp", (lambda e, g=g: e.dma_start(out=sVcT[g, :, P0 + TS:P0 + 16], in_=zpad[:, :])), reads=["zpad"], writes=["sfm_vc"])
    for t0 in range(0, WLC, 128):
        fm_tile(plain_loader(cwk_d[t0:t0 + 128, :], "cwk", DKN, 128, eng="pool"), DKN, sKw, t0, 128, "sfm_kw")
    fm_tile(plain_loader(kv_s["kw"][T:T + TS, :], "s_kw", DKN, TS), DKN, sKw, WLC, TS, "sfm_kw")
    P.dma("pool", lambda e: e.dma_start(out=sVw[0:WLC, :], in_=cwv_d[:, :]), writes=["s_svw"])
    P.dma("sp", lambda e: e.dma_start(out=sVw[WLC:WLC + TS, :], in_=kv_s["vw"][T:T + TS, :]), reads=["s_vw"], writes=["s_svw"])

    P.barrier()
    CV.reset()
    w1ka = CV.alloc("w1ka", [128, 32, 256], BF16)
    w1kb = CV.alloc("w1kb", [128, 32, 256], BF16)
    w1v = CV.alloc("w1v", [128, 32, 256], BF16)
    w2k = CV.alloc("w2k", [128, 2, DKN], BF16)
    w2v = CV.alloc("w2v", [128, 2, DVN], BF16)
    peka = CV.alloc("peka", [128, 32], BF16)
    pekb = CV.alloc("pekb", [128, 32], BF16)
    pev = CV.alloc("pev", [128, 32], BF16)
    kncb = CV.alloc("kncb", [128, DKN])
    cpe = CV.alloc("cpe", [1, 2, 256])
    cpeh = CV.alloc("cpeh", [1, 2, 256], BF16)
    cpel = CV.alloc("cpel", [1, 2, 256], BF16)
    cpr = CV.alloc("cpr", [1, 2, 256])
    LR = 16 * 127 + 32
    rka = CV.alloc("rka", [128, LR], BF16)
    rkb = CV.alloc("rkb", [128, LR], BF16)
    rv = CV.alloc("rv", [128, LR], BF16)
    hid = CV.alloc("hid", [128, 2, 256], BF16)
    hidT = CV.alloc("hidT", [128, 4, 128], BF16)
    ckn = CV.alloc("ckn", [128, DKN], BF16)
    cvn = CV.alloc("cvn", [128, DVN], BF16)
    ckT = CV.alloc("ckT", [128, 2, 128], BF16)
    P.dma("pool", lambda e: e.dma_start(out=w1ka[:], in_=w1k_d[0:128, :, :]), writes=["w1ka"])
    P.dma("pool", lambda e: e.dma_start(out=w1kb[:64], in_=w1k_d[128:192, :, :]), writes=["w1kb"])
    P.dma("pool", lambda e: e.dma_start(out=w1v[:], in_=w1v_d[:, :, :]), writes=["w1v"])
    P.dma("pool", lambda e: e.dma_start(out=w2k[:], in_=w2k_d.rearrange("(c p) n -> p c n", p=128)), writes=["w2k"])
    P.dma("pool", lambda e: e.dma_start(out=w2v[:], in_=w2v_d.rearrange("(c p) n -> p c n", p=128)), writes=["w2v"])
    P.dma("pool", lambda e: e.dma_start(out=peka[:], in_=pek_d[0:128, :]), writes=["peka"])
    P.dma("pool", lambda e: e.dma_start(out=pekb[:64], in_=pek_d[128:192, :]), writes=["pekb"])
    P.dma("pool", lambda e: e.dma_start(out=pev[:], in_=pev_d[:, :]), writes=["pev"])
    P.dma("sp", lambda e: e.dma_start(out=kncb[:], in_=kncb_d[:, :]), writes=["kncb"])
    kparts = ((peka, w1ka, 128, "peka", "w1ka"), (pekb, w1kb, 64, "pekb", "w1kb"))
    nmm = 0
    for (pe_, w_, kk, pk, wk) in kparts:
        for l in range(32):
            P.op("pe", (lambda e, pe_=pe_, w_=w_, kk=kk, l=l, nmm=nmm: e.matmul(psM[3][0:1, 0:256], lhsT=pe_[:kk, l:l + 1], rhs=w_[:kk, l, :], start=(nmm == 0), stop=(nmm == 63))), reads=[pk, wk], writes=["psM3"])
            nmm += 1
    for l in range(32):
        P.op("pe", (lambda e, l=l: e.matmul(psM[3][0:1, 256:512], lhsT=pev[:, l:l + 1], rhs=w1v[:, l, :], start=(l == 0), stop=(l == 31))), reads=["pev", "w1v"], writes=["psM3"])
    P.op("act", lambda e: e.activation(out=cpe[0:1, :, :], in_=psM[3][0:1, 0:512].rearrange("p (a b) -> p a b", b=256), func=AF.Copy), reads=["psM3"], writes=["cpe"])
    P.op("dve", lambda e: e.tensor_copy(out=cpeh[0:1], in_=cpe[0:1]), reads=["cpe"], writes=["cpeh"])
    P.op("dve", lambda e: e.tensor_tensor(out=cpr[0:1], in0=cpe[0:1], in1=cpeh[0:1], op=ALU.subtract), reads=["cpe", "cpeh"], writes=["cpr"])
    P.op("dve", lambda e: e.tensor_copy(out=cpel[0:1], in_=cpr[0:1]), reads=["cpr"], writes=["cpel"])

    def compress_tile(srcK, srcV, g, n0, nb, dstK, dstV, L, kkey="fm_kc", vkey="fm_vc"):
        c0 = 16 * n0
        ln = min(16 * (nb - 1) + 32, L - c0)
        P.dma("sp", (lambda e: e.dma_start(out=rka[:, :ln], in_=srcK[g, 0:128, c0:c0 + ln])), reads=[kkey], writes=["rka"])
        P.dma("sp", (lambda e: e.dma_start(out=rkb[:64, :ln], in_=srcK[g, 128:192, c0:c0 + ln])), reads=[kkey], writes=["rkb"])
        P.dma("sp", (lambda e: e.dma_start(out=rv[:, :ln], in_=srcV[g, :, c0:c0 + ln])), reads=[vkey], writes=["rv"])
        nsg = nb + 1
        for (which, parts, col0, cidx) in (("k", ((rka, w1ka, 128, "rka", "w1ka"), (rkb, w1kb, 64, "rkb", "w1kb")), 0, 0), ("v", ((rv, w1v, 128, "rv", "w1v"),), 256, 1)):
            tot = 32 * len(parts) + 2
            i_ = 0
            for (r_, w_, kk, rk_, wk_) in parts:
                rview = r_[:, 0:16 * nsg].rearrange("p (n s) -> p n s", s=16)
                for l in range(32):
                    a0, l2 = (0, l) if l < 16 else (1, l - 16)
                    P.op("pe", (lambda e, rview=rview, w_=w_, kk=kk, l=l, a0=a0, l2=l2, i_=i_, tot=tot, col0=col0: e.matmul(psM[3][:nb, col0:col0 + 256], lhsT=rview[:kk, a0:a0 + nb, l2], rhs=w_[:kk, l, :], start=(i_ == 0), stop=False)),
                         reads=[rk_, wk_], writes=["psM3"])
                    i_ += 1
            P.op("pe", (lambda e, col0=col0, cidx=cidx: e.matmul(psM[3][:nb, col0:col0 + 256], lhsT=ones_b[0:1, :nb], rhs=cpeh[0:1, cidx, :], start=False, stop=False)), reads=["ones_b", "cpeh"], writes=["psM3"])
            P.op("pe", (lambda e, col0=col0, cidx=cidx: e.matmul(psM[3][:nb, col0:col0 + 256], lhsT=ones_b[0:1, :nb], rhs=cpel[0:1, cidx, :], start=False, stop=True)), reads=["ones_b", "cpel"], writes=["psM3"])
        P.op("act", (lambda e: e.activation(out=hid[:nb, :, :], in_=psM[3][:nb, 0:512].rearrange("p (a b) -> p a b", b=256), func=AF.Silu)), reads=["psM3"], writes=["hid"])
        for j in range(4):
            P.op("pe", (lambda e, j=j: e.transpose(out=psT[:, j * 128:j * 128 + nb], in_=hid[:nb, j // 2, (j % 2) * 128:(j % 2 + 1) * 128], identity=ident_b[:nb, :nb])), reads=["hid", "ident_b"], writes=["psT"])
        P.op("act", (lambda e: e.activation(out=hidT[:, :, :nb], in_=psT[:, 0:512].rearrange("p (j t) -> p j t", t=128)[:, :, :nb], func=AF.Copy)), reads=["psT"], writes=["hidT"])
        for c in range(2):
            P.op("pe", (lambda e, c=c: e.matmul(psM[2][:nb, 0:DKN], lhsT=hidT[:, c, :nb], rhs=w2k[:, c, :], start=(c == 0), stop=(c == 1))), reads=["hidT", "w2k"], writes=["psM2"])
        for c in range(2):
            P.op("pe", (lambda e, c=c: e.matmul(psM[2][:nb, 256:256 + DVN], lhsT=hidT[:, 2 + c, :nb], rhs=w2v[:, c, :], start=(c == 0), stop=(c == 1))), reads=["hidT", "w2v"], writes=["psM2"])
        P.op("act", (lambda e: e.activation(out=sqC[:nb, :], in_=psM[2][:nb, 0:DKN], func=AF.Square, accum_out=ssC[:nb, 0:1])), reads=["psM2"], writes=["sqC", "ssC"])
        P.op("act", (lambda e: e.activation(out=ssC[:nb, 0:1], in_=ssC[:nb, 0:1], func=AF.Sqrt, scale=1.0 / DKN, bias=EPS)), reads=["ssC"], writes=["ssC"])
        P.op("dve", (lambda e: e.reciprocal(out=ssC[:nb, 0:1], in_=ssC[:nb, 0:1])), reads=["ssC"], writes=["ssC"])
        P.op("dve", (lambda e: e.scalar_tensor_tensor(out=ckn[:nb, :], in0=psM[2][:nb, 0:DKN], scalar=ssC[:nb, 0:1], in1=kncb[:nb, :], op0=ALU.mult, op1=ALU.mult)), reads=["psM2", "ssC", "kncb"], writes=["ckn"])
        P.op("act", (lambda e: e.activation(out=cvn[:nb, :], in_=psM[2][:nb, 256:256 + DVN], func=AF.Copy)), reads=["psM2"], writes=["cvn"])
        P.op("pe", (lambda e: e.transpose(out=psT[:, 0:nb], in_=ckn[:nb, 0:128], identity=ident_b[:nb, :nb])), reads=["ckn", "ident_b"], writes=["psT"])
        P.op("pe", (lambda e: e.transpose(out=psT[:64, 128:128 + nb], in_=ckn[:nb, 128:192], identity=ident_b[:nb, :nb])), reads=["ckn", "ident_b"], writes=["psT"])
        P.op("act", (lambda e: e.activation(out=ckT[:, :, :nb], in_=psT[:, 0:256].rearrange("p (j t) -> p j t", t=128)[:, :, :nb], func=AF.Copy)), reads=["psT"], writes=["ckT"])
        P.dma("sp", (lambda e: e.dma_start(out=dstK[g, 0:128, n0:n0 + nb], in_=ckT[:, 0, :nb])), reads=["ckT"], writes=["ck_s"])
        P.dma("sp", (lambda e: e.dma_start(out=dstK[g, 128:192, n0:n0 + nb], in_=ckT[:64, 1, :nb])), reads=["ckT"], writes=["ck_s"])
        P.dma("sp", (lambda e: e.dma_start(out=dstV[g, n0:n0 + nb, :], in_=cvn[:nb, :])), reads=["cvn"], writes=["cv_s"])

    ssC = CV.alloc("ssC", [128, 4])
    sqC = CV.alloc("sqC", [128, DKN])
    for g_ in range(G):
        compress_tile(fmK["kc"], fmV, g_, 0, NCP, ckT_s, cv_s, T)
    for g_ in range(G):
        for n0_ in range(0, NCS, 128):
            compress_tile(sK["kc"], sVcT, g_, n0_, 128, ckTS_s, cvS_s, LS, kkey="sfm_kc", vkey="sfm_vc")
    if stop <= 13:
        raise _Stop()

    P.barrier()
    CV.reset()
    BIGM = 30000.0
    maskP = CV.alloc("maskP", [128, 8 + NQT, 512], BF16)
    eone = CV.alloc("eone", [128, 64, 128], BF16)
    oacc = CV.alloc("oacc", [128, 8, 512])
    qa = CV.alloc("qa", [128, 8, 512], BF16)
    qb = CV.alloc("qb", [128, 8, 512], BF16)
    gbc = CV.alloc("gbc", [128, 3, 512])
    pf = CV.alloc("pf", [128, 512])
    pn = CV.alloc("pn", [128, 512])
    rden = CV.alloc("rden", [128, 512])
    tmpo = CV.alloc("tmpo", [128, 512])
    ksa = CV.alloc("ksa", [128, T], BF16)
    ksb = CV.alloc("ksb", [128, T], BF16)
    kwa = CV.alloc("kwa", [128, T], BF16)
    kwb = CV.alloc("kwb", [128, T], BF16)
    vsr = CV.alloc("vsr", [128, NT, 128], BF16)
    vwr = CV.alloc("vwr", [128, NT, 128], BF16)
    ptb = [CV.alloc("pt%d" % i, [128, 512], BF16) for i in range(2)]
    phi = CV.alloc("phi", [128, 512], BF16)
    plo = CV.alloc("plo", [128, 512], BF16)
    zt = CV.alloc("zt", [128, 512], BF16)
    ogo = CV.alloc("ogo", [128, 512], BF16)
    nmT = CV.alloc("nmT", [128, 512], BF16)
    cka = CV.alloc("cka", [128, 128], BF16)
    ckb = CV.alloc("ckb", [128, 128], BF16)
    cvr = CV.alloc("cvr", [128, 128], BF16)
    movP = CV.alloc("movP", [128, 32], BF16)
    fmP = CV.alloc("fmP", [128, NT, 32])
    vmP = CV.alloc("vmP", [128, NT, 32])
    i3 = CV.alloc("i3", [128, 32])
    i4 = CV.alloc("i4", [128, 32])
    m16 = CV.alloc("m16", [128, 16])
    negm = CV.alloc("negm", [128, 32], BF16)
    P.dma("sp", lambda e: e.dma_start(out=maskP[:], in_=maskP_d[:, :, :]), writes=["maskP"])
    P.dma("sp", lambda e: e.dma_start(out=eone[:], in_=eone_d[:, :, :]), writes=["eone"])
    P.dma("sp", lambda e: e.dma_start(out=movP[:], in_=movP_d[:, :]), writes=["movP"])
    P.dma("sp", lambda e: e.dma_start(out=fmP[:], in_=fmP_d[:, :, :]), writes=["fmP"])
    P.dma("sp", lambda e: e.dma_start(out=vmP[:], in_=vmP_d[:, :, :]), writes=["vmP"])
    sbank = [0]

    def attend(qa_ap, qb_ap, N, Ka, Kb, nk, extras, Vap, first, last, kkeys, qkeys=("qa", "qb")):
        pg = psG[sbank[0] % 2]
        pt_ = ptb[sbank[0] % 2]
        sbank[0] += 1
        P.op("pe", (lambda e: e.matmul(pg[:nk, :N], lhsT=Ka, rhs=qa_ap, start=True, stop=False)), reads=kkeys + [qkeys[0]], writes=[pg.name])
        P.op("pe", (lambda e: e.matmul(pg[:nk, :N], lhsT=Kb, rhs=qb_ap, start=False, stop=(len(extras) == 0))), reads=kkeys + [qkeys[1]], writes=[pg.name])
        for i_, (l_, r_, ks_) in enumerate(extras):
            P.op("pe", (lambda e, l_=l_, r_=r_, i_=i_: e.matmul(pg[:nk, :N], lhsT=l_, rhs=r_, start=False, stop=(i_ == len(extras) - 1))), reads=ks_, writes=[pg.name])
        P.op("act", (lambda e: e.activation(out=pt_[:nk, :N], in_=pg[:nk, :N], func=AF.Exp)), reads=[pg.name], writes=[pt_.name])
        P.op("pe", (lambda e: e.matmul(psM[0][:, :N], lhsT=Vap, rhs=pt_[:nk, :N], start=first, stop=last)), reads=kkeys + [pt_.name], writes=["psM0"])
        P.op("pe", (lambda e: e.matmul(psM[1][:, :N], lhsT=ones_b[:nk, :], rhs=pt_[:nk, :N], start=first, stop=last)), reads=["ones_b", pt_.name], writes=["psM1"])
        return pg, pt_

    def finish_branch(N, gate_ap, dst_ap, accumulate, dkey, gkey="gbc", W=None):
        rd, tp = (rden, tmpo) if W is None else W
        rk_, tk_ = ("rden", "tmpo") if W is None else ("rdenS", "tmpoS")
        P.op("dve", (lambda e: e.tensor_scalar(out=rd[:, :N], in0=psM[1][:, :N], scalar1=1e-30, scalar2=None, op0=ALU.max)), reads=["psM1"], writes=[rk_])
        P.op("dve", (lambda e: e.reciprocal(out=rd[:, :N], in_=rd[:, :N])), reads=[rk_], writes=[rk_])
        P.op("pool", (lambda e: e.tensor_tensor(out=tp[:, :N], in0=rd[:, :N], in1=gate_ap, op=ALU.mult)), reads=[rk_, gkey], writes=[tk_])
        if accumulate:
            P.op("dve", (lambda e: e.tensor_tensor(out=tp[:, :N], in0=psM[0][:, :N], in1=tp[:, :N], op=ALU.mult)), reads=["psM0", tk_], writes=[tk_])
            P.op("pool", (lambda e: e.tensor_tensor(out=dst_ap, in0=dst_ap, in1=tp[:, :N], op=ALU.add)), reads=[tk_, dkey], writes=[dkey])
        else:
            P.op("dve", (lambda e: e.tensor_tensor(out=dst_ap, in0=psM[0][:, :N], in1=tp[:, :N], op=ALU.mult)), reads=["psM0", tk_], writes=[dkey])

    def prompt_group(g):
        P.dma("sp", (lambda e: e.dma_start(out=ksa[:], in_=fmK["ks"][g, 0:128, :])), reads=["fm_ks"], writes=["ksa"])
        P.dma("sp", (lambda e: e.dma_start(out=ksb[:64], in_=fmK["ks"][g, 128:192, :])), reads=["fm_ks"], writes=["ksb"])
        P.dma("sp", (lambda e: e.dma_start(out=ksb[64:71], in_=kaugP_d[:, :])), writes=["ksb"])
        P.dma("sp", (lambda e: e.dma_start(out=kwa[:], in_=fmK["kw"][g, 0:128, :])), reads=["fm_kw"], writes=["kwa"])
        P.dma("sp", (lambda e: e.dma_start(out=kwb[:64], in_=fmK["kw"][g, 128:192, :])), reads=["fm_kw"], writes=["kwb"])
        P.dma("sp", (lambda e: e.dma_start(out=kwb[64:71], in_=kaugP_d[:, :])), writes=["kwb"])
        P.dma("sp", (lambda e: e.dma_start(out=vsr[:], in_=kv_s["vs"][0:T, g * DVN:(g + 1) * DVN].rearrange("(n p) d -> p n d", p=128))), reads=["s_vs"], writes=["vsr"])
        P.dma("sp", (lambda e: e.dma_start(out=vwr[:], in_=kv_s["vw"][0:T, g * DVN:(g + 1) * DVN].rearrange("(n p) d -> p n d", p=128))), reads=["s_vw"], writes=["vwr"])
        P.dma("sp", (lambda e: e.dma_start(out=cka[:, :NCP], in_=ckT_s[g, 0:128, 0:NCP])), reads=["ck_s"], writes=["cka"])
        P.dma("sp", (lambda e: e.dma_start(out=ckb[:64, :NCP], in_=ckT_s[g, 128:192, 0:NCP])), reads=["ck_s"], writes=["ckb"])
        P.dma("sp", (lambda e: e.dma_start(out=ckb[64:71, :NCP], in_=kaugC_d[:, 0:NCP])), writes=["ckb"])
        P.dma("sp", (lambda e: e.dma_start(out=cvr[:NCP, :], in_=cv_s[g, 0:NCP, :])), reads=["cv_s"], writes=["cvr"])
        for qt in range(NQT):
            prompt_qtile(g, qt)

    def prompt_qtile(g, qt):
        q0 = 512 * qt
        hs = slice(g * 8, (g + 1) * 8)
        P.dma("sp", (lambda e: e.dma_start(out=qa[:], in_=qT_s[hs, 0:128, q0:q0 + 512].rearrange("h p t -> p h t"))), reads=["qT_s"], writes=["qa"])
        P.dma("sp", (lambda e: e.dma_start(out=qb[:64], in_=qT_s[hs, 128:192, q0:q0 + 512].rearrange("h p t -> p h t"))), reads=["qT_s"], writes=["qb"])
        P.dma("sp", (lambda e: e.dma_start(out=qb[64:71], in_=qaugP_d[:, hs, q0:q0 + 512])), writes=["qb"])

        def load_gate(hn):
            P.dma("sp", (lambda e: e.dma_start(out=gbc[:], in_=gT_s[hn * 3:hn * 3 + 3, q0:q0 + 512].partition_broadcast(128))), reads=["gT_s"], writes=["gbc"])

        for r in range(8):
            hn = g * 8 + r
            load_gate(hn)
            pg, pt_ = attend(qa[:, r, :], qb[:71, r, :], 512, cka[:, :NCP], ckb[:71, :NCP], NCP,
                             [(ident_b[:NCP, :NCP], maskP[:NCP, 8 + qt, :], ["ident_b", "maskP"])], cvr[:NCP, :], True, True, ["cka", "ckb", "cvr"])
            P.op("act", (lambda e, pg=pg: e.activation(out=pf[:NCP, :], in_=pg[:NCP, :512], func=AF.Exp)), reads=[pg.name], writes=["pf"])
            finish_branch(512, gbc[:, 0, :], oacc[:, r, :], False, "oacc")
            P.op("dve", (lambda e: e.tensor_tensor(out=pn[:NCP, :], in0=pf[:NCP, :], in1=rden[:NCP, :], op=ALU.mult)), reads=["pf", "rden"], writes=["pn"])
            P.op("dve", (lambda e: e.tensor_copy(out=phi[:NCP, :], in_=pn[:NCP, :])), reads=["pn"], writes=["phi"])
            P.op("dve", (lambda e: e.tensor_tensor(out=pn[:NCP, :], in0=pn[:NCP, :], in1=phi[:NCP, :], op=ALU.subtract)), reads=["pn", "phi"], writes=["pn"])
            P.op("dve", (lambda e: e.tensor_copy(out=plo[:NCP, :], in_=pn[:NCP, :])), reads=["pn"], writes=["plo"])
            for qs in range(4):
                P.op("pe", (lambda e, qs=qs, r=r: e.matmul(psM[2][:, qs * 32:(qs + 1) * 32], lhsT=phi[:NCP, qs * 128:(qs + 1) * 128], rhs=movP[:NCP, :], start=(r == 0), stop=False)), reads=["phi", "movP"], writes=["psM2"])
                P.op("pe", (lambda e, qs=qs, r=r: e.matmul(psM[2][:, qs * 32:(qs + 1) * 32], lhsT=plo[:NCP, qs * 128:(qs + 1) * 128], rhs=movP[:NCP, :], start=False, stop=(r == 7))), reads=["plo", "movP"], writes=["psM2"])
        for qs in range(4):
            itq = qt * 4 + qs
            P.op("dve", (lambda e, qs=qs, itq=itq: e.tensor_tensor(out=i3[:], in0=psM[2][:, qs * 32:(qs + 1) * 32], in1=fmP[:, itq, :], op=ALU.max)), reads=["psM2", "fmP"], writes=["i3"])
            P.op("dve", (lambda e, itq=itq: e.tensor_tensor(out=i3[:], in0=i3[:], in1=vmP[:, itq, :], op=ALU.min)), reads=["i3", "vmP"], writes=["i3"])
            P.op("dve", (lambda e: e.max(out=m16[:, 0:8], in_=i3[:])), reads=["i3"], writes=["m16"])
            P.op("dve", (lambda e: e.match_replace(out=i4[:], in_to_replace=m16[:, 0:8], in_values=i3[:], imm_value=-3e38)), reads=["i3", "m16"], writes=["i4"])
            P.op("dve", (lambda e: e.max(out=m16[:, 8:16], in_=i4[:])), reads=["i4"], writes=["m16"])
            P.op("dve", (lambda e: e.tensor_scalar(out=i4[:], in0=i3[:], scalar1=m16[:, 15:16], scalar2=BIGM, op0=ALU.is_ge, op1=ALU.mult)), reads=["i3", "m16"], writes=["i4"])
            P.op("dve", (lambda e: e.tensor_scalar(out=negm[:], in0=i4[:], scalar1=-BIGM, scalar2=None, op0=ALU.add)), reads=["i4"], writes=["negm"])
            P.op("pe", (lambda e, qs=qs: e.transpose(out=psT[:32, qs * 128:(qs + 1) * 128], in_=negm[:, :], identity=ident_b[:, :])), reads=["negm", "ident_b"], writes=["psT"])
        P.op("act", (lambda e: e.activation(out=nmT[:32, :], in_=psT[:32, 0:512], func=AF.Copy)), reads=["psT"], writes=["nmT"])
        for r in range(8):
            hn = g * 8 + r
            load_gate(hn)
            P.dma("sp", (lambda e, hn=hn: e.dma_start(out=zt[:], in_=zs_s[hn, :, q0:q0 + 512])), reads=["zs_s"], writes=["zt"])
            kts = list(range(0, 4 * qt + 4))
            for kt in kts:
                ex = [(eone[:32, kt, :], nmT[:32, :], ["eone", "nmT"])]
                if kt >= 4 * qt:
                    ex.append((ident_b[:, :], maskP[:, kt - 4 * qt, :], ["ident_b", "maskP"]))
                attend(qa[:, r, :], qb[:71, r, :], 512, ksa[:, kt * 128:(kt + 1) * 128], ksb[:71, kt * 128:(kt + 1) * 128], 128, ex, vsr[:, kt, :], kt == kts[0], kt == kts[-1], ["ksa", "ksb", "vsr"])
            finish_branch(512, gbc[:, 1, :], oacc[:, r, :], True, "oacc")
            kts = list(range(max(0, 4 * qt - 4), 4 * qt + 4))
            for kt in kts:
                d_ = kt - 4 * qt
                mi = d_ if d_ >= 0 else 8 + d_
                ex = [(ident_b[:, :], maskP[:, mi, :], ["ident_b", "maskP"])]
                attend(qa[:, r, :], qb[:71, r, :], 512, kwa[:, kt * 128:(kt + 1) * 128], kwb[:71, kt * 128:(kt + 1) * 128], 128, ex, vwr[:, kt, :], kt == kts[0], kt == kts[-1], ["kwa", "kwb", "vwr"])
            finish_branch(512, gbc[:, 2, :], oacc[:, r, :], True, "oacc")
            P.op("dve", (lambda e, r=r: e.tensor_tensor(out=ogo[:], in0=oacc[:, r, :], in1=zt[:], op=ALU.mult)), reads=["oacc", "zt"], writes=["ogo"])
            P.dma("sp", (lambda e, hn=hn: e.dma_start(out=og_d[qt * 4:(qt + 1) * 4].rearrange("n p (h t) -> p n h t", t=128)[:, :, hn, :], in_=ogo[:].rearrange("p (n t) -> p n t", t=128))), reads=["ogo"], writes=["og_d"])

    for g_ in range(G):
        prompt_group(g_)

    P.barrier()
    CV.reset()
    rdenS = CV.alloc("rdenS", [128, 512])
    tmpoS = CV.alloc("tmpoS", [128, 512])
    WS = (rdenS, tmpoS)
    ptb2 = [CV.alloc("ptS%d" % i, [128, 512], BF16) for i in range(2)]
    ptb[0], ptb[1] = ptb2[0], ptb2[1]
    eoneS = CV.alloc("eoneS", [128, 64, 128], BF16)
    maskS = CV.alloc("maskS", [128, 3, 64], BF16)
    movS = CV.alloc("movS", [128, NCS // 128, NSB], BF16)
    fmS = CV.alloc("fmS", [TS, NSB])
    vmS = CV.alloc("vmS", [TS, NSB])
    qaS = CV.alloc("qaS", [128, 8, TS], BF16)
    qbS = CV.alloc("qbS", [128, 8, TS], BF16)
    gbcS = CV.alloc("gbcS", [128, 3, 64])
    ztS = CV.alloc("ztS", [128, 8, TS], BF16)
    oaccS = CV.alloc("oaccS", [128, 64])
    ogoS = CV.alloc("ogoS", [128, 8, TS], BF16)
    ckaS = CV.alloc("ckaS", [128, NCS], BF16)
    ckbS = CV.alloc("ckbS", [128, NCS], BF16)
    cvS = CV.alloc("cvS", [128, NCS // 128, 128], BF16)
    pfS = CV.alloc("pfS", [128, NCS // 128, 64])
    pnS = CV.alloc("pnS", [128, 64])
    phiS = CV.alloc("phiS", [128, 64], BF16)
    ploS = CV.alloc("ploS", [128, 64], BF16)
    i3S = CV.alloc("i3S", [TS, NSB])
    i4S = CV.alloc("i4S", [TS, NSB])
    m16S = CV.alloc("m16S", [TS, 16])
    negmS = CV.alloc("negmS", [TS, NCK * 128 + 128], BF16)
    nmTS = CV.alloc("nmTS", [128, NCK, 64], BF16)
    kcaS = [CV.alloc("kcaS%d" % i, [128, 1024], BF16) for i in range(2)]
    kcbS = [CV.alloc("kcbS%d" % i, [128, 1024], BF16) for i in range(2)]
    vcS = [CV.alloc("vcS%d" % i, [128, 8, 128], BF16) for i in range(2)]
    knaS = CV.alloc("knaS", [128, TS], BF16)
    knbS = CV.alloc("knbS", [128, TS], BF16)
    vnS = CV.alloc("vnS", [TS, 128], BF16)
    kwaS = CV.alloc("kwaS", [128, WLC + TS], BF16)
    kwbS = CV.alloc("kwbS", [128, WLC + TS], BF16)
    vwS = CV.alloc("vwS", [128, 5, 128], BF16)
    P.dma("sp", lambda e: e.dma_start(out=eoneS[:], in_=eone_d[:, :, :]), writes=["eoneS"])
    P.dma("sp", lambda e: e.dma_start(out=maskS[:], in_=maskS_d[:, :, :]), writes=["maskS"])
    P.dma("sp", lambda e: e.dma_start(out=movS[:], in_=movS_d[:, :, :]), writes=["movS"])
    P.dma("sp", lambda e: e.dma_start(out=fmS[:], in_=fmS_d[:, :]), writes=["fmS"])
    P.dma("sp", lambda e: e.dma_start(out=vmS[:], in_=vmS_d[:, :]), writes=["vmS"])
    QK_S = ("qaS", "qbS")
    NCT = NCS // 128

    def sample_group(g):
        hs = slice(g * 8, (g + 1) * 8)
        gc_ = slice(g * DVN, (g + 1) * DVN)
        qaf = qaS[:].rearrange("p a b -> p (a b)")
        qbf = qbS[:].rearrange("p a b -> p (a b)")
        P.dma("sp", (lambda e: e.dma_start(out=qaS[:], in_=qT_s[hs, 0:128, T:T + TS].rearrange("h p t -> p h t"))), reads=["qT_s"], writes=["qaS"])
        P.dma("sp", (lambda e: e.dma_start(out=qbS[:64], in_=qT_s[hs, 128:192, T:T + TS].rearrange("h p t -> p h t"))), reads=["qT_s"], writes=["qbS"])
        P.dma("sp", (lambda e: e.dma_start(out=qbS[64:71], in_=qaugS_d[:, hs, :])), writes=["qbS"])
        for x_ in range(3):
            P.dma("sp", (lambda e, x_=x_: e.dma_start(out=gbcS[:, x_, :].rearrange("p (r q) -> p r q", q=TS), in_=gT_s[g * 24:(g + 1) * 24, T:T + TS].rearrange("(r x) q -> x r q", x=3)[x_].partition_broadcast(128))), reads=["gT_s"], writes=["gbcS"])
        P.dma("sp", (lambda e: e.dma_start(out=ztS[:], in_=zs_s[hs, :, T:T + TS].rearrange("h p t -> p h t"))), reads=["zs_s"], writes=["ztS"])
        P.dma("sp", (lambda e: e.dma_start(out=ckaS[:], in_=ckTS_s[g, 0:128, :])), reads=["ck_s"], writes=["ckaS"])
        P.dma("sp", (lambda e: e.dma_start(out=ckbS[:64], in_=ckTS_s[g, 128:192, :])), reads=["ck_s"], writes=["ckbS"])
        P.dma("sp", (lambda e: e.dma_start(out=ckbS[64:71], in_=kaugCS_d[:, :])), writes=["ckbS"])
        P.dma("sp", (lambda e: e.dma_start(out=cvS[:], in_=cvS_s[g].rearrange("(n p) d -> p n d", p=128))), reads=["cv_s"], writes=["cvS"])
        for ct in range(NCT):
            ex = [(ident_b[:, :], maskS[:, 2, :], ["ident_b", "maskS"])] if ct == NCT - 1 else []
            pg, pt_ = attend(qaf, qbf[:71], 64, ckaS[:, ct * 128:(ct + 1) * 128], ckbS[:71, ct * 128:(ct + 1) * 128], 128, ex, cvS[:, ct, :], ct == 0, ct == NCT - 1, ["ckaS", "ckbS", "cvS"], QK_S)
            P.op("act", (lambda e, pg=pg, ct=ct: e.activation(out=pfS[:, ct, :], in_=pg[:, :64], func=AF.Exp)), reads=[pg.name], writes=["pfS"])
        finish_branch(64, gbcS[:, 0, :], oaccS[:, :], False, "oaccS", "gbcS", WS)
        for ct in range(NCT):
            P.op("dve", (lambda e, ct=ct: e.tensor_tensor(out=pnS[:], in0=pfS[:, ct, :], in1=rdenS[:, :64], op=ALU.mult)), reads=["pfS", "rdenS"], writes=["pnS"])
            P.op("dve", (lambda e: e.tensor_copy(out=phiS[:], in_=pnS[:])), reads=["pnS"], writes=["phiS"])
            P.op("dve", (lambda e: e.tensor_tensor(out=pnS[:], in0=pnS[:], in1=phiS[:], op=ALU.subtract)), reads=["pnS", "phiS"], writes=["pnS"])
            P.op("dve", (lambda e: e.tensor_copy(out=ploS[:], in_=pnS[:])), reads=["pnS"], writes=["ploS"])
            for r in range(8):
                first = (ct == 0 and r == 0)
                last = (ct == NCT - 1 and r == 7)
                P.op("pe", (lambda e, ct=ct, r=r, first=first: e.matmul(psM[2][:TS, 0:NSB], lhsT=phiS[:, r * TS:(r + 1) * TS], rhs=movS[:, ct, :], start=first, stop=False)), reads=["phiS", "movS"], writes=["psM2"])
                P.op("pe", (lambda e, ct=ct, r=r, last=last: e.matmul(psM[2][:TS, 0:NSB], lhsT=ploS[:, r * TS:(r + 1) * TS], rhs=movS[:, ct, :], start=False, stop=last)), reads=["ploS", "movS"], writes=["psM2"])
        P.op("dve", (lambda e: e.tensor_tensor(out=i3S[:], in0=psM[2][:TS, 0:NSB], in1=fmS[:], op=ALU.max)), reads=["psM2", "fmS"], writes=["i3S"])
        P.op("dve", (lambda e: e.tensor_tensor(out=i3S[:], in0=i3S[:], in1=vmS[:], op=ALU.min)), reads=["i3S", "vmS"], writes=["i3S"])
        P.op("dve", (lambda e: e.max(out=m16S[:, 0:8], in_=i3S[:])), reads=["i3S"], writes=["m16S"])
        P.op("dve", (lambda e: e.match_replace(out=i4S[:], in_to_replace=m16S[:, 0:8], in_values=i3S[:], imm_value=-3e38)), reads=["i3S", "m16S"], writes=["i4S"])
        P.op("dve", (lambda e: e.max(out=m16S[:, 8:16], in_=i4S[:])), reads=["i4S"], writes=["m16S"])
        P.op("dve", (lambda e: e.tensor_scalar(out=i4S[:], in0=i3S[:], scalar1=m16S[:, 15:16], scalar2=BIGM, op0=ALU.is_ge, op1=ALU.mult)), reads=["i3S", "m16S"], writes=["i4S"])
        P.op("dve", (lambda e: e.tensor_scalar(out=negmS[:, 0:NSB], in0=i4S[:], scalar1=-BIGM, scalar2=None, op0=ALU.add)), reads=["i4S"], writes=["negmS"])
        for c in range(NCK):
            cw = min(128, NBP - 128 * c)
            P.op("pe", (lambda e, c=c, cw=cw: e.transpose(out=psT[:cw, c * 128:c * 128 + TS], in_=negmS[:, 128 * c:128 * c + cw], identity=ident_b[:TS, :TS])), reads=["negmS", "ident_b"], writes=["psT"])
            for r in range(8):
                eng = "act" if r % 2 == 0 else "dve"
                if eng == "act":
                    P.op("act", (lambda e, c=c, cw=cw, r=r: e.activation(out=nmTS[:cw, c, r * TS:(r + 1) * TS], in_=psT[:cw, c * 128:c * 128 + TS], func=AF.Copy)), reads=["psT"], writes=["nmTS"])
                else:
                    P.op("dve", (lambda e, c=c, cw=cw, r=r: e.tensor_copy(out=nmTS[:cw, c, r * TS:(r + 1) * TS], in_=psT[:cw, c * 128:c * 128 + TS])), reads=["psT"], writes=["nmTS"])
        nchunk = (NPG + 7) // 8
        for ch in range(nchunk):
            i2 = ch % 2
            k0 = ch * 1024
            nt_ = min(8, NPG - ch * 8)
            ka_, kb_, v_ = kcaS[i2], kcbS[i2], vcS[i2]
            P.dma("sp", (lambda e, ka_=ka_, k0=k0, nt_=nt_: e.dma_start(out=ka_[:, :nt_ * 128], in_=sK["ks"][g, 0:128, k0:k0 + nt_ * 128])), reads=["sfm_ks"], writes=[ka_.name])
            P.dma("sp", (lambda e, kb_=kb_, k0=k0, nt_=nt_: e.dma_start(out=kb_[:64, :nt_ * 128], in_=sK["ks"][g, 128:192, k0:k0 + nt_ * 128])), reads=["sfm_ks"], writes=[kb_.name])
            P.dma("sp", (lambda e, kb_=kb_, k0=k0, nt_=nt_: e.dma_start(out=kb_[64:71, :nt_ * 128], in_=kaugS_d[:, k0:k0 + nt_ * 128])), writes=[kb_.name])
            P.dma("sp", (lambda e, v_=v_, k0=k0, nt_=nt_: e.dma_start(out=v_[:, :nt_, :], in_=sVs[k0:k0 + nt_ * 128, gc_].rearrange("(n p) d -> p n d", p=128))), reads=["s_svs"], writes=[v_.name])
            for j in range(nt_):
                kt = ch * 8 + j
                c = (2 * kt) // 128
                cw = min(128, NBP - 128 * c)
                ex = [(eoneS[:cw, kt % 64, :], nmTS[:cw, c, :], ["eoneS", "nmTS"])]
                attend(qaf, qbf[:71], 64, ka_[:, j * 128:(j + 1) * 128], kb_[:71, j * 128:(j + 1) * 128], 128, ex, v_[:, j, :], kt == 0, False, [ka_.name, kb_.name, v_.name], QK_S)
        P.dma("sp", (lambda e: e.dma_start(out=knaS[:], in_=sK["ks"][g, 0:128, P0:P0 + TS])), reads=["sfm_ks"], writes=["knaS"])
        P.dma("sp", (lambda e: e.dma_start(out=knbS[:64], in_=sK["ks"][g, 128:192, P0:P0 + TS])), reads=["sfm_ks"], writes=["knbS"])
        P.dma("sp", (lambda e: e.dma_start(out=knbS[64:71], in_=kaugS_d[:, P0:P0 + TS])), writes=["knbS"])
        P.dma("sp", (lambda e: e.dma_start(out=vnS[:], in_=sVs[P0:P0 + TS, gc_])), reads=["s_svs"], writes=["vnS"])
        attend(qaf, qbf[:71], 64, knaS[:, :], knbS[:71, :], TS, [(ident_b[:TS, :TS], maskS[:TS, 0, :], ["ident_b", "maskS"])], vnS[:, :], False, True, ["knaS", "knbS", "vnS"], QK_S)
        finish_branch(64, gbcS[:, 1, :], oaccS[:, :], True, "oaccS", "gbcS", WS)
        P.dma("sp", (lambda e: e.dma_start(out=kwaS[:], in_=sKw[g, 0:128, :])), reads=["sfm_kw"], writes=["kwaS"])
        P.dma("sp", (lambda e: e.dma_start(out=kwbS[:64], in_=sKw[g, 128:192, :])), reads=["sfm_kw"], writes=["kwbS"])
        P.dma("sp", (lambda e: e.dma_start(out=kwbS[64:71], in_=kaugS_d[:, P0 - WLC:P0 + TS])), writes=["kwbS"])
        P.dma("sp", (lambda e: e.dma_start(out=vwS[:, 0:4, :], in_=sVw[0:WLC, gc_].rearrange("(n p) d -> p n d", p=128))), reads=["s_svw"], writes=["vwS"])
        P.dma("sp", (lambda e: e.dma_start(out=vwS[:TS, 4, :], in_=sVw[WLC:WLC + TS, gc_])), reads=["s_svw"], writes=["vwS"])
        for j in range(5):
            nk = 128 if j < 4 else TS
            ex = []
            if j == 0:
                ex = [(ident_b[:, :], maskS[:, 1, :], ["ident_b", "maskS"])]
            if j == 4:
                ex = [(ident_b[:TS, :TS], maskS[:TS, 0, :], ["ident_b", "maskS"])]
            attend(qaf, qbf[:71], 64, kwaS[:, j * 128:j * 128 + nk], kwbS[:71, j * 128:j * 128 + nk], nk, ex, vwS[:nk, j, :], j == 0, j == 4, ["kwaS", "kwbS", "vwS"], QK_S)
        finish_branch(64, gbcS[:, 2, :], oaccS[:, :], True, "oaccS", "gbcS", WS)
        P.op("dve", (lambda e: e.tensor_tensor(out=ogoS[:].rearrange("p a b -> p (a b)"), in0=oaccS[:, :], in1=ztS[:].rearrange("p a b -> p (a b)"), op=ALU.mult)), reads=["oaccS", "ztS"], writes=["ogoS"])
        P.dma("sp", (lambda e: e.dma_start(out=og_d[NT].rearrange("p (h t) -> p h t", t=128)[:, hs, 0:TS], in_=ogoS[:])), reads=["ogoS"], writes=["og_d"])

    if not cfg.get("nosample"):
        for g_ in range(G):
            sample_group(g_)
    if stop <= 14:
        raise _Stop()
    P.barrier()
    outproj(wout2_d, o_x1, "x1src", o_y, "o_y")


def simulate_deadlock(P):
    sem = {}
    pc = {e: 0 for e in ENGS}
    progress = True
    while progress:
        progress = False
        for e in ENGS:
            while pc[e] < len(P.streams[e]):
                waits, fn, sk, inc = P.streams[e][pc[e]]
                if all(sem.get(k, 0) >= v for k, v in waits):
                    sem[sk] = sem.get(sk, 0) + inc
                    pc[e] += 1
                    progress = True
                else:
                    break
    stuck = {e: (pc[e], len(P.streams[e])) for e in ENGS if pc[e] < len(P.streams[e])}
    return stuck, sem


def _chan(a):
    return np.ascontiguousarray(a.T.reshape(-1, 128, a.shape[0]).transpose(1, 0, 2))


def _unchan(a):
    return np.ascontiguousarray(a.transpose(2, 1, 0).reshape(a.shape[2], -1))


def kernel(x_prompt, x_sample, state_delta, state_conv, cache_cmp_k, cache_cmp_v,
           cache_slc_k, cache_slc_v, cache_win_k, cache_win_v, page_table,
           norm_dn, w_in_dn, conv_w_dn, a_log_dn, dt_bias_dn, out_norm_dn, w_out_dn,
           norm_nsa, w_in_nsa, q_norm_nsa, k_norm_cmp, k_norm_slc, k_norm_win,
           cmp_pe_k, cmp_w1_k, cmp_w2_k, cmp_pe_v, cmp_w1_v, cmp_w2_v, w_out_nsa):
    f = lambda a: np.ascontiguousarray(np.asarray(a, dtype=np.float32))
    B, T, D = x_prompt.shape
    NS, TS, _ = x_sample.shape
    H = a_log_dn.shape[1]
    cfg = dict(D=D, H=H, T=T, TB=128, G=cache_cmp_k.shape[3], NPG=page_table.shape[1], NPOOL=cache_cmp_k.shape[1])
    nc = build_program(cfg)
    cst = host_consts()
    shared = {
        "norm_dn_b": np.ascontiguousarray(np.broadcast_to(f(norm_dn[0]), (128, D))),
        "w_in_dn": f(w_in_dn[0]), "conv_w": _chan(f(conv_w_dn[0])),
        "a_log_b": np.ascontiguousarray(np.broadcast_to(f(a_log_dn[0]), (128, H))),
        "dt_bias_b": np.ascontiguousarray(np.broadcast_to(f(dt_bias_dn[0]), (128, H))),
        "out_norm_c": f(out_norm_dn[0]).reshape(128, 1), "w_out_dn": f(w_out_dn[0]),
    }
    for k, v in cst.items():
        shared["c_" + k] = v
    shared["norm_nsa_b"] = np.ascontiguousarray(np.broadcast_to(f(norm_nsa[0]), (128, D)))
    shared["w_in_nsa"] = f(w_in_nsa[0])
    shared["knorm_b"] = np.ascontiguousarray(np.broadcast_to(np.stack([f(k_norm_slc[0]), f(k_norm_win[0])]), (128, 2, 192)))
    HNn = q_norm_nsa.shape[1] and (w_out_nsa.shape[1] // 128)
    shared.update(nsa_consts(T, HNn))
    qn = f(q_norm_nsa[0])
    qc = np.zeros((128, 2), np.float32)
    qc[:, 0] = qn[0:128]
    qc[:64, 1] = qn[128:192]
    shared["qnorm_c"] = qc
    shared["w1k_t"] = np.ascontiguousarray(f(cmp_w1_k[0]).transpose(1, 0, 2))
    shared["w1v_t"] = np.ascontiguousarray(f(cmp_w1_v[0]).transpose(1, 0, 2))
    shared["w2k"] = f(cmp_w2_k[0])
    shared["w2v"] = f(cmp_w2_v[0])
    shared["pek_t"] = np.ascontiguousarray(f(cmp_pe_k[0]).T)
    shared["pev_t"] = np.ascontiguousarray(f(cmp_pe_v[0]).T)
    shared["knc_b"] = np.ascontiguousarray(np.broadcast_to(f(k_norm_cmp[0]), (128, 192)))
    shared["w_out_nsa"] = f(w_out_nsa[0])
    NPGn = page_table.shape[1]
    NPOOLn = cache_cmp_k.shape[1]
    shared.update(nsa_consts_sample(HNn, NPGn))
    shared["pool_ck"] = f(cache_cmp_k[0]).reshape(NPOOLn * 128, -1)
    shared["pool_cv"] = f(cache_cmp_v[0]).reshape(NPOOLn * 128, -1)
    shared["pool_sk"] = f(cache_slc_k[0]).reshape(NPOOLn * 128, -1)
    shared["pool_sv"] = f(cache_slc_v[0]).reshape(NPOOLn * 128, -1)
    in_maps = []
    for c in range(8):
        m = dict(shared)
        m["x"] = np.concatenate([f(x_prompt[c % B]), f(x_sample[c])], 0)
        m["conv0_in"] = _chan(f(state_conv[0, c]))
        m["s0"] = f(state_delta[0, c])
        m["page_tab"] = np.ascontiguousarray(np.asarray(page_table[c], dtype=np.int32).reshape(1, -1))
        m["cwin_k"] = f(cache_win_k[0, c]).reshape(cache_win_k.shape[2], -1)
        m["cwin_v"] = f(cache_win_v[0, c]).reshape(cache_win_v.shape[2], -1)
        in_maps.append(m)
    res = run_bass_kernel_spmd(nc, in_maps, core_ids=list(range(8))).results
    y_prompt = np.stack([res[b]["o_y"][:T] for b in range(B)])
    y_sample = np.stack([res[c]["o_y"][T:] for c in range(NS)])
    p_sd = np.stack([res[b]["o_sdP"] for b in range(B)])[None]
    p_sc = np.stack([_unchan(res[b]["o_scP"]) for b in range(B)])[None]
    s_sd = np.stack([res[c]["o_sdS"] for c in range(NS)])[None]
    s_sc = np.stack([_unchan(res[c]["o_scS"]) for c in range(NS)])[None]
    G, DK, DV = cache_cmp_k.shape[3], cache_cmp_k.shape[4], cache_cmp_v.shape[4]
    WL = cache_win_k.shape[2]
    z = lambda *s: np.zeros(s, np.float32)
    wlp = min(512, T)
    gp = lambda key, n, dd: np.stack([res[b][key].reshape(n, G, dd) for b in range(B)])[None]
    gs = lambda key, n, dd: np.stack([res[c][key].reshape(n, G, dd) for c in range(NS)])[None]
    return (y_prompt.astype(np.float32), y_sample.astype(np.float32), p_sd, p_sc,
            gp("o_pck", T, DK), gp("o_pcv", T, DV), gp("o_psk", T, DK), gp("o_psv", T, DV),
            gp("o_pwk", wlp, DK), gp("o_pwv", wlp, DV),
            s_sd, s_sc,
            gs("o_sck", TS, DK), gs("o_scv", TS, DV), gs("o_ssk", TS, DK), gs("o_ssv", TS, DV),
            gs("o_swk", WL, DK), gs("o_swv", WL, DV))
```

```python
import math
from contextlib import ExitStack
import numpy as np
import ml_dtypes
import concourse.bass as bass
import concourse.mybir as mybir
from concourse.bass_utils import run_bass_kernel_spmd

F32 = mybir.dt.float32
BF16 = mybir.dt.bfloat16
AF = mybir.ActivationFunctionType
ALU = mybir.AluOpType
AX = mybir.AxisListType
EPS = 1e-6
ENGS = ("pe", "act", "dve", "pool", "sp")


class Prog:
    def __init__(self, ndma=40):
        self.streams = {e: [] for e in ENGS}
        self.cnt = {e: 0 for e in ("pe", "act", "dve", "pool")}
        self.waited = {e: {} for e in ENGS}
        self.lastw = {}
        self.readers = {}
        self.ndma = ndma
        self.dma_i = 0
        self.dma_final = {}
        self.floor = {e: {} for e in ENGS}
        import threading
        self._tl = threading.local()

    def barrier(self):
        snap = {k: v for k, v in self.cnt.items() if v}
        snap.update(self.dma_final)
        for e in ENGS:
            self.floor[e] = dict(snap)

    def _need(self, eng, reads, writes):
        need = dict(self.floor[eng])
        self.floor[eng] = {}

        def add(tok):
            k, v = tok
            if need.get(k, 0) < v:
                need[k] = v

        for r in reads:
            t = self.lastw.get(r)
            if t:
                add(t)
        for w in writes:
            t = self.lastw.get(w)
            if t:
                add(t)
            for k, v in self.readers.get(w, {}).items():
                add((k, v))
        out = []
        for k, v in need.items():
            if eng == "pe" and k == "pe":
                continue
            if self.waited[eng].get(k, 0) >= v:
                continue
            self.waited[eng][k] = v
            out.append((k, v))
        return out

    def _commit(self, tok, reads, writes):
        for w in writes:
            self.lastw[w] = tok
            self.readers[w] = {}
        for r in reads:
            if r not in writes:
                d = self.readers.setdefault(r, {})
                if d.get(tok[0], 0) < tok[1]:
                    d[tok[0]] = tok[1]

    def op(self, eng, fn, reads=(), writes=()):
        writes = list(writes) + [r for r in reads if r.startswith("ps") and r not in writes]
        waits = self._need(eng, reads, writes)
        self.cnt[eng] += 1
        tok = (eng, self.cnt[eng])
        self.streams[eng].append((waits, fn, eng, 1))
        self._commit(tok, reads, writes)
        self._yield()

    def dma(self, eng, fn, reads=(), writes=()):
        i = self.dma_i
        self.dma_i += 1
        slot, gen = i % self.ndma, i // self.ndma
        key = ("d", slot)
        waits = self._need(eng, reads, writes)
        if gen > 0 and self.waited[eng].get(key, 0) < 16 * gen:
            waits.append((key, 16 * gen))
            self.waited[eng][key] = 16 * gen
        tok = (key, 16 * (gen + 1))
        self.dma_final[key] = 16 * (gen + 1)
        self.streams[eng].append((waits, fn, key, 16))
        self._commit(tok, reads, writes)
        self._yield()

    def _yield(self):
        h = getattr(self._tl, "hook", None)
        if h is not None:
            h()

    def interleave(self, fa, fb):
        import threading
        sems = {"a": threading.Semaphore(0), "b": threading.Semaphore(0)}
        done = {"a": False, "b": False}
        errs = []

        def runner(f, me, other):
            sems[me].acquire()

            def hook():
                if not done[other]:
                    sems[other].release()
                    sems[me].acquire()
            self._tl.hook = hook
            try:
                f()
            except BaseException as ex:
                errs.append(ex)
            finally:
                done[me] = True
                self._tl.hook = None
                sems[other].release()

        ta = threading.Thread(target=runner, args=(fa, "a", "b"))
        tb = threading.Thread(target=runner, args=(fb, "b", "a"))
        ta.start()
        tb.start()
        sems["a"].release()
        ta.join()
        tb.join()
        if errs:
            raise errs[0]

    def emit(self, nc, stack):
        sems = {}
        for k in list(self.cnt.keys()) + list(self.dma_final.keys()):
            nm = k if isinstance(k, str) else "d%d" % k[1]
            sems[k] = stack.enter_context(nc.semaphore("s_" + nm))
        block = stack.enter_context(nc.Block())

        def run(name, e):
            for waits, fn, sk, inc in self.streams[name]:
                for k, v in waits:
                    e.wait_ge(sems[k], v)
                fn(e).then_inc(sems[sk], inc)
            if name == "sp":
                for k, v in self.dma_final.items():
                    e.wait_ge(sems[k], v)
                for k, v in self.cnt.items():
                    if v:
                        e.wait_ge(sems[k], v)

        @block.tensor
        def _(e):
            run("pe", e)

        @block.scalar
        def _(e):
            run("act", e)

        @block.vector
        def _(e):
            run("dve", e)

        @block.gpsimd
        def _(e):
            run("pool", e)

        @block.sync
        def _(e):
            run("sp", e)


def _bf(x):
    return np.asarray(x, dtype=np.float64).astype(ml_dtypes.bfloat16)


def _split_bf(x, n):
    x = np.asarray(x, dtype=np.float64)
    out = []
    for _ in range(n):
        h = x.astype(ml_dtypes.bfloat16)
        out.append(h)
        x = x - h.astype(np.float64)
    return out


def _kaug(pos):
    pos = np.asarray(pos, dtype=np.int64)
    ph, pl = 128 * (pos // 128), pos % 128
    one = np.ones_like(pos)
    return np.stack([ph, ph, pl, pl, one, one, one]).astype(np.float64).astype(ml_dtypes.bfloat16)


def _qaug(slopes, tq):
    HN, Tq = len(slopes), len(tq)
    sh, sl = _split_bf(slopes, 2)
    sp = sh.astype(np.float64) + sl.astype(np.float64)
    n3 = _split_bf(-sp[:, None] * np.asarray(tq, np.float64)[None, :], 3)
    b = lambda v: np.broadcast_to(v[:, None], (HN, Tq))
    return np.stack([b(sh), b(sl), b(sh), b(sl), n3[0], n3[1], n3[2]]).astype(ml_dtypes.bfloat16)


def nsa_consts_sample(HN, NPG, TS=8):
    c = {}
    hh = np.arange(1, HN + 1, dtype=np.float32)
    slopes = np.exp2(np.float32(-8.0) * hh / np.float32(HN)).astype(np.float32)
    P0 = NPG * 128
    LS, NCS, NBP = P0 + 16, NPG * 8, 2 * NPG
    NSB = NBP + 1
    c["kaugS"] = np.ascontiguousarray(_kaug(np.arange(LS)))
    c["kaugCS"] = np.ascontiguousarray(_kaug(16 * np.arange(NCS) + 31))
    c["qaugS"] = np.ascontiguousarray(_qaug(slopes, P0 + np.arange(TS)))
    kl = np.arange(128)[:, None]
    q = (np.arange(64) % TS)[None, :]
    m = np.zeros((128, 3, 64), np.float32)
    m[:, 0, :] = np.where(kl <= q, 0.0, -30000.0)
    m[:, 1, :] = np.where(kl > q, 0.0, -30000.0)
    m[:, 2, :] = np.where(16 * (NCS - 128 + kl) + 31 <= P0 + q, 0.0, -30000.0)
    c["maskS"] = m.astype(ml_dtypes.bfloat16)
    i_ = np.arange(NCS)[:, None] * 16
    jb = np.arange(NSB)[None, :] * 64
    ov = (np.maximum(np.minimum(i_ + 32, jb + 64) - np.maximum(i_, jb), 0) / 16.0).astype(np.float32)
    c["movS"] = np.ascontiguousarray(ov.reshape(NCS // 128, 128, NSB).transpose(1, 0, 2)).astype(ml_dtypes.bfloat16)
    tq = (P0 + np.arange(TS))[:, None]
    jb = np.arange(NSB)[None, :]
    cur = tq // 64
    forced = (jb == 0) | (jb == cur) | (jb == cur - 1)
    valid = jb * 64 <= tq
    c["fmS"] = np.where(forced, 1e30, -1e30).astype(np.float32)
    c["vmS"] = np.where((~valid) & (~forced), -1e30, 1e30).astype(np.float32)
    c["iota_c"] = np.arange(128, dtype=np.float32).reshape(128, 1)
    return c


def nsa_consts(T, HN):
    c = {}
    hh = np.arange(1, HN + 1, dtype=np.float32)
    slopes = np.exp2(np.float32(-8.0) * hh / np.float32(HN)).astype(np.float32)
    c["kaugP"] = np.ascontiguousarray(_kaug(np.arange(T)))
    c["kaugC"] = np.ascontiguousarray(_kaug(16 * np.arange(128) + 31))
    c["qaugP"] = np.ascontiguousarray(_qaug(slopes, np.arange(T)))
    NQT = T // 512
    kl = np.arange(128)[:, None]
    ql = np.arange(512)[None, :]
    m = np.zeros((128, 8 + NQT, 512), np.float32)
    for d in range(4):
        m[:, d, :] = np.where(128 * d + kl <= ql, 0.0, -30000.0)
    for i, d in enumerate(range(-4, 0)):
        m[:, 4 + i, :] = np.where(ql - (128 * d + kl) < 512, 0.0, -30000.0)
    for qt in range(NQT):
        m[:, 8 + qt, :] = np.where((16 * kl + 31 <= 512 * qt + ql) & (kl < T // 16 - 1), 0.0, -30000.0)
    c["maskP"] = m.astype(ml_dtypes.bfloat16)
    j = np.arange(128)[:, None, None]
    mm_ = np.arange(64)[None, :, None]
    key = np.arange(128)[None, None, :]
    c["eone"] = (j == 2 * mm_ + key // 64).astype(np.float32).astype(ml_dtypes.bfloat16)
    ncp, nsb = T // 16 - 1, T // 64
    i_ = np.arange(128)[:, None] * 16
    jb = np.arange(nsb)[None, :] * 64
    ov = np.maximum(np.minimum(i_ + 32, jb + 64) - np.maximum(i_, jb), 0) / 16.0
    ov[ncp:] = 0
    c["movP"] = ov.astype(np.float32).astype(ml_dtypes.bfloat16)
    t = np.arange(T)[:, None]
    jb = np.arange(nsb)[None, :]
    cur = t // 64
    forced = (jb == 0) | (jb == cur) | (jb == cur - 1)
    valid = jb * 64 <= t
    fm = np.where(forced, 1e30, -1e30).astype(np.float32)
    vm = np.where((~valid) & (~forced), -1e30, 1e30).astype(np.float32)
    lay = lambda a: np.ascontiguousarray(a.reshape(T // 128, 128, nsb).transpose(1, 0, 2))
    c["fmP"], c["vmP"] = lay(fm), lay(vm)
    return c


def host_consts():
    i = np.arange(128)
    same = (i[:, None] // 64) == (i[None, :] // 64)
    c = {}
    c["ident"] = np.eye(128, dtype=np.float32)
    c["tri"] = (same & (i[:, None] <= i[None, :])).astype(np.float32)
    c["stri"] = (same & (i[:, None] < i[None, :])).astype(np.float32)
    c["last"] = (same & ((i[:, None] % 64) == 63)).astype(np.float32)
    lastS = np.zeros((128, 128), np.float32)
    lastS[7, :] = 1.0
    c["lastS"] = lastS
    c["ones"] = np.ones((128, 128), np.float32)
    return c


class _Stop(Exception):
    pass


class V:
    def __init__(self, ap, name):
        self.ap, self.name = ap, name

    def __getitem__(self, k):
        return self.ap[k]


class Carver:
    fallback = None
    nfb = 0

    def __init__(self, regions):
        self.regions = regions
        self.reset()

    def reset(self):
        self.off = [0] * len(self.regions)

    def alloc(self, name, shape, dt=F32):
        n = 1
        for d_ in shape[1:]:
            n *= d_
        n32 = (n + 1) // 2 if dt == BF16 else n
        for i, r in enumerate(self.regions):
            if self.off[i] + n32 <= r.shape[1]:
                v = r[0:shape[0], self.off[i]:self.off[i] + n32]
                self.off[i] += n32
                if dt != F32:
                    v = v.bitcast(dt)[:, 0:n]

                if len(shape) == 3:
                    v = v.rearrange("p (a b) -> p a b", b=shape[2])
                elif len(shape) == 4:
                    v = v.rearrange("p (a b c) -> p a b c", b=shape[2], c=shape[3])
                return V(v, name)
        if Carver.fallback is not None:
            Carver.nfb += 1
            return V(Carver.fallback("%s_fb%d" % (name, Carver.nfb), shape, dt)[:], name)
        raise RuntimeError("carver out of space for %s %s" % (name, shape))


def build_program(cfg):
    nc, P, st = bass.Bass("TRN2", target_bir_lowering=False), Prog(), ExitStack()
    try:
        _build(cfg, nc, P, st)
    except _Stop:
        pass
    P.emit(nc, st)
    st.close()
    return nc


def _build(cfg, nc, P, st):
    D, H, T = cfg["D"], cfg["H"], cfg["T"]
    stop = cfg.get("stop", 99)
    TS = 8
    TT = T + TS
    KC = D // 128
    QK = H * 128
    CONV = 3 * QK
    DN_IN = CONV + QK + 2 * H
    TB = cfg.get("TB", 256)
    assert T % 128 == 0 and T % TB == 0
    NT = T // 128

    def din(name, shape, dt=F32):
        return nc.dram_tensor(name, list(shape), dt, kind="ExternalInput").ap()

    def dout(name, shape, dt=F32):
        return nc.dram_tensor(name, list(shape), dt, kind="ExternalOutput").ap()

    def dscr(name, shape, dt):
        return nc.dram_tensor(name, list(shape), dt, kind="Internal").ap()

    def sb(name, shape, dt=F32):
        return st.enter_context(nc.sbuf_tensor(name, list(shape), dt))

    def ps(name, shape, dt=F32):
        return st.enter_context(nc.psum_tensor(name, list(shape), dt))

    x_d = din("x", [TT, D])
    wnb_d = din("norm_dn_b", [128, D])
    win_d = din("w_in_dn", [D, DN_IN])
    convw_d = din("conv_w", [128, 3 * H, 4])
    conv0_d = din("conv0_in", [128, 3 * H, 3])
    s0_d = din("s0", [H, 128, 128])
    alog_d = din("a_log_b", [128, H])
    dtb_d = din("dt_bias_b", [128, H])
    onw_d = din("out_norm_c", [128, 1])
    wout_d = din("w_out_dn", [QK, D])
    cst = {k: din("c_" + k, [128, 128]) for k in ("ident", "tri", "stri", "last", "lastS", "ones")}

    o_sdP = dout("o_sdP", [H, 128, 128])
    o_sdS = dout("o_sdS", [H, 128, 128])
    o_scP = dout("o_scP", [128, 3 * H, 3])
    o_scS = dout("o_scS", [128, 3 * H, 3])
    o_x1 = dout("o_x1", [TT, D])

    o_dbg = dout("o_dbg", [10, 128, T]) if cfg.get("dbg") else None
    G = cfg.get("G", 4)
    HN = G * 8
    DKN, DVN = 192, 128
    NQ, KW_, VW_ = HN * DKN, G * DKN, G * DVN
    c_kc = NQ
    c_vc = c_kc + KW_
    c_ks = c_vc + VW_
    c_vs = c_ks + KW_
    c_kw = c_vs + VW_
    c_vw = c_kw + KW_
    c_gl = c_vw + VW_
    c_z = c_gl + 3 * HN
    NSA_IN = c_z + HN * DVN
    WL = min(512, T)
    WLC = 512
    wnb2_d = din("norm_nsa_b", [128, D])
    win2_d = din("w_in_nsa", [D, NSA_IN])
    knb_d = din("knorm_b", [128, 2, DKN])
    cwk_d = din("cwin_k", [WLC, KW_])
    cwv_d = din("cwin_v", [WLC, VW_])
    o_pck = dout("o_pck", [T, KW_]); o_pcv = dout("o_pcv", [T, VW_])
    o_psk = dout("o_psk", [T, KW_]); o_psv = dout("o_psv", [T, VW_])
    o_pwk = dout("o_pwk", [WL, KW_]); o_pwv = dout("o_pwv", [WL, VW_])
    o_sck = dout("o_sck", [TS, KW_]); o_scv = dout("o_scv", [TS, VW_])
    o_ssk = dout("o_ssk", [TS, KW_]); o_ssv = dout("o_ssv", [TS, VW_])
    o_swk = dout("o_swk", [WLC, KW_]); o_swv = dout("o_swv", [WLC, VW_])
    h1T_d = dscr("h1T_scr", [(TT + 127) // 128, 128, KC * 128], BF16)
    kv_s = {nm: dscr(nm + "_scr", [TT, w_], BF16) for nm, w_ in (("kc", KW_), ("vc", VW_), ("ks", KW_), ("vs", VW_), ("kw", KW_), ("vw", VW_))}
    gate_s = dscr("gate_scr", [TT, 3 * HN], F32)
    qnw_d = din("qnorm_c", [128, 2])
    NQT = T // 512
    NCP = T // 16 - 1
    w1k_d = din("w1k_t", [DKN, 32, 256])
    w1v_d = din("w1v_t", [DVN, 32, 256])
    w2k_d = din("w2k", [256, DKN])
    w2v_d = din("w2v", [256, DVN])
    pek_d = din("pek_t", [DKN, 32])
    pev_d = din("pev_t", [DVN, 32])
    kncb_d = din("knc_b", [128, DKN])
    kaugP_d = din("kaugP", [7, T], BF16)
    kaugC_d = din("kaugC", [7, 128], BF16)
    qaugP_d = din("qaugP", [7, HN, T], BF16)
    maskP_d = din("maskP", [128, 8 + NQT, 512], BF16)
    eone_d = din("eone", [128, 64, 128], BF16)
    movP_d = din("movP", [128, 32], BF16)
    fmP_d = din("fmP", [128, T // 128, 32])
    vmP_d = din("vmP", [128, T // 128, 32])
    wout2_d = din("w_out_nsa", [HN * DVN, D])
    o_y = dout("o_y", [TT, D])
    fmK = {nm: dscr(nm + "T_scr", [G, DKN, T], BF16) for nm in ("kc", "ks", "kw")}
    fmV = dscr("vcT_scr", [G, DVN, T], BF16)
    ckT_s = dscr("ckT_scr", [G, DKN, 128], BF16)
    cv_s = dscr("cv_scr", [G, 128, DVN], BF16)
    NPG = cfg.get("NPG", 128)
    NPOOL = cfg.get("NPOOL", 1280)
    P0 = NPG * 128
    LS = P0 + 16
    NCS = NPG * 8
    NBP = 2 * NPG
    NSB = NBP + 1
    NCK = (NBP + 127) // 128
    pt_d = din("page_tab", [1, NPG], mybir.dt.int32)
    iota_d = din("iota_c", [128, 1])
    pool_d = {"kc": din("pool_ck", [NPOOL * 128, KW_]), "vc": din("pool_cv", [NPOOL * 128, VW_]),
              "ks": din("pool_sk", [NPOOL * 128, KW_]), "vs": din("pool_sv", [NPOOL * 128, VW_])}
    kaugS_d = din("kaugS", [7, LS], BF16)
    kaugCS_d = din("kaugCS", [7, NCS], BF16)
    qaugS_d = din("qaugS", [7, HN, TS], BF16)
    maskS_d = din("maskS", [128, 3, 64], BF16)
    movS_d = din("movS", [128, NCS // 128, NSB], BF16)
    fmS_d = din("fmS", [TS, NSB])
    vmS_d = din("vmS", [TS, NSB])
    sK = {nm: dscr("s" + nm + "T_scr", [G, DKN, LS], BF16) for nm in ("kc", "ks")}
    sKw = dscr("skwT_scr", [G, DKN, WLC + TS], BF16)
    sVcT = dscr("svcT_scr", [G, DVN, LS], BF16)
    sVs = dscr("svs_scr", [LS, VW_], BF16)
    sVw = dscr("svw_scr", [WLC + TS, VW_], BF16)
    ckTS_s = dscr("ckTS_scr", [G, DKN, NCS], BF16)
    cvS_s = dscr("cvS_scr", [G, NCS, DVN], BF16)
    qT_s = dscr("qT_scr", [HN, DKN, TT], BF16)
    zs_s = dscr("zs_scr", [HN, DVN, TT], BF16)
    gT_s = dscr("gT_scr", [3 * HN, TT], F32)
    NTA_ = (TT + 127) // 128
    hT_d = dscr("hT_scr", [NTA_, 128, KC * 128], BF16)
    hblk = lambda d_, it_: d_[it_].rearrange("p (k t) -> p k t", t=128)
    og_d = dscr("og_scr", [NTA_, 128, H * 128], BF16)

    ident_f = sb("ident_f", [128, 128])
    ident_b = sb("ident_b", [128, 128], BF16)
    tri_f = sb("tri_f", [128, 128])
    stri_f = sb("stri_f", [128, 128])
    last_f = sb("last_f", [128, 128])
    lastS_f = sb("lastS_f", [128, 128])
    ones_f = sb("ones_f", [128, 128])
    ones_b = sb("ones_b", [128, 128], BF16)
    convw = sb("convw", [128, 3 * H, 4])
    conv0 = sb("conv0", [128, 3 * H, 3])
    convoP = sb("convoP", [128, 3 * H, 3])
    convoS = sb("convoS", [128, 3 * H, 3])
    alog = sb("alog", [128, H])
    dtb = sb("dtb", [128, H])
    onw = sb("onw", [128, 1])

    for t_, nm in ((ident_f, "ident"), (tri_f, "tri"), (stri_f, "stri"), (last_f, "last"), (lastS_f, "lastS"), (ones_f, "ones")):
        P.dma("sp", (lambda e, a=t_, b=cst[nm]: e.dma_start(out=a[:], in_=b[:, :])), writes=[t_.name])
    P.dma("pool", lambda e: e.dma_start(out=ident_b[:], in_=cst["ident"][:, :]), writes=["ident_b"])
    P.dma("pool", lambda e: e.dma_start(out=ones_b[:], in_=cst["ones"][:, :]), writes=["ones_b"])
    tri_b = sb("tri_b", [128, 128], BF16)
    last_b = sb("last_b", [128, 128], BF16)
    lastS_b = sb("lastS_b", [128, 128], BF16)
    P.dma("pool", lambda e: e.dma_start(out=tri_b[:], in_=cst["tri"][:, :]), writes=["tri_b"])
    P.dma("pool", lambda e: e.dma_start(out=last_b[:], in_=cst["last"][:, :]), writes=["last_b"])
    P.dma("pool", lambda e: e.dma_start(out=lastS_b[:], in_=cst["lastS"][:, :]), writes=["lastS_b"])
    for t_, d_ in ((alog, alog_d), (dtb, dtb_d), (onw, onw_d)):
        P.dma("sp", (lambda e, a=t_, b=d_: e.dma_start(out=a[:], in_=b[:, :])), writes=[t_.name])
    for t_, d_ in ((convw, convw_d), (conv0, conv0_d)):
        P.dma("sp", (lambda e, a=t_, b=d_: e.dma_start(out=a[:], in_=b[:, :, :])), writes=[t_.name])

    psG = [ps("psG%d" % i, [128, 512]) for i in range(2)]
    psT = ps("psT", [128, 1024], BF16)
    psM = [ps("psM%d" % i, [128, 512]) for i in range(4)]
    psS = ps("psS", [128, 512])

    AW = max(2 * D + D // 2 + KC * 64, 3 * (T + 3) + 3 * T)
    arena = sb("arena", [128, AW])
    wnb = arena[:, 0:D]
    xt = [arena[:, D:2 * D]] * 2
    xb = arena[:, 2 * D:2 * D + D // 2].bitcast(BF16)
    junk = xb
    hTt = arena[:, 2 * D + D // 2:2 * D + D // 2 + KC * 64].bitcast(BF16).rearrange("p (k t) -> p k t", t=128)
    ntile_all = (TT + 127) // 128
    ssq = sb("ssq", [128, ntile_all])

    def rms_phase(src_d, w_d, dst_d, dkey):
        P.dma("sp", (lambda e: e.dma_start(out=wnb, in_=w_d[:, :])), writes=["wnb"])
        P.op("dve", lambda e: e.memset(ssq[:], 0.0), writes=["ssq"])
        for it in range(ntile_all):
            r0 = it * 128
            nr = min(128, TT - r0)
            xx = xt[it % 2]
            P.dma("sp", (lambda e, xx=xx, r0=r0, nr=nr: e.dma_start(out=xx[:nr], in_=src_d[r0:r0 + nr, :])), reads=["x1src"], writes=["xt"])
            sl = ssq[:nr, it:it + 1]
            P.op("act", (lambda e, xx=xx, nr=nr, sl=sl: e.activation(out=junk[:nr], in_=xx[:nr], func=AF.Square, accum_out=sl)),
                 reads=["xt", "ssq"], writes=["xb", "ssq"])
            P.op("act", (lambda e, sl=sl: e.activation(out=sl, in_=sl, func=AF.Sqrt, scale=1.0 / D, bias=EPS)), reads=["ssq"], writes=["ssq"])
            P.op("dve", (lambda e, sl=sl: e.reciprocal(out=sl, in_=sl)), reads=["ssq"], writes=["ssq"])
            P.op("dve", (lambda e, xx=xx, nr=nr, sl=sl: e.scalar_tensor_tensor(out=xb[:nr], in0=xx[:nr], scalar=sl, in1=wnb[:nr], op0=ALU.mult, op1=ALU.mult)),
                 reads=["xt", "ssq", "wnb"], writes=["xb"])
            for k0 in range(0, KC, 8):
                kn = min(8, KC - k0)
                for k in range(kn):
                    P.op("pe", (lambda e, k=k, k0=k0, nr=nr: e.transpose(out=psT[:, k * 128:k * 128 + nr], in_=xb[:nr, (k0 + k) * 128:(k0 + k + 1) * 128], identity=ident_b[:nr, :nr])),
                         reads=["xb", "ident_b"], writes=["psT"])
                P.op("act", (lambda e, k0=k0, kn=kn, nr=nr: e.activation(out=hTt[:, k0:k0 + kn, :nr], in_=psT[:, 0:kn * 128].rearrange("p (k t) -> p k t", t=128)[:, :, :nr], func=AF.Copy)),
                     reads=["psT"], writes=["hTt"])
            P.dma("sp", (lambda e, r0=r0, nr=nr: e.dma_start(out=hblk(dst_d, r0 // 128)[:, :, :nr], in_=hTt[:, :, :nr])), reads=["hTt"], writes=[dkey])

    rms_phase(x_d, wnb_d, hT_d, "hT_d")

    if stop <= 1:
        raise _Stop()
    hTb = [sb("hTb%d" % i, [128, KC, TB], BF16) for i in range(2)]
    wba = sb("wba", [128, KC, 2 * H], BF16)
    win_r = win_d.rearrange("(kc p) n -> p kc n", p=128)
    P.dma("pool", lambda e: e.dma_start(out=wba[:], in_=win_r[:, :, CONV + QK:CONV + QK + 2 * H]), writes=["wba"])
    NTA = ntile_all
    beta_tm = sb("beta_tm", [128, NTA, H])
    g_tm = sb("g_tm", [128, NTA, H])
    gc_tm = sb("gc_tm", [128, NTA, H])
    ngc_tm = sb("ngc_tm", [128, NTA, H])
    bg_tm = sb("bg_tm", [128, NTA, H])
    ed_tm = sb("ed_tm", [128, NTA, H])
    tmpH = sb("tmpH", [128, H])
    tmpR = sb("tmpR", [128, H])
    gk_b = [sb("gk_b%d" % i, [128, NTA, H], BF16) for i in range(3)]
    gk_f = [sb("gk_f%d" % i, [128, NTA, H]) for i in range(3)]
    bk_f = [sb("bk_f%d" % i, [128, NTA, H]) for i in range(2)]
    ck_b = [sb("ck_b%d" % i, [128, H], BF16) for i in range(3)]
    for t_ in gk_b + gk_f + bk_f:
        P.op("pool", (lambda e, t_=t_: e.memset(t_[:], 0.0)), writes=[t_.name])

    def split3(src_ap, outs_b, outs_f, nr, key_src, keys_b, keys_f, n=3):
        cur = src_ap
        curk = key_src
        for k in range(n):
            ob = outs_b[k]
            P.op("dve", (lambda e, ob=ob, cur=cur: e.tensor_copy(out=ob, in_=cur)), reads=[curk], writes=[keys_b[k]])
            if outs_f is not None:
                of = outs_f[k]
                P.op("dve", (lambda e, ob=ob, of=of: e.tensor_copy(out=of, in_=ob)), reads=[keys_b[k]], writes=[keys_f[k]])
            if k < n - 1:
                P.op("dve", (lambda e, ob=ob, cur=cur: e.tensor_tensor(out=tmpR[:nr], in0=cur, in1=ob, op=ALU.subtract)), reads=[curk, keys_b[k]], writes=["tmpR"])
                cur = tmpR[:nr]
                curk = "tmpR"

    nea = sb("nea", [128, H])
    for t_ in (beta_tm, g_tm, gc_tm, ngc_tm, bg_tm, ed_tm):
        P.op("pool", (lambda e, t_=t_: e.memset(t_[:], 0.0)), writes=[t_.name])
    P.op("act", lambda e: e.activation(out=nea[:], in_=alog[:], func=AF.Exp), reads=["alog"], writes=["nea"])
    P.op("dve", lambda e: e.tensor_scalar(out=nea[:], in0=nea[:], scalar1=-1.0, scalar2=None, op0=ALU.mult), reads=["nea"], writes=["nea"])

    blocks = [(b * TB, TB) for b in range(T // TB)] + [(T, TS)]
    bi = [0]
    for (c0, n) in blocks:
        hb = hTb[bi[0] % 2]
        bi[0] += 1
        P.dma("sp", (lambda e, hb=hb, c0=c0, n=n: e.dma_start(out=hb[:, :, :n], in_=hblk(hT_d, c0 // 128)[:, :, :n])), reads=["hT_d"], writes=[hb.name])
        for t0 in range(0, n, 128):
            nr = min(128, n - t0)
            it = (c0 + t0) // 128
            pm = psM[it % 2]
            for k in range(KC):
                P.op("pe", (lambda e, hb=hb, k=k, t0=t0, nr=nr, pm=pm: e.matmul(pm[:nr, 0:2 * H], lhsT=hb[:, k, t0:t0 + nr], rhs=wba[:, k, :], start=(k == 0), stop=(k == KC - 1))),
                     reads=[hb.name, "wba"], writes=[pm.name])
            P.op("act", (lambda e, pm=pm, nr=nr, it=it: e.activation(out=beta_tm[:nr, it, :], in_=pm[:nr, 0:H], func=AF.Sigmoid)), reads=[pm.name], writes=["beta_tm"])
            P.op("dve", (lambda e, pm=pm, nr=nr: e.tensor_tensor(out=tmpH[:nr], in0=pm[:nr, H:2 * H], in1=dtb[:nr], op=ALU.add)), reads=[pm.name, "dtb"], writes=["tmpH"])
            P.op("act", (lambda e, nr=nr: e.activation(out=tmpH[:nr], in_=tmpH[:nr], func=AF.Exp)), reads=["tmpH"], writes=["tmpH"])
            P.op("act", (lambda e, nr=nr: e.activation(out=tmpH[:nr], in_=tmpH[:nr], func=AF.Ln, bias=1.0)), reads=["tmpH"], writes=["tmpH"])
            P.op("dve", (lambda e, nr=nr, it=it: e.tensor_tensor(out=g_tm[:nr, it, :], in0=tmpH[:nr], in1=nea[:nr], op=ALU.mult)), reads=["tmpH", "nea"], writes=["g_tm"])
            pc = psM[2 + it % 2]
            split3(g_tm[:nr, it, :], [t_[:nr, it, :] for t_ in gk_b], [t_[:nr, it, :] for t_ in gk_f], nr, "g_tm",
                   [t_.name for t_ in gk_b], [t_.name for t_ in gk_f])
            split3(beta_tm[:nr, it, :], [ck_b[0][:nr, :], ck_b[1][:nr, :]], [t_[:nr, it, :] for t_ in bk_f], nr, "beta_tm",
                   [ck_b[0].name, ck_b[1].name], [t_.name for t_ in bk_f], n=2)
            for k3 in range(3):
                P.op("pe", (lambda e, nr=nr, it=it, pc=pc, k3=k3: e.matmul(pc[:nr, 0:H], lhsT=tri_b[:nr, :nr], rhs=gk_b[k3][:nr, it, :], start=(k3 == 0), stop=(k3 == 2))),
                     reads=["tri_b", gk_b[k3].name], writes=[pc.name])
            P.op("act", (lambda e, nr=nr, it=it, pc=pc: e.activation(out=gc_tm[:nr, it, :], in_=pc[:nr, 0:H], func=AF.Copy)), reads=[pc.name], writes=["gc_tm"])
            P.op("dve", (lambda e, nr=nr, it=it, pc=pc: e.tensor_scalar(out=ngc_tm[:nr, it, :], in0=pc[:nr, 0:H], scalar1=-1.0, scalar2=None, op0=ALU.mult)), reads=[pc.name], writes=["ngc_tm"])
            P.op("act", (lambda e, nr=nr, it=it, pc=pc: e.activation(out=bg_tm[:nr, it, :], in_=pc[:nr, 0:H], func=AF.Exp)), reads=[pc.name], writes=["bg_tm"])
            P.op("dve", (lambda e, nr=nr, it=it: e.tensor_tensor(out=bg_tm[:nr, it, :], in0=bg_tm[:nr, it, :], in1=beta_tm[:nr, it, :], op=ALU.mult)), reads=["bg_tm", "beta_tm"], writes=["bg_tm"])
            lastm = last_b if nr == 128 else lastS_b
            split3(gc_tm[:nr, it, :], [t_[:nr, :] for t_ in ck_b], None, nr, "gc_tm", [t_.name for t_ in ck_b], None)
            for k3 in range(3):
                P.op("pe", (lambda e, nr=nr, it=it, pc=pc, lastm=lastm, k3=k3: e.matmul(pc[:nr, H:2 * H], lhsT=lastm[:nr, :nr], rhs=ck_b[k3][:nr, :], start=(k3 == 0), stop=(k3 == 2))),
                     reads=[lastm.name, ck_b[k3].name], writes=[pc.name])
            P.op("dve", (lambda e, nr=nr, it=it, pc=pc: e.tensor_tensor(out=ed_tm[:nr, it, :], in0=pc[:nr, H:2 * H], in1=gc_tm[:nr, it, :], op=ALU.subtract)), reads=[pc.name, "gc_tm"], writes=["ed_tm"])
            P.op("act", (lambda e, nr=nr, it=it: e.activation(out=ed_tm[:nr, it, :], in_=ed_tm[:nr, it, :], func=AF.Exp)), reads=["ed_tm"], writes=["ed_tm"])

    if stop <= 2:
        raise _Stop()
    wh = sb("wh", [128, KC, 4, 128], BF16)
    P.barrier()
    raw = [arena[:, j * (T + 3):(j + 1) * (T + 3)] for j in range(3)]
    rawn = ["raw0", "raw1", "raw2"]
    o3 = 3 * (T + 3)
    zT = arena[:, o3:o3 + T]
    cacc = arena[:, o3 + T:o3 + 2 * T]
    QT = sb("QT", [128, T], BF16)
    KT = sb("KT", [128, T], BF16)
    VT = sb("VT", [128, T], BF16)
    sqb = sb("sqb", [128, T], BF16)
    Gbc = sb("Gbc", [128, T])
    Ebc = sb("Ebc", [128, T])
    Bbc = sb("Bbc", [128, T])
    QgT = sb("QgT", [128, T], BF16)
    KbT = sb("KbT", [128, T], BF16)
    oT = arena[:, o3 + 2 * T:o3 + 3 * T]
    ogT = sb("ogT", [128, T], BF16)
    rhsk = [sb("rhsk%d" % i, [128, 128], BF16) for i in range(5)]
    S32 = sb("S32", [128, 128])
    Sbf = sb("Sbf", [128, 128], BF16)
    dbl = lambda nm, shape, dt=F32: [sb("%s%d" % (nm, i), shape, dt) for i in range(2)]
    decT = dbl("decT", [128, 128])
    decTs = dbl("decTs", [128, 128])
    Am = dbl("Am", [128, 128], BF16)
    Atm = dbl("Atm", [128, 128], BF16)
    Pm = dbl("Pm", [128, 128], BF16)
    Ptm = dbl("Ptm", [128, 128], BF16)
    X32 = dbl("X32", [128, 128])
    Xb = dbl("Xb", [128, 128], BF16)
    attnT = dbl("attnT", [128, 128], BF16)
    Vb = dbl("Vb", [128, 128], BF16)
    Kbg = dbl("Kbg", [128, 128], BF16)
    Kd = dbl("Kd", [128, 128], BF16)
    u32 = dbl("u32", [128, 128])
    WT = dbl("WT", [128, 128], BF16)
    vnew = sb("vnew", [128, 128], BF16)
    egl = sb("egl", [128, 1])

    def R(*names):
        return list(names)

    def head_body(h):
        for j, cbase in enumerate((h * 128, QK + h * 128, 2 * QK + h * 128, CONV + h * 128)):
            P.dma("pool", (lambda e, j=j, cbase=cbase: e.dma_start(out=wh[:, :, j, :], in_=win_r[:, :, cbase:cbase + 128])), writes=["wh"])
        def seq_body(seq, Tn):
            c_base = 0 if seq == "P" else T
            it_base = 0 if seq == "P" else NT
            for j in range(3):
                if seq == "P":
                    P.op("pool", (lambda e, j=j: e.memset(raw[j][:, 0:3], 0.0)), writes=[rawn[j]])
                else:
                    P.op("pool", (lambda e, j=j: e.tensor_copy(out=raw[j][:, 0:3], in_=conv0[:, j * H + h, :])), reads=["conv0"], writes=[rawn[j]])
            for b0 in range(0, Tn, TB):
                n = min(TB, Tn - b0)
                hb = hTb[bi[0] % 2]
                bi[0] += 1
                P.dma("sp", (lambda e, hb=hb, c=c_base + b0, n=n: e.dma_start(out=hb[:, :, :n], in_=hblk(hT_d, c // 128)[:, :, :n])), reads=["hT_d"], writes=[hb.name])
                for j in range(4):
                    pg = psG[j % 2]
                    for k in range(KC):
                        P.op("pe", (lambda e, hb=hb, k=k, j=j, n=n, pg=pg: e.matmul(pg[:, :n], lhsT=wh[:, k, j, :], rhs=hb[:, k, :n], start=(k == 0), stop=(k == KC - 1))),
                             reads=[hb.name, "wh"], writes=[pg.name])
                    dst = raw[j][:, 3 + b0:3 + b0 + n] if j < 3 else zT[:, b0:b0 + n]
                    dname = rawn[j] if j < 3 else "zT"
                    P.op("act", (lambda e, pg=pg, n=n, dst=dst: e.activation(out=dst, in_=pg[:, :n], func=AF.Copy)), reads=[pg.name], writes=[dname])
            if stop <= 3:
                raise _Stop()
            co = convoP if seq == "P" else convoS
            for j in range(3):
                P.op("pool", (lambda e, j=j, co=co, Tn=Tn: e.tensor_copy(out=co[:, j * H + h, :], in_=raw[j][:, Tn:Tn + 3])), reads=[rawn[j]], writes=[co.name])
            for j, dstT in enumerate((QT, KT, VT)):
                ch = j * H + h
                P.op("dve", (lambda e, j=j, ch=ch, Tn=Tn: e.tensor_scalar(out=cacc[:, :Tn], in0=raw[j][:, 0:Tn], scalar1=convw[:, ch, 0:1], scalar2=None, op0=ALU.mult)),
                     reads=[rawn[j], "convw"], writes=["cacc"])
                for jj in range(1, 4):
                    P.op("dve", (lambda e, j=j, ch=ch, jj=jj, Tn=Tn: e.scalar_tensor_tensor(out=cacc[:, :Tn], in0=raw[j][:, jj:jj + Tn], scalar=convw[:, ch, jj:jj + 1], in1=cacc[:, :Tn], op0=ALU.mult, op1=ALU.add)),
                         reads=[rawn[j], "convw", "cacc"], writes=["cacc"])
                if j == 2:
                    P.op("act", (lambda e, Tn=Tn: e.activation(out=VT[:, :Tn], in_=cacc[:, :Tn], func=AF.Silu)), reads=["cacc"], writes=["VT"])
                else:
                    P.op("act", (lambda e, Tn=Tn: e.activation(out=cacc[:, :Tn], in_=cacc[:, :Tn], func=AF.Silu)), reads=["cacc"], writes=["cacc"])
                    P.op("act", (lambda e, Tn=Tn: e.activation(out=sqb[:, :Tn], in_=cacc[:, :Tn], func=AF.Square)), reads=["cacc"], writes=["sqb"])
                    for c0 in range(0, Tn, 512):
                        n = min(512, Tn - c0)
                        pm = psM[(c0 // 512) % 2]
                        P.op("pe", (lambda e, pm=pm, c0=c0, n=n: e.matmul(pm[:, :n], lhsT=ones_b[:, :], rhs=sqb[:, c0:c0 + n], start=True, stop=True)),
                             reads=["ones_b", "sqb"], writes=[pm.name])
                        P.op("act", (lambda e, pm=pm, c0=c0, n=n: e.activation(out=oT[:, c0:c0 + n], in_=pm[:, :n], func=AF.Sqrt, bias=EPS)), reads=[pm.name], writes=["oT"])
                    P.op("dve", (lambda e, Tn=Tn: e.reciprocal(out=oT[:, :Tn], in_=oT[:, :Tn])), reads=["oT"], writes=["oT"])
                    scl = (128.0 ** -0.5) if j == 0 else 1.0
                    P.op("dve", (lambda e, Tn=Tn, dstT=dstT, scl=scl: e.scalar_tensor_tensor(out=dstT[:, :Tn], in0=cacc[:, :Tn], scalar=scl, in1=oT[:, :Tn], op0=ALU.mult, op1=ALU.mult)),
                         reads=["cacc", "oT"], writes=[dstT.name])
            if stop <= 4:
                raise _Stop()
            ntl = (Tn + 127) // 128
            for tl in cfg.get("tls", range(ntl)):
                nr = min(128, Tn - tl * 128)
                it = it_base + tl
                cs = slice(tl * 128, tl * 128 + nr)
                pm = psM[tl % 2]
                for k3 in range(3):
                    rk = rhsk[k3]
                    P.op("dve", (lambda e, nr=nr, it=it, rk=rk, k3=k3: e.tensor_scalar(out=rk[:nr, :nr], in0=tri_f[:nr, :nr], scalar1=gk_f[k3][:nr, it, h:h + 1], scalar2=None, op0=ALU.mult)),
                         reads=["tri_f", gk_f[k3].name], writes=[rk.name])
                    P.op("pe", (lambda e, pm=pm, nr=nr, rk=rk, k3=k3: e.matmul(pm[:, 0:nr], lhsT=ones_b[:nr, :], rhs=rk[:nr, :nr], start=(k3 == 0), stop=(k3 == 2))), reads=["ones_b", rk.name], writes=[pm.name])
                for k2 in range(2):
                    rk = rhsk[3 + k2]
                    P.op("dve", (lambda e, nr=nr, it=it, rk=rk, k2=k2: e.tensor_scalar(out=rk[:nr, :nr], in0=ident_f[:nr, :nr], scalar1=bk_f[k2][:nr, it, h:h + 1], scalar2=None, op0=ALU.mult)),
                         reads=["ident_f", bk_f[k2].name], writes=[rk.name])
                    P.op("pe", (lambda e, pm=pm, nr=nr, rk=rk, k2=k2: e.matmul(pm[:, 128:128 + nr], lhsT=ones_b[:nr, :], rhs=rk[:nr, :nr], start=(k2 == 0), stop=(k2 == 1))), reads=["ones_b", rk.name], writes=[pm.name])
                if stop <= 4.2:
                    raise _Stop()
                if not cfg.get("skipE"):
                    P.op("act", (lambda e, pm=pm, nr=nr, cs=cs: e.activation(out=Ebc[:, cs], in_=pm[:, 0:nr], func=AF.Exp)), reads=[pm.name], writes=["Ebc"])
                if stop <= 4.21:
                    raise _Stop()
                if not cfg.get("skipG"):
                    P.op("dve", (lambda e, pm=pm, nr=nr, cs=cs: e.tensor_copy(out=Gbc[:, cs], in_=pm[:, 0:nr])), reads=[pm.name], writes=["Gbc"])
                if stop <= 4.22:
                    raise _Stop()
                if not cfg.get("skipB"):
                    P.op("act", (lambda e, pm=pm, nr=nr, cs=cs: e.activation(out=Bbc[:, cs], in_=pm[:, 128:128 + nr], func=AF.Copy)), reads=[pm.name], writes=["Bbc"])
                if stop <= 4.25:
                    raise _Stop()
            if stop <= 4.3:
                raise _Stop()
            P.op("dve", (lambda e, Tn=Tn: e.tensor_tensor(out=QgT[:, :Tn], in0=QT[:, :Tn], in1=Ebc[:, :Tn], op=ALU.mult)), reads=["QT", "Ebc"], writes=["QgT"])
            P.op("pool", (lambda e, Tn=Tn: e.tensor_tensor(out=KbT[:, :Tn], in0=KT[:, :Tn], in1=Bbc[:, :Tn], op=ALU.mult)), reads=["KT", "Bbc"], writes=["KbT"])
            if stop <= 4.6:
                raise _Stop()
            if seq == "P":
                P.op("dve", lambda e: e.memset(S32[:], 0.0), writes=["S32"])
            else:
                P.dma("sp", (lambda e: e.dma_start(out=S32[:], in_=s0_d[h, :, :])), writes=["S32"])
            P.op("act", lambda e: e.activation(out=Sbf[:], in_=S32[:], func=AF.Copy), reads=["S32"], writes=["Sbf"])

            def precompute(tl):
                nr = min(128, Tn - tl * 128)
                it = it_base + tl
                q = tl % 2
                cs = slice(tl * 128, tl * 128 + nr)
                nm = lambda t_: t_[q].name
                P.op("dve", (lambda e: e.tensor_scalar(out=decT[q][:nr, :nr], in0=Gbc[:nr, cs], scalar1=ngc_tm[:nr, it, h:h + 1], scalar2=0.0, op0=ALU.add, op1=ALU.min)),
                     reads=["Gbc", "ngc_tm"], writes=[nm(decT)])
                P.op("act", (lambda e: e.activation(out=decT[q][:nr, :nr], in_=decT[q][:nr, :nr], func=AF.Exp)), reads=[nm(decT)], writes=[nm(decT)])
                P.op("pool", (lambda e: e.tensor_tensor(out=decTs[q][:nr, :nr], in0=decT[q][:nr, :nr], in1=stri_f[:nr, :nr], op=ALU.mult)), reads=[nm(decT), "stri_f"], writes=[nm(decTs)])
                P.op("pool", (lambda e: e.tensor_tensor(out=decT[q][:nr, :nr], in0=decT[q][:nr, :nr], in1=tri_f[:nr, :nr], op=ALU.mult)), reads=[nm(decT), "tri_f"], writes=[nm(decT)])
                pa = psM[2]
                P.op("pe", (lambda e: e.matmul(pa[:nr, 0:nr], lhsT=KT[:, cs], rhs=KbT[:, cs], start=True, stop=True)), reads=["KT", "KbT"], writes=[pa.name])
                P.op("pe", (lambda e: e.matmul(pa[:nr, 128:128 + nr], lhsT=KT[:, cs], rhs=QT[:, cs], start=True, stop=True)), reads=["KT", "QT"], writes=[pa.name])
                P.op("dve", (lambda e: e.scalar_tensor_tensor(out=Am[q][:nr, :nr], in0=pa[:nr, 0:nr], scalar=-1.0, in1=decTs[q][:nr, :nr], op0=ALU.mult, op1=ALU.mult)),
                     reads=[pa.name, nm(decTs)], writes=[nm(Am)])
                P.op("dve", (lambda e: e.tensor_tensor(out=attnT[q][:nr, :nr], in0=pa[:nr, 128:128 + nr], in1=decT[q][:nr, :nr], op=ALU.mult)),
                     reads=[pa.name, nm(decT)], writes=[nm(attnT)])
                P.op("pe", (lambda e: e.transpose(out=psT[:nr, 0:128], in_=KT[:, cs], identity=ident_b[:, :])), reads=["KT", "ident_b"], writes=["psT"])
                P.op("pe", (lambda e: e.transpose(out=psT[:nr, 128:256], in_=VT[:, cs], identity=ident_b[:, :])), reads=["VT", "ident_b"], writes=["psT"])
                P.op("pe", (lambda e: e.transpose(out=psT[:nr, 256:256 + nr], in_=Am[q][:nr, :nr], identity=ident_b[:nr, :nr])), reads=[nm(Am), "ident_b"], writes=["psT"])
                P.op("act", (lambda e: e.activation(out=Kbg[q][:nr, :], in_=psT[:nr, 0:128], func=AF.Copy, scale=bg_tm[:nr, it, h:h + 1])), reads=["psT", "bg_tm"], writes=[nm(Kbg)])
                P.op("dve", (lambda e: e.tensor_scalar(out=Kd[q][:nr, :], in0=psT[:nr, 0:128], scalar1=ed_tm[:nr, it, h:h + 1], scalar2=None, op0=ALU.mult)), reads=["psT", "ed_tm"], writes=[nm(Kd)])
                P.op("dve", (lambda e: e.tensor_scalar(out=Vb[q][:nr, :], in0=psT[:nr, 128:256], scalar1=beta_tm[:nr, it, h:h + 1], scalar2=None, op0=ALU.mult)), reads=["psT", "beta_tm"], writes=[nm(Vb)])
                P.op("act", (lambda e: e.activation(out=Atm[q][:nr, :nr], in_=psT[:nr, 256:256 + nr], func=AF.Copy)), reads=["psT"], writes=[nm(Atm)])
                P.op("dve", (lambda e: e.tensor_tensor(out=X32[q][:nr, :nr], in0=Am[q][:nr, :nr], in1=ident_f[:nr, :nr], op=ALU.add)), reads=[nm(Am), "ident_f"], writes=[nm(X32)])
                P.op("act", (lambda e: e.activation(out=Xb[q][:nr, :nr], in_=X32[q][:nr, :nr], func=AF.Copy)), reads=[nm(X32)], writes=[nm(Xb)])
                cur, curT = Am[q], Atm[q]
                for kk in range(1, 6):
                    pb = psM[3]
                    nxt, nxtT = (Pm[q], Ptm[q]) if (kk % 2 == 1) else (Am[q], Atm[q])
                    P.op("pe", (lambda e, cur=cur, curT=curT: e.matmul(pb[:nr, 0:nr], lhsT=cur[:nr, :nr], rhs=curT[:nr, :nr], start=True, stop=True)), reads=[cur.name, curT.name], writes=[pb.name])
                    if kk < 5:
                        P.op("pe", (lambda e, cur=cur, curT=curT: e.matmul(pb[:nr, 128:128 + nr], lhsT=curT[:nr, :nr], rhs=cur[:nr, :nr], start=True, stop=True)), reads=[cur.name, curT.name], writes=[pb.name])
                    P.op("act", (lambda e, nxtT=nxtT: e.activation(out=nxtT[:nr, :nr], in_=pb[:nr, 0:nr], func=AF.Copy)), reads=[pb.name], writes=[nxtT.name])
                    if kk < 5:
                        P.op("dve", (lambda e, nxt=nxt: e.tensor_copy(out=nxt[:nr, :nr], in_=pb[:nr, 128:128 + nr])), reads=[pb.name], writes=[nxt.name])
                    P.op("pe", (lambda e, nxtT=nxtT: e.matmul(pb[:nr, 256:256 + nr], lhsT=nxtT[:nr, :nr], rhs=Xb[q][:nr, :nr], start=True, stop=True)), reads=[nxtT.name, nm(Xb)], writes=[pb.name])
                    P.op("dve", (lambda e: e.tensor_tensor(out=X32[q][:nr, :nr], in0=X32[q][:nr, :nr], in1=pb[:nr, 256:256 + nr], op=ALU.add)), reads=[pb.name, nm(X32)], writes=[nm(X32)])
                    P.op("act", (lambda e: e.activation(out=Xb[q][:nr, :nr], in_=X32[q][:nr, :nr], func=AF.Copy)), reads=[nm(X32)], writes=[nm(Xb)])
                    cur, curT = nxt, nxtT
                pc = psM[2]
                P.op("pe", (lambda e: e.matmul(pc[:nr, 256:384], lhsT=Xb[q][:nr, :nr], rhs=Vb[q][:nr, :], start=True, stop=True)), reads=[nm(Xb), nm(Vb)], writes=[pc.name])
                P.op("pe", (lambda e: e.matmul(pc[:, 384:384 + nr], lhsT=Kbg[q][:nr, :], rhs=Xb[q][:nr, :nr], start=True, stop=True)), reads=[nm(Xb), nm(Kbg)], writes=[pc.name])
                P.op("act", (lambda e: e.activation(out=u32[q][:nr, :], in_=pc[:nr, 256:384], func=AF.Copy)), reads=[pc.name], writes=[nm(u32)])
                P.op("dve", (lambda e: e.tensor_copy(out=WT[q][:, :nr], in_=pc[:, 384:384 + nr])), reads=[pc.name], writes=[nm(WT)])

            def chain(tl):
                nr = min(128, Tn - tl * 128)
                q = tl % 2
                nm = lambda t_: t_[q].name
                for p0 in range(0, nr, 64):
                    n = min(64, nr - p0)
                    rs = slice(p0, p0 + n)
                    c0 = tl * 128 + p0
                    P.op("pe", (lambda e: e.matmul(psS[:nr, 0:128], lhsT=WT[q][:, :nr], rhs=Sbf[:, :], start=True, stop=True)), reads=[nm(WT), "Sbf"], writes=["psS"])
                    P.op("dve", (lambda e, rs=rs: e.tensor_tensor(out=vnew[rs, :], in0=u32[q][rs, :], in1=psS[rs, 0:128], op=ALU.subtract)), reads=[nm(u32), "psS"], writes=["vnew"])
                    P.op("pe", (lambda e, c0=c0, n=n: e.matmul(psS[:, 128:128 + n], lhsT=Sbf[:, :], rhs=QgT[:, c0:c0 + n], start=True, stop=False)), reads=["Sbf", "QgT"], writes=["psS"])
                    P.op("pe", (lambda e, rs=rs, p0=p0, n=n: e.matmul(psS[:, 128:128 + n], lhsT=vnew[rs, :], rhs=attnT[q][rs, p0:p0 + n], start=False, stop=True)), reads=["vnew", nm(attnT)], writes=["psS"])
                    P.op("pe", (lambda e, rs=rs: e.matmul(psS[:, 256:384], lhsT=Kd[q][rs, :], rhs=vnew[rs, :], start=True, stop=True)), reads=["vnew", nm(Kd)], writes=["psS"])
                    P.op("act", (lambda e, c0=c0, n=n: e.activation(out=oT[:, c0:c0 + n], in_=psS[:, 128:128 + n], func=AF.Copy)), reads=["psS"], writes=["oT"])
                    lc = c0 + n - 1
                    P.op("dve", (lambda e, lc=lc: e.scalar_tensor_tensor(out=S32[:, :], in0=S32[:, :], scalar=Ebc[:, lc:lc + 1], in1=psS[:, 256:384], op0=ALU.mult, op1=ALU.add)),
                         reads=["S32", "Ebc", "psS"], writes=["S32"])
                    P.op("act", (lambda e: e.activation(out=Sbf[:], in_=S32[:], func=AF.Copy)), reads=["S32"], writes=["Sbf"])

            if stop <= 5:
                raise _Stop()
            precompute(0)
            if stop <= 6:
                raise _Stop()
            for tl in range(ntl):
                if tl + 1 < ntl:
                    P.interleave((lambda tl=tl: precompute(tl + 1)), (lambda tl=tl: chain(tl)))
                else:
                    chain(tl)
            if o_dbg is not None and h == cfg["dbg"] - 1 and seq == "P":
                for i_, (t_, k_) in enumerate(((QT, "QT"), (KT, "KT"), (VT, "VT"), (Gbc, "Gbc"), (Bbc, "Bbc"), (Ebc, "Ebc"), (oT, "oT"), (zT, "zT"), (raw[0][:, 3:3 + T], "raw0"), (QgT, "QgT"))):
                    P.dma("pool", (lambda e, i_=i_, t_=t_: e.dma_start(out=o_dbg[i_, :, :], in_=t_[:, :T])), reads=[k_], writes=["o_dbg%d" % i_])
            so = o_sdP if seq == "P" else o_sdS
            P.dma("sp", (lambda e, so=so: e.dma_start(out=so[h, :, :], in_=S32[:])), reads=["S32"], writes=["o_sd"])
            P.op("act", (lambda e, Tn=Tn: e.activation(out=sqb[:, :Tn], in_=oT[:, :Tn], func=AF.Square)), reads=["oT"], writes=["sqb"])
            for c0 in range(0, Tn, 512):
                n = min(512, Tn - c0)
                pm = psM[(c0 // 512) % 2]
                P.op("pe", (lambda e, pm=pm, c0=c0, n=n: e.matmul(pm[:, :n], lhsT=ones_b[:, :], rhs=sqb[:, c0:c0 + n], start=True, stop=True)), reads=["ones_b", "sqb"], writes=[pm.name])
                P.op("act", (lambda e, pm=pm, c0=c0, n=n: e.activation(out=cacc[:, c0:c0 + n], in_=pm[:, :n], func=AF.Sqrt, scale=1.0 / 128, bias=EPS)), reads=[pm.name], writes=["cacc"])
            P.op("dve", (lambda e, Tn=Tn: e.reciprocal(out=cacc[:, :Tn], in_=cacc[:, :Tn])), reads=["cacc"], writes=["cacc"])
            P.op("dve", (lambda e, Tn=Tn: e.scalar_tensor_tensor(out=oT[:, :Tn], in0=oT[:, :Tn], scalar=onw[:, 0:1], in1=cacc[:, :Tn], op0=ALU.mult, op1=ALU.mult)), reads=["oT", "onw", "cacc"], writes=["oT"])
            P.op("act", (lambda e, Tn=Tn: e.activation(out=zT[:, :Tn], in_=zT[:, :Tn], func=AF.Silu)), reads=["zT"], writes=["zT"])
            P.op("dve", (lambda e, Tn=Tn: e.tensor_tensor(out=ogT[:, :Tn], in0=oT[:, :Tn], in1=zT[:, :Tn], op=ALU.mult)), reads=["oT", "zT"], writes=["ogT"])
            if seq == "P":
                P.dma("sp", (lambda e: e.dma_start(out=og_d[0:NT].rearrange("n p (h t) -> p n h t", t=128)[:, :, h, :], in_=ogT[:, :T].rearrange("p (n t) -> p n t", t=128))), reads=["ogT"], writes=["og_d"])
            else:
                P.dma("sp", (lambda e: e.dma_start(out=og_d[NT].rearrange("p (h t) -> p h t", t=128)[:, h, 0:TS], in_=ogT[:, :TS])), reads=["ogT"], writes=["og_d"])

        for sq_ in (("P", T), ("S", TS)):
            seq_body(*sq_)

    for h_ in range(H):
        head_body(h_)
    P.dma("sp", lambda e: e.dma_start(out=o_scP[:, :, :], in_=convoP[:]), reads=["convoP"], writes=["o_scP"])
    P.dma("sp", lambda e: e.dma_start(out=o_scS[:, :, :], in_=convoS[:]), reads=["convoS"], writes=["o_scS"])

    P.barrier()
    CB = min(512, D)
    NHALF = 2 if D >= 2 * CB else 1
    whf0 = wh[:].rearrange("p a b c -> p (a b c)")
    if H * CB <= KC * 512 and H * 128 <= 2 * T and NHALF * 2 * CB <= T and H * CB <= 2 * (2 * D + D // 2):
        wo2 = [whf0[:, 0:H * CB].rearrange("p (h n) -> p h n", n=CB),
               arena[:, 0:H * CB // 2].bitcast(BF16).rearrange("p (h n) -> p h n", n=CB)]
        ogt = [t_[:].bitcast(BF16)[:, 0:H * 128].rearrange("p (h n) -> p h n", n=128) for t_ in (Gbc, Ebc)]
        x1t = [V(Bbc[:, i * NHALF * CB:(i + 1) * NHALF * CB], "x1t%d" % i) for i in range(2)]
    else:
        wo2 = [sb("wo_t%d" % i, [128, H, CB], BF16)[:] for i in range(2)]
        ogt = [sb("ogt_t%d" % i, [128, H, 128], BF16)[:] for i in range(2)]
        x1t = [sb("x1t%d" % i, [128, NHALF * CB]) for i in range(2)]
    ogn = ["ogt0", "ogt1"]
    obank = [[psG[0], psG[1]], [psM[0], psM[1]]]

    def outproj(w_dram, src_d, skey, dst_d, dkey):
        w_r = w_dram.rearrange("(h p) n -> p h n", p=128)
        ti = 0
        CW = NHALF * CB
        for cb in range(0, D, CW):
            for hf in range(NHALF):
                P.dma("pool", (lambda e, cb=cb, hf=hf: e.dma_start(out=wo2[hf], in_=w_r[:, :, cb + hf * CB:cb + (hf + 1) * CB])), writes=["wo%d" % hf])
            for it in range(ntile_all):
                r0 = it * 128
                nr = min(128, TT - r0)
                og = ogt[ti % 2]
                ogk = ogn[ti % 2]
                xo = x1t[ti % 2]
                banks = obank[ti % 2]
                ti += 1
                P.dma("sp", (lambda e, og=og, r0=r0, nr=nr: e.dma_start(out=og[:, :, :nr], in_=og_d[r0 // 128].rearrange("p (h t) -> p h t", t=128)[:, :, :nr])), reads=["og_d"], writes=[ogk])
                P.dma("sp", (lambda e, xo=xo, r0=r0, nr=nr, cb=cb: e.dma_start(out=xo[:nr, :], in_=src_d[r0:r0 + nr, cb:cb + CW])), reads=[skey], writes=[xo.name])
                for hf in range(NHALF):
                    pg = banks[hf]
                    for hh in range(H):
                        P.op("pe", (lambda e, og=og, hh=hh, nr=nr, pg=pg, hf=hf: e.matmul(pg[:nr, :CB], lhsT=og[:, hh, :nr], rhs=wo2[hf][:, hh, :], start=(hh == 0), stop=(hh == H - 1))),
                             reads=[ogk, "wo%d" % hf], writes=[pg.name])
                    P.op("dve", (lambda e, xo=xo, nr=nr, pg=pg, hf=hf: e.tensor_tensor(out=xo[:nr, hf * CB:(hf + 1) * CB], in0=xo[:nr, hf * CB:(hf + 1) * CB], in1=pg[:nr, :CB], op=ALU.add)), reads=[xo.name, pg.name], writes=[xo.name])
                P.dma("sp", (lambda e, xo=xo, r0=r0, nr=nr, cb=cb: e.dma_start(out=dst_d[r0:r0 + nr, cb:cb + CW], in_=xo[:nr, :])), reads=[xo.name], writes=[dkey])

    outproj(wout_d, x_d, "xin", o_x1, "x1src")

    if stop <= 10:
        raise _Stop()
    P.barrier()
    rms_phase(o_x1, wnb2_d, h1T_d, "h1T_d")

    P.barrier()
    whf = wh[:].rearrange("p a b c -> p (a b c)")
    wkv = whf[:, 0:KC * 512].rearrange("p (k n) -> p k n", n=512)
    h1t = [hTb[i][:, :, 0:128] for i in range(2)]
    h1n = ["hTb0", "hTb1"]
    Carver.fallback = sb if D < 1024 else None
    f32v = lambda t_: t_[:].bitcast(F32)
    CV = Carver([arena[:, :], Gbc[:], Ebc[:], Bbc[:], f32v(QT), f32v(KT), f32v(VT), f32v(sqb), f32v(QgT), f32v(KbT), f32v(ogT)])
    kvt = [CV.alloc("kvt%d" % i, [128, 512]) for i in range(2)]
    knb = CV.alloc("knb", [128, 2, DKN])
    ss2 = CV.alloc("ss2", [128, 4])
    sqj = CV.alloc("sqj", [128, DKN])
    P.dma("sp", lambda e: e.dma_start(out=knb[:], in_=knb_d[:, :, :]), writes=["knb"])
    win2_r = win2_d.rearrange("(kc p) n -> p kc n", p=128)
    P.dma("sp", lambda e: e.dma_start(out=o_swk[0:WLC - TS, :], in_=cwk_d[TS:WLC, :]), writes=["o_swk"])
    P.dma("sp", lambda e: e.dma_start(out=o_swv[0:WLC - TS, :], in_=cwv_d[TS:WLC, :]), writes=["o_swv"])
    specs = [("kc", c_kc, DKN, None, o_pck, o_sck, False), ("vc", c_vc, DVN, None, o_pcv, o_scv, False),
             ("ks", c_ks, DKN, 0, o_psk, o_ssk, False), ("vs", c_vs, DVN, None, o_psv, o_ssv, False),
             ("kw", c_kw, DKN, 1, o_pwk, o_swk, True), ("vw", c_vw, DVN, None, o_pwv, o_swv, True)]
    tix = [0]
    for (nm_, cbase, gw, nidx, outP, outS, iswin) in specs:
        gpc = max(1, min(G, 512 // gw))
        for g0 in range(0, G, gpc):
            ng = min(gpc, G - g0)
            ncol = ng * gw
            co_ = g0 * gw
            P.dma("pool", (lambda e, cbase=cbase, co_=co_, ncol=ncol: e.dma_start(out=wkv[:, :, :ncol], in_=win2_r[:, :, cbase + co_:cbase + co_ + ncol])), writes=["wh"])
            for it in range(ntile_all):
                r0 = it * 128
                nr = min(128, TT - r0)
                i2 = tix[0] % 2
                tix[0] += 1
                ht, hk, kt, pg = h1t[i2], h1n[i2], kvt[i2], psG[i2]
                P.dma("sp", (lambda e, ht=ht, r0=r0, nr=nr: e.dma_start(out=ht[:, :, :nr], in_=hblk(h1T_d, r0 // 128)[:, :, :nr])), reads=["h1T_d"], writes=[hk])
                for k in range(KC):
                    P.op("pe", (lambda e, ht=ht, k=k, nr=nr, pg=pg, ncol=ncol: e.matmul(pg[:nr, :ncol], lhsT=ht[:, k, :nr], rhs=wkv[:, k, :ncol], start=(k == 0), stop=(k == KC - 1))),
                         reads=[hk, "wh"], writes=[pg.name])
                if nidx is None:
                    P.op("act", (lambda e, kt=kt, pg=pg, nr=nr, ncol=ncol: e.activation(out=kt[:nr, :ncol], in_=pg[:nr, :ncol], func=AF.Copy)), reads=[pg.name], writes=[kt.name])
                else:
                    for gi in range(ng):
                        cs_ = slice(gi * gw, (gi + 1) * gw)
                        P.op("act", (lambda e, pg=pg, nr=nr, cs_=cs_, gi=gi: e.activation(out=sqj[:nr, :], in_=pg[:nr, cs_], func=AF.Square, accum_out=ss2[:nr, gi:gi + 1])),
                             reads=[pg.name], writes=["sqj", "ss2"])
                    P.op("act", (lambda e, nr=nr, ng=ng: e.activation(out=ss2[:nr, :ng], in_=ss2[:nr, :ng], func=AF.Sqrt, scale=1.0 / DKN, bias=EPS)), reads=["ss2"], writes=["ss2"])
                    P.op("dve", (lambda e, nr=nr, ng=ng: e.reciprocal(out=ss2[:nr, :ng], in_=ss2[:nr, :ng])), reads=["ss2"], writes=["ss2"])
                    for gi in range(ng):
                        cs_ = slice(gi * gw, (gi + 1) * gw)
                        P.op("dve", (lambda e, kt=kt, pg=pg, nr=nr, cs_=cs_, gi=gi, nidx=nidx: e.scalar_tensor_tensor(out=kt[:nr, cs_], in0=pg[:nr, cs_], scalar=ss2[:nr, gi:gi + 1], in1=knb[:nr, nidx, :], op0=ALU.mult, op1=ALU.mult)),
                             reads=[pg.name, "ss2", "knb"], writes=[kt.name])
                if it < NT:
                    if not iswin:
                        P.dma("sp", (lambda e, kt=kt, outP=outP, r0=r0, co_=co_, ncol=ncol: e.dma_start(out=outP[r0:r0 + 128, co_:co_ + ncol], in_=kt[:, :ncol])), reads=[kt.name], writes=["o_" + nm_])
                    elif r0 >= T - WL:
                        P.dma("sp", (lambda e, kt=kt, outP=outP, r0=r0, co_=co_, ncol=ncol: e.dma_start(out=outP[r0 - (T - WL):r0 - (T - WL) + 128, co_:co_ + ncol], in_=kt[:, :ncol])), reads=[kt.name], writes=["o_" + nm_])
                else:
                    ro = WLC - TS if iswin else 0
                    P.dma("sp", (lambda e, kt=kt, outS=outS, ro=ro, co_=co_, ncol=ncol: e.dma_start(out=outS[ro:ro + TS, co_:co_ + ncol], in_=kt[:TS, :ncol])), reads=[kt.name], writes=["o_s" + nm_])
                P.dma("pool", (lambda e, kt=kt, nm_=nm_, r0=r0, nr=nr, co_=co_, ncol=ncol: e.dma_start(out=kv_s[nm_][r0:r0 + nr, co_:co_ + ncol], in_=kt[:nr, :ncol])), reads=[kt.name], writes=["s_" + nm_])
    if stop <= 11:
        raise _Stop()

    P.barrier()
    CV.reset()
    HW_ = DKN + DVN
    wq2 = CV.alloc("wq2", [128, KC, 2 * HW_], BF16)
    wq = wh[:].rearrange("p a b c -> p (a b c)")[:, 0:KC * 512].rearrange("p (k n) -> p k n", n=512)
    qnw = CV.alloc("qnw", [128, 2])
    sqa = [CV.alloc("sqa%d" % i, [128, 128], BF16) for i in range(2)]
    sqb2 = [CV.alloc("sqb2%d" % i, [128, 128], BF16) for i in range(2)]
    rq = [CV.alloc("rq%d" % i, [128, 128]) for i in range(2)]
    qo = [CV.alloc("qo%d" % i, [128, 128], BF16) for i in range(2)]
    qo2 = [CV.alloc("qo2%d" % i, [128, 128], BF16) for i in range(2)]
    zo = [CV.alloc("zo%d" % i, [128, 128], BF16) for i in range(2)]
    go = [CV.alloc("go%d" % i, [128, 128]) for i in range(2)]
    P.dma("sp", lambda e: e.dma_start(out=qnw[:], in_=qnw_d[:, :]), writes=["qnw"])

    def b3_pair(h0):
        for u in range(2):
            hn = h0 + u
            P.dma("pool", (lambda e, u=u, hn=hn: e.dma_start(out=wq2[:, :, u * HW_:u * HW_ + DKN], in_=win2_r[:, :, hn * DKN:(hn + 1) * DKN])), writes=["wq2"])
            P.dma("pool", (lambda e, u=u, hn=hn: e.dma_start(out=wq2[:, :, u * HW_ + DKN:(u + 1) * HW_], in_=win2_r[:, :, c_z + hn * DVN:c_z + (hn + 1) * DVN])), writes=["wq2"])
        for it in range(ntile_all):
            r0 = it * 128
            n = min(128, TT - r0)
            i2 = tix[0] % 2
            tix[0] += 1
            ht, hk = h1t[i2], h1n[i2]
            P.dma("sp", (lambda e, ht=ht, r0=r0, n=n: e.dma_start(out=ht[:, :, :n], in_=hblk(h1T_d, r0 // 128)[:, :, :n])), reads=["h1T_d"], writes=[hk])
            for u in range(2):
                hn = h0 + u
                pq, pz, po = psG[u], psM[u], psM[2 + u]
                c0_ = u * HW_
                for k in range(KC):
                    P.op("pe", (lambda e, ht=ht, k=k, n=n, pq=pq, c0_=c0_: e.matmul(pq[:, 0:n], lhsT=wq2[:, k, c0_:c0_ + 128], rhs=ht[:, k, :n], start=(k == 0), stop=(k == KC - 1))), reads=[hk, "wq2"], writes=[pq.name])
                for k in range(KC):
                    P.op("pe", (lambda e, ht=ht, k=k, n=n, pq=pq, c0_=c0_: e.matmul(pq[:64, 128:128 + n], lhsT=wq2[:, k, c0_ + 128:c0_ + 192], rhs=ht[:, k, :n], start=(k == 0), stop=(k == KC - 1))), reads=[hk, "wq2"], writes=[pq.name])
                for k in range(KC):
                    P.op("pe", (lambda e, ht=ht, k=k, n=n, pz=pz, c0_=c0_: e.matmul(pz[:, 0:n], lhsT=wq2[:, k, c0_ + DKN:c0_ + HW_], rhs=ht[:, k, :n], start=(k == 0), stop=(k == KC - 1))), reads=[hk, "wq2"], writes=[pz.name])
                sa_, sb_, rq_ = sqa[u], sqb2[u], rq[u]
                P.op("act", (lambda e, pq=pq, n=n, sa_=sa_: e.activation(out=sa_[:, :n], in_=pq[:, 0:n], func=AF.Square)), reads=[pq.name], writes=[sa_.name])
                P.op("act", (lambda e, pq=pq, n=n, sb_=sb_: e.activation(out=sb_[:64, :n], in_=pq[:64, 128:128 + n], func=AF.Square)), reads=[pq.name], writes=[sb_.name])
                P.op("pe", (lambda e, n=n, po=po, sa_=sa_: e.matmul(po[:, 0:n], lhsT=ones_b[:, :], rhs=sa_[:, :n], start=True, stop=False)), reads=["ones_b", sa_.name], writes=[po.name])
                P.op("pe", (lambda e, n=n, po=po, sb_=sb_: e.matmul(po[:, 0:n], lhsT=ones_b[:64, :], rhs=sb_[:64, :n], start=False, stop=True)), reads=["ones_b", sb_.name], writes=[po.name])
                P.op("act", (lambda e, n=n, po=po, rq_=rq_: e.activation(out=rq_[:, :n], in_=po[:, 0:n], func=AF.Sqrt, scale=1.0, bias=DKN * EPS)), reads=[po.name], writes=[rq_.name])
                P.op("dve", (lambda e, n=n, rq_=rq_: e.reciprocal(out=rq_[:, :n], in_=rq_[:, :n])), reads=[rq_.name], writes=[rq_.name])
                qa_, qb_, zz_ = qo[u], qo2[u], zo[u]
                P.op("dve", (lambda e, pq=pq, n=n, qa_=qa_, rq_=rq_: e.scalar_tensor_tensor(out=qa_[:, :n], in0=pq[:, 0:n], scalar=qnw[:, 0:1], in1=rq_[:, :n], op0=ALU.mult, op1=ALU.mult)), reads=[pq.name, "qnw", rq_.name], writes=[qa_.name])
                P.op("dve", (lambda e, pq=pq, n=n, qb_=qb_, rq_=rq_: e.scalar_tensor_tensor(out=qb_[:64, :n], in0=pq[:64, 128:128 + n], scalar=qnw[:64, 1:2], in1=rq_[:64, :n], op0=ALU.mult, op1=ALU.mult)), reads=[pq.name, "qnw", rq_.name], writes=[qb_.name])
                P.op("act", (lambda e, pz=pz, n=n, zz_=zz_: e.activation(out=zz_[:, :n], in_=pz[:, 0:n], func=AF.Silu)), reads=[pz.name], writes=[zz_.name])
                P.dma("sp", (lambda e, qa_=qa_, r0=r0, n=n, hn=hn: e.dma_start(out=qT_s[hn, 0:128, r0:r0 + n], in_=qa_[:, :n])), reads=[qa_.name], writes=["qT_s"])
                P.dma("sp", (lambda e, qb_=qb_, r0=r0, n=n, hn=hn: e.dma_start(out=qT_s[hn, 128:192, r0:r0 + n], in_=qb_[:64, :n])), reads=[qb_.name], writes=["qT_s"])
                P.dma("sp", (lambda e, zz_=zz_, r0=r0, n=n, hn=hn: e.dma_start(out=zs_s[hn, :, r0:r0 + n], in_=zz_[:, :n])), reads=[zz_.name], writes=["zs_s"])

    for hn_ in range(0, HN, 2):
        b3_pair(hn_)
    NGL = 3 * HN
    P.dma("pool", (lambda e: e.dma_start(out=wq[:, :, 0:NGL], in_=win2_r[:, :, c_gl:c_gl + NGL])), writes=["wh"])
    for it in range(ntile_all):
        r0 = it * 128
        n = min(128, TT - r0)
        i2 = tix[0] % 2
        tix[0] += 1
        ht, hk, pq, gg = h1t[i2], h1n[i2], psG[i2], go[i2]
        P.dma("sp", (lambda e, ht=ht, r0=r0, n=n: e.dma_start(out=ht[:, :, :n], in_=hblk(h1T_d, r0 // 128)[:, :, :n])), reads=["h1T_d"], writes=[hk])
        for k in range(KC):
            P.op("pe", (lambda e, ht=ht, k=k, n=n, pq=pq: e.matmul(pq[:NGL, 0:n], lhsT=wq[:, k, 0:NGL], rhs=ht[:, k, :n], start=(k == 0), stop=(k == KC - 1))), reads=[hk, "wh"], writes=[pq.name])
        P.op("act", (lambda e, pq=pq, n=n, gg=gg: e.activation(out=gg[:NGL, :n], in_=pq[:NGL, 0:n], func=AF.Sigmoid)), reads=[pq.name], writes=[gg.name])
        P.dma("sp", (lambda e, gg=gg, r0=r0, n=n: e.dma_start(out=gT_s[:, r0:r0 + n], in_=gg[:NGL, :n])), reads=[gg.name], writes=["gT_s"])
    if stop <= 12:
        raise _Stop()

    P.barrier()
    flat32 = lambda t_, pat: t_[:].rearrange(pat).bitcast(F32)
    CV = Carver([arena[:, :], flat32(wh, "p a b c -> p (a b c)"), Gbc[:], Ebc[:], Bbc[:], flat32(hTb[0], "p a b -> p (a b)"), flat32(hTb[1], "p a b -> p (a b)"),
                 f32v(QT), f32v(KT), f32v(VT), f32v(sqb), f32v(QgT), f32v(KbT), f32v(ogT)])
    tmt = [CV.alloc("tmt%d" % i, [128, G * DKN], BF16) for i in range(2)]
    fmt = [CV.alloc("fmt%d" % i, [128, 2 * G, 128], BF16) for i in range(2)]

    def fm_tile(loader, gw, dst, col0, nr, wkey):
        i2 = tix[0] % 2
        tix[0] += 1
        tt_, ff_ = tmt[i2], fmt[i2]
        loader(tt_)
        nch = 2 if gw == DKN else 1
        for g in range(G):
            for c in range(nch):
                w_ = 128 if c == 0 else gw - 128
                j = g * nch + c
                P.op("pe", (lambda e, tt_=tt_, g=g, c=c, w_=w_, j=j: e.transpose(out=psT[:w_, j * 128:j * 128 + nr], in_=tt_[:nr, g * gw + c * 128:g * gw + c * 128 + w_], identity=ident_b[:nr, :nr])),
                     reads=[tt_.name, "ident_b"], writes=["psT"])
        nj = G * nch
        P.op("act", (lambda e, ff_=ff_, nj=nj: e.activation(out=ff_[:, 0:nj, :nr], in_=psT[:, 0:nj * 128].rearrange("p (j t) -> p j t", t=128)[:, :, :nr], func=AF.Copy)), reads=["psT"], writes=[ff_.name])
        for c in range(nch):
            w_ = 128 if c == 0 else gw - 128
            P.dma("sp", (lambda e, ff_=ff_, c=c, w_=w_: e.dma_start(out=dst[:, c * 128:c * 128 + w_, col0:col0 + nr].rearrange("g p t -> p g t"),
                                                                  in_=ff_[:w_, 0:nj, :nr].rearrange("p (g c) t -> p g c t", c=nch)[:, :, c, :])), reads=[ff_.name], writes=[wkey])
        return tt_

    def plain_loader(src_ap, rkey, gw, nr, eng="sp"):
        def ld(tt_):
            P.dma(eng, (lambda e: e.dma_start(out=tt_[:nr, 0:G * gw], in_=src_ap)), reads=[rkey], writes=[tt_.name])
        return ld

    for nm_ in ("kc", "ks", "kw"):
        for it in range(NT):
            fm_tile(plain_loader(kv_s[nm_][it * 128:(it + 1) * 128, :], "s_" + nm_, DKN, 128), DKN, fmK[nm_], it * 128, 128, "fm_" + nm_)
    for it in range(NT):
        fm_tile(plain_loader(kv_s["vc"][it * 128:(it + 1) * 128, :], "s_vc", DVN, 128), DVN, fmV, it * 128, 128, "fm_vc")

    I32 = mybir.dt.int32
    pti = CV.alloc("pti", [128, NPG], I32)
    ptf = CV.alloc("ptf", [128, NPG])
    idx = CV.alloc("idx", [128, NPG], I32)
    iot = CV.alloc("iot", [128, 1])
    zpad = CV.alloc("zpad", [128, 8], BF16)
    P.dma("sp", lambda e: e.dma_start(out=pti[:], in_=pt_d[0:1, :].partition_broadcast(128)), writes=["pti"])
    P.dma("sp", lambda e: e.dma_start(out=iot[:], in_=iota_d[:, :]), writes=["iot"])
    P.op("dve", lambda e: e.memset(zpad[:], 0.0), writes=["zpad"])
    P.op("dve", lambda e: e.tensor_copy(out=ptf[:], in_=pti[:]), reads=["pti"], writes=["ptf"])
    P.op("dve", lambda e: e.tensor_scalar(out=ptf[:], in0=ptf[:], scalar1=128.0, scalar2=None, op0=ALU.mult), reads=["ptf"], writes=["ptf"])
    P.op("dve", lambda e: e.tensor_scalar(out=ptf[:], in0=ptf[:], scalar1=iot[:, 0:1], scalar2=None, op0=ALU.add), reads=["ptf", "iot"], writes=["ptf"])
    P.op("dve", lambda e: e.tensor_copy(out=idx[:], in_=ptf[:]), reads=["ptf"], writes=["idx"])

    def gather_loader(pool_ap, gw, L):
        def ld(tt_):
            P.dma("pool", (lambda e: e.indirect_dma_start(out=tt_[:, 0:G * gw], out_offset=None, in_=pool_ap[:, :],
                                                          in_offset=bass.IndirectOffsetOnAxis(ap=idx[:, L:L + 1], axis=0))), reads=["idx"], writes=[tt_.name])
        return ld

    for L_ in range(NPG):
        fm_tile(gather_loader(pool_d["kc"], DKN, L_), DKN, sK["kc"], L_ * 128, 128, "sfm_kc")
        fm_tile(gather_loader(pool_d["ks"], DKN, L_), DKN, sK["ks"], L_ * 128, 128, "sfm_ks")
        fm_tile(gather_loader(pool_d["vc"], DVN, L_), DVN, sVcT, L_ * 128, 128, "sfm_vc")
        i2 = tix[0] % 2
        tix[0] += 1
        tt_ = tmt[i2]
        gather_loader(pool_d["vs"], DVN, L_)(tt_)
        P.dma("sp", (lambda e, tt_=tt_, L_=L_: e.dma_start(out=sVs[L_ * 128:(L_ + 1) * 128, :], in_=tt_[:, 0:VW_])), reads=[tt_.name], writes=["s_svs"])
    fm_tile(plain_loader(kv_s["kc"][T:T + TS, :], "s_kc", DKN, TS), DKN, sK["kc"], P0, TS, "sfm_kc")
    fm_tile(plain_loader(kv_s["ks"][T:T + TS, :], "s_ks", DKN, TS), DKN, sK["ks"], P0, TS, "sfm_ks")
    fm_tile(plain_loader(kv_s["vc"][T:T + TS, :], "s_vc", DVN, TS), DVN, sVcT, P0, TS, "sfm_vc")
    P.dma("sp", lambda e: e.dma_start(out=sVs[P0:P0 + TS, :], in_=kv_s["vs"][T:T + TS, :]), reads=["s_vs"], writes=["s_svs"])
    for g in range(G):
        P.dma("sp", (lambda e, g=g: e.dma_start(out=sK["kc"][g, 0:128, P0 + TS:P0 + 16], in_=zpad[:, :])), reads=["zpad"], writes=["sfm_kc"])
        P.dma("sp", (lambda e, g=g: e.dma_start(out=sK["kc"][g, 128:192, P0 + TS:P0 + 16], in_=zpad[:64, :])), reads=["zpad"], writes=["sfm_kc"])
        P.dma("sp", (lambda e, g=g: e.dma_start(out=sVcT[g, :, P0 + TS:P0 + 16], in_=zpad[:, :])), reads=["zpad"], writes=["sfm_vc"])
    for t0 in range(0, WLC, 128):
        fm_tile(plain_loader(cwk_d[t0:t0 + 128, :], "cwk", DKN, 128, eng="pool"), DKN, sKw, t0, 128, "sfm_kw")
    fm_tile(plain_loader(kv_s["kw"][T:T + TS, :], "s_kw", DKN, TS), DKN, sKw, WLC, TS, "sfm_kw")
    P.dma("pool", lambda e: e.dma_start(out=sVw[0:WLC, :], in_=cwv_d[:, :]), writes=["s_svw"])
    P.dma("sp", lambda e: e.dma_start(out=sVw[WLC:WLC + TS, :], in_=kv_s["vw"][T:T + TS, :]), reads=["s_vw"], writes=["s_svw"])

    P.barrier()
    CV.reset()
    w1ka = CV.alloc("w1ka", [128, 32, 256], BF16)
    w1kb = CV.alloc("w1kb", [128, 32, 256], BF16)
    w1v = CV.alloc("w1v", [128, 32, 256], BF16)
    w2k = CV.alloc("w2k", [128, 2, DKN], BF16)
    w2v = CV.alloc("w2v", [128, 2, DVN], BF16)
    peka = CV.alloc("peka", [128, 32], BF16)
    pekb = CV.alloc("pekb", [128, 32], BF16)
    pev = CV.alloc("pev", [128, 32], BF16)
    kncb = CV.alloc("kncb", [128, DKN])
    cpe = CV.alloc("cpe", [1, 2, 256])
    cpeh = CV.alloc("cpeh", [1, 2, 256], BF16)
    cpel = CV.alloc("cpel", [1, 2, 256], BF16)
    cpr = CV.alloc("cpr", [1, 2, 256])
    LR = 16 * 127 + 32
    rka = CV.alloc("rka", [128, LR], BF16)
    rkb = CV.alloc("rkb", [128, LR], BF16)
    rv = CV.alloc("rv", [128, LR], BF16)
    hid = CV.alloc("hid", [128, 2, 256], BF16)
    hidT = CV.alloc("hidT", [128, 4, 128], BF16)
    ckn = CV.alloc("ckn", [128, DKN], BF16)
    cvn = CV.alloc("cvn", [128, DVN], BF16)
    ckT = CV.alloc("ckT", [128, 2, 128], BF16)
    P.dma("pool", lambda e: e.dma_start(out=w1ka[:], in_=w1k_d[0:128, :, :]), writes=["w1ka"])
    P.dma("pool", lambda e: e.dma_start(out=w1kb[:64], in_=w1k_d[128:192, :, :]), writes=["w1kb"])
    P.dma("pool", lambda e: e.dma_start(out=w1v[:], in_=w1v_d[:, :, :]), writes=["w1v"])
    P.dma("pool", lambda e: e.dma_start(out=w2k[:], in_=w2k_d.rearrange("(c p) n -> p c n", p=128)), writes=["w2k"])
    P.dma("pool", lambda e: e.dma_start(out=w2v[:], in_=w2v_d.rearrange("(c p) n -> p c n", p=128)), writes=["w2v"])
    P.dma("pool", lambda e: e.dma_start(out=peka[:], in_=pek_d[0:128, :]), writes=["peka"])
    P.dma("pool", lambda e: e.dma_start(out=pekb[:64], in_=pek_d[128:192, :]), writes=["pekb"])
    P.dma("pool", lambda e: e.dma_start(out=pev[:], in_=pev_d[:, :]), writes=["pev"])
    P.dma("sp", lambda e: e.dma_start(out=kncb[:], in_=kncb_d[:, :]), writes=["kncb"])
    kparts = ((peka, w1ka, 128, "peka", "w1ka"), (pekb, w1kb, 64, "pekb", "w1kb"))
    nmm = 0
    for (pe_, w_, kk, pk, wk) in kparts:
        for l in range(32):
            P.op("pe", (lambda e, pe_=pe_, w_=w_, kk=kk, l=l, nmm=nmm: e.matmul(psM[3][0:1, 0:256], lhsT=pe_[:kk, l:l + 1], rhs=w_[:kk, l, :], start=(nmm == 0), stop=(nmm == 63))), reads=[pk, wk], writes=["psM3"])
            nmm += 1
    for l in range(32):
        P.op("pe", (lambda e, l=l: e.matmul(psM[3][0:1, 256:512], lhsT=pev[:, l:l + 1], rhs=w1v[:, l, :], start=(l == 0), stop=(l == 31))), reads=["pev", "w1v"], writes=["psM3"])
    P.op("act", lambda e: e.activation(out=cpe[0:1, :, :], in_=psM[3][0:1, 0:512].rearrange("p (a b) -> p a b", b=256), func=AF.Copy), reads=["psM3"], writes=["cpe"])
    P.op("dve", lambda e: e.tensor_copy(out=cpeh[0:1], in_=cpe[0:1]), reads=["cpe"], writes=["cpeh"])
    P.op("dve", lambda e: e.tensor_tensor(out=cpr[0:1], in0=cpe[0:1], in1=cpeh[0:1], op=ALU.subtract), reads=["cpe", "cpeh"], writes=["cpr"])
    P.op("dve", lambda e: e.tensor_copy(out=cpel[0:1], in_=cpr[0:1]), reads=["cpr"], writes=["cpel"])

    def compress_tile(srcK, srcV, g, n0, nb, dstK, dstV, L, kkey="fm_kc", vkey="fm_vc"):
        c0 = 16 * n0
        ln = min(16 * (nb - 1) + 32, L - c0)
        P.dma("sp", (lambda e: e.dma_start(out=rka[:, :ln], in_=srcK[g, 0:128, c0:c0 + ln])), reads=[kkey], writes=["rka"])
        P.dma("sp", (lambda e: e.dma_start(out=rkb[:64, :ln], in_=srcK[g, 128:192, c0:c0 + ln])), reads=[kkey], writes=["rkb"])
        P.dma("sp", (lambda e: e.dma_start(out=rv[:, :ln], in_=srcV[g, :, c0:c0 + ln])), reads=[vkey], writes=["rv"])
        nsg = nb + 1
        for (which, parts, col0, cidx) in (("k", ((rka, w1ka, 128, "rka", "w1ka"), (rkb, w1kb, 64, "rkb", "w1kb")), 0, 0), ("v", ((rv, w1v, 128, "rv", "w1v"),), 256, 1)):
            tot = 32 * len(parts) + 2
            i_ = 0
            for (r_, w_, kk, rk_, wk_) in parts:
                rview = r_[:, 0:16 * nsg].rearrange("p (n s) -> p n s", s=16)
                for l in range(32):
                    a0, l2 = (0, l) if l < 16 else (1, l - 16)
                    P.op("pe", (lambda e, rview=rview, w_=w_, kk=kk, l=l, a0=a0, l2=l2, i_=i_, tot=tot, col0=col0: e.matmul(psM[3][:nb, col0:col0 + 256], lhsT=rview[:kk, a0:a0 + nb, l2], rhs=w_[:kk, l, :], start=(i_ == 0), stop=False)),
                         reads=[rk_, wk_], writes=["psM3"])
                    i_ += 1
            P.op("pe", (lambda e, col0=col0, cidx=cidx: e.matmul(psM[3][:nb, col0:col0 + 256], lhsT=ones_b[0:1, :nb], rhs=cpeh[0:1, cidx, :], start=False, stop=False)), reads=["ones_b", "cpeh"], writes=["psM3"])
            P.op("pe", (lambda e, col0=col0, cidx=cidx: e.matmul(psM[3][:nb, col0:col0 + 256], lhsT=ones_b[0:1, :nb], rhs=cpel[0:1, cidx, :], start=False, stop=True)), reads=["ones_b", "cpel"], writes=["psM3"])
        P.op("act", (lambda e: e.activation(out=hid[:nb, :, :], in_=psM[3][:nb, 0:512].rearrange("p (a b) -> p a b", b=256), func=AF.Silu)), reads=["psM3"], writes=["hid"])
        for j in range(4):
            P.op("pe", (lambda e, j=j: e.transpose(out=psT[:, j * 128:j * 128 + nb], in_=hid[:nb, j // 2, (j % 2) * 128:(j % 2 + 1) * 128], identity=ident_b[:nb, :nb])), reads=["hid", "ident_b"], writes=["psT"])
        P.op("act", (lambda e: e.activation(out=hidT[:, :, :nb], in_=psT[:, 0:512].rearrange("p (j t) -> p j t", t=128)[:, :, :nb], func=AF.Copy)), reads=["psT"], writes=["hidT"])
        for c in range(2):
            P.op("pe", (lambda e, c=c: e.matmul(psM[2][:nb, 0:DKN], lhsT=hidT[:, c, :nb], rhs=w2k[:, c, :], start=(c == 0), stop=(c == 1))), reads=["hidT", "w2k"], writes=["psM2"])
        for c in range(2):
            P.op("pe", (lambda e, c=c: e.matmul(psM[2][:nb, 256:256 + DVN], lhsT=hidT[:, 2 + c, :nb], rhs=w2v[:, c, :], start=(c == 0), stop=(c == 1))), reads=["hidT", "w2v"], writes=["psM2"])
        P.op("act", (lambda e: e.activation(out=sqC[:nb, :], in_=psM[2][:nb, 0:DKN], func=AF.Square, accum_out=ssC[:nb, 0:1])), reads=["psM2"], writes=["sqC", "ssC"])
        P.op("act", (lambda e: e.activation(out=ssC[:nb, 0:1], in_=ssC[:nb, 0:1], func=AF.Sqrt, scale=1.0 / DKN, bias=EPS)), reads=["ssC"], writes=["ssC"])
        P.op("dve", (lambda e: e.reciprocal(out=ssC[:nb, 0:1], in_=ssC[:nb, 0:1])), reads=["ssC"], writes=["ssC"])
        P.op("dve", (lambda e: e.scalar_tensor_tensor(out=ckn[:nb, :], in0=psM[2][:nb, 0:DKN], scalar=ssC[:nb, 0:1], in1=kncb[:nb, :], op0=ALU.mult, op1=ALU.mult)), reads=["psM2", "ssC", "kncb"], writes=["ckn"])
        P.op("act", (lambda e: e.activation(out=cvn[:nb, :], in_=psM[2][:nb, 256:256 + DVN], func=AF.Copy)), reads=["psM2"], writes=["cvn"])
        P.op("pe", (lambda e: e.transpose(out=psT[:, 0:nb], in_=ckn[:nb, 0:128], identity=ident_b[:nb, :nb])), reads=["ckn", "ident_b"], writes=["psT"])
        P.op("pe", (lambda e: e.transpose(out=psT[:64, 128:128 + nb], in_=ckn[:nb, 128:192], identity=ident_b[:nb, :nb])), reads=["ckn", "ident_b"], writes=["psT"])
        P.op("act", (lambda e: e.activation(out=ckT[:, :, :nb], in_=psT[:, 0:256].rearrange("p (j t) -> p j t", t=128)[:, :, :nb], func=AF.Copy)), reads=["psT"], writes=["ckT"])
        P.dma("sp", (lambda e: e.dma_start(out=dstK[g, 0:128, n0:n0 + nb], in_=ckT[:, 0, :nb])), reads=["ckT"], writes=["ck_s"])
        P.dma("sp", (lambda e: e.dma_start(out=dstK[g, 128:192, n0:n0 + nb], in_=ckT[:64, 1, :nb])), reads=["ckT"], writes=["ck_s"])
        P.dma("sp", (lambda e: e.dma_start(out=dstV[g, n0:n0 + nb, :], in_=cvn[:nb, :])), reads=["cvn"], writes=["cv_s"])

    ssC = CV.alloc("ssC", [128, 4])
    sqC = CV.alloc("sqC", [128, DKN])
    for g_ in range(G):
        compress_tile(fmK["kc"], fmV, g_, 0, NCP, ckT_s, cv_s, T)
    for g_ in range(G):
        for n0_ in range(0, NCS, 128):
            compress_tile(sK["kc"], sVcT, g_, n0_, 128, ckTS_s, cvS_s, LS, kkey="sfm_kc", vkey="sfm_vc")
    if stop <= 13:
        raise _Stop()

    P.barrier()
    CV.reset()
    BIGM = 30000.0
    maskP = CV.alloc("maskP", [128, 8 + NQT, 512], BF16)
    eone = CV.alloc("eone", [128, 64, 128], BF16)
    oacc = CV.alloc("oacc", [128, 8, 512])
    qa = CV.alloc("qa", [128, 8, 512], BF16)
    qb = CV.alloc("qb", [128, 8, 512], BF16)
    gbc = CV.alloc("gbc", [128, 3, 512])
    pf = CV.alloc("pf", [128, 512])
    pn = CV.alloc("pn", [128, 512])
    rden = CV.alloc("rden", [128, 512])
    tmpo = CV.alloc("tmpo", [128, 512])
    ksa = CV.alloc("ksa", [128, T], BF16)
    ksb = CV.alloc("ksb", [128, T], BF16)
    kwa = CV.alloc("kwa", [128, T], BF16)
    kwb = CV.alloc("kwb", [128, T], BF16)
    vsr = CV.alloc("vsr", [128, NT, 128], BF16)
    vwr = CV.alloc("vwr", [128, NT, 128], BF16)
    ptb = [CV.alloc("pt%d" % i, [128, 512], BF16) for i in range(2)]
    phi = CV.alloc("phi", [128, 512], BF16)
    plo = CV.alloc("plo", [128, 512], BF16)
    zt = CV.alloc("zt", [128, 512], BF16)
    ogo = CV.alloc("ogo", [128, 512], BF16)
    nmT = CV.alloc("nmT", [128, 512], BF16)
    cka = CV.alloc("cka", [128, 128], BF16)
    ckb = CV.alloc("ckb", [128, 128], BF16)
    cvr = CV.alloc("cvr", [128, 128], BF16)
    movP = CV.alloc("movP", [128, 32], BF16)
    fmP = CV.alloc("fmP", [128, NT, 32])
    vmP = CV.alloc("vmP", [128, NT, 32])
    i3 = CV.alloc("i3", [128, 32])
    i4 = CV.alloc("i4", [128, 32])
    m16 = CV.alloc("m16", [128, 16])
    negm = CV.alloc("negm", [128, 32], BF16)
    P.dma("sp", lambda e: e.dma_start(out=maskP[:], in_=maskP_d[:, :, :]), writes=["maskP"])
    P.dma("sp", lambda e: e.dma_start(out=eone[:], in_=eone_d[:, :, :]), writes=["eone"])
    P.dma("sp", lambda e: e.dma_start(out=movP[:], in_=movP_d[:, :]), writes=["movP"])
    P.dma("sp", lambda e: e.dma_start(out=fmP[:], in_=fmP_d[:, :, :]), writes=["fmP"])
    P.dma("sp", lambda e: e.dma_start(out=vmP[:], in_=vmP_d[:, :, :]), writes=["vmP"])
    sbank = [0]

    def attend(qa_ap, qb_ap, N, Ka, Kb, nk, extras, Vap, first, last, kkeys, qkeys=("qa", "qb")):
        pg = psG[sbank[0] % 2]
        pt_ = ptb[sbank[0] % 2]
        sbank[0] += 1
        P.op("pe", (lambda e: e.matmul(pg[:nk, :N], lhsT=Ka, rhs=qa_ap, start=True, stop=False)), reads=kkeys + [qkeys[0]], writes=[pg.name])
        P.op("pe", (lambda e: e.matmul(pg[:nk, :N], lhsT=Kb, rhs=qb_ap, start=False, stop=(len(extras) == 0))), reads=kkeys + [qkeys[1]], writes=[pg.name])
        for i_, (l_, r_, ks_) in enumerate(extras):
            P.op("pe", (lambda e, l_=l_, r_=r_, i_=i_: e.matmul(pg[:nk, :N], lhsT=l_, rhs=r_, start=False, stop=(i_ == len(extras) - 1))), reads=ks_, writes=[pg.name])
        P.op("act", (lambda e: e.activation(out=pt_[:nk, :N], in_=pg[:nk, :N], func=AF.Exp)), reads=[pg.name], writes=[pt_.name])
        P.op("pe", (lambda e: e.matmul(psM[0][:, :N], lhsT=Vap, rhs=pt_[:nk, :N], start=first, stop=last)), reads=kkeys + [pt_.name], writes=["psM0"])
        P.op("pe", (lambda e: e.matmul(psM[1][:, :N], lhsT=ones_b[:nk, :], rhs=pt_[:nk, :N], start=first, stop=last)), reads=["ones_b", pt_.name], writes=["psM1"])
        return pg, pt_

    def finish_branch(N, gate_ap, dst_ap, accumulate, dkey, gkey="gbc", W=None):
        rd, tp = (rden, tmpo) if W is None else W
        rk_, tk_ = ("rden", "tmpo") if W is None else ("rdenS", "tmpoS")
        P.op("dve", (lambda e: e.tensor_scalar(out=rd[:, :N], in0=psM[1][:, :N], scalar1=1e-30, scalar2=None, op0=ALU.max)), reads=["psM1"], writes=[rk_])
        P.op("dve", (lambda e: e.reciprocal(out=rd[:, :N], in_=rd[:, :N])), reads=[rk_], writes=[rk_])
        P.op("pool", (lambda e: e.tensor_tensor(out=tp[:, :N], in0=rd[:, :N], in1=gate_ap, op=ALU.mult)), reads=[rk_, gkey], writes=[tk_])
        if accumulate:
            P.op("dve", (lambda e: e.tensor_tensor(out=tp[:, :N], in0=psM[0][:, :N], in1=tp[:, :N], op=ALU.mult)), reads=["psM0", tk_], writes=[tk_])
            P.op("pool", (lambda e: e.tensor_tensor(out=dst_ap, in0=dst_ap, in1=tp[:, :N], op=ALU.add)), reads=[tk_, dkey], writes=[dkey])
        else:
            P.op("dve", (lambda e: e.tensor_tensor(out=dst_ap, in0=psM[0][:, :N], in1=tp[:, :N], op=ALU.mult)), reads=["psM0", tk_], writes=[dkey])

    def prompt_group(g):
        P.dma("sp", (lambda e: e.dma_start(out=ksa[:], in_=fmK["ks"][g, 0:128, :])), reads=["fm_ks"], writes=["ksa"])
        P.dma("sp", (lambda e: e.dma_start(out=ksb[:64], in_=fmK["ks"][g, 128:192, :])), reads=["fm_ks"], writes=["ksb"])
        P.dma("sp", (lambda e: e.dma_start(out=ksb[64:71], in_=kaugP_d[:, :])), writes=["ksb"])
        P.dma("sp", (lambda e: e.dma_start(out=kwa[:], in_=fmK["kw"][g, 0:128, :])), reads=["fm_kw"], writes=["kwa"])
        P.dma("sp", (lambda e: e.dma_start(out=kwb[:64], in_=fmK["kw"][g, 128:192, :])), reads=["fm_kw"], writes=["kwb"])
        P.dma("sp", (lambda e: e.dma_start(out=kwb[64:71], in_=kaugP_d[:, :])), writes=["kwb"])
        P.dma("sp", (lambda e: e.dma_start(out=vsr[:], in_=kv_s["vs"][0:T, g * DVN:(g + 1) * DVN].rearrange("(n p) d -> p n d", p=128))), reads=["s_vs"], writes=["vsr"])
        P.dma("sp", (lambda e: e.dma_start(out=vwr[:], in_=kv_s["vw"][0:T, g * DVN:(g + 1) * DVN].rearrange("(n p) d -> p n d", p=128))), reads=["s_vw"], writes=["vwr"])
        P.dma("sp", (lambda e: e.dma_start(out=cka[:, :NCP], in_=ckT_s[g, 0:128, 0:NCP])), reads=["ck_s"], writes=["cka"])
        P.dma("sp", (lambda e: e.dma_start(out=ckb[:64, :NCP], in_=ckT_s[g, 128:192, 0:NCP])), reads=["ck_s"], writes=["ckb"])
        P.dma("sp", (lambda e: e.dma_start(out=ckb[64:71, :NCP], in_=kaugC_d[:, 0:NCP])), writes=["ckb"])
        P.dma("sp", (lambda e: e.dma_start(out=cvr[:NCP, :], in_=cv_s[g, 0:NCP, :])), reads=["cv_s"], writes=["cvr"])
        for qt in range(NQT):
            prompt_qtile(g, qt)

    def prompt_qtile(g, qt):
        q0 = 512 * qt
        hs = slice(g * 8, (g + 1) * 8)
        P.dma("sp", (lambda e: e.dma_start(out=qa[:], in_=qT_s[hs, 0:128, q0:q0 + 512].rearrange("h p t -> p h t"))), reads=["qT_s"], writes=["qa"])
        P.dma("sp", (lambda e: e.dma_start(out=qb[:64], in_=qT_s[hs, 128:192, q0:q0 + 512].rearrange("h p t -> p h t"))), reads=["qT_s"], writes=["qb"])
        P.dma("sp", (lambda e: e.dma_start(out=qb[64:71], in_=qaugP_d[:, hs, q0:q0 + 512])), writes=["qb"])

        def load_gate(hn):
            P.dma("sp", (lambda e: e.dma_start(out=gbc[:], in_=gT_s[hn * 3:hn * 3 + 3, q0:q0 + 512].partition_broadcast(128))), reads=["gT_s"], writes=["gbc"])

        for r in range(8):
            hn = g * 8 + r
            load_gate(hn)
            pg, pt_ = attend(qa[:, r, :], qb[:71, r, :], 512, cka[:, :NCP], ckb[:71, :NCP], NCP,
                             [(ident_b[:NCP, :NCP], maskP[:NCP, 8 + qt, :], ["ident_b", "maskP"])], cvr[:NCP, :], True, True, ["cka", "ckb", "cvr"])
            P.op("act", (lambda e, pg=pg: e.activation(out=pf[:NCP, :], in_=pg[:NCP, :512], func=AF.Exp)), reads=[pg.name], writes=["pf"])
            finish_branch(512, gbc[:, 0, :], oacc[:, r, :], False, "oacc")
            P.op("dve", (lambda e: e.tensor_tensor(out=pn[:NCP, :], in0=pf[:NCP, :], in1=rden[:NCP, :], op=ALU.mult)), reads=["pf", "rden"], writes=["pn"])
            P.op("dve", (lambda e: e.tensor_copy(out=phi[:NCP, :], in_=pn[:NCP, :])), reads=["pn"], writes=["phi"])
            P.op("dve", (lambda e: e.tensor_tensor(out=pn[:NCP, :], in0=pn[:NCP, :], in1=phi[:NCP, :], op=ALU.subtract)), reads=["pn", "phi"], writes=["pn"])
            P.op("dve", (lambda e: e.tensor_copy(out=plo[:NCP, :], in_=pn[:NCP, :])), reads=["pn"], writes=["plo"])
            for qs in range(4):
                P.op("pe", (lambda e, qs=qs, r=r: e.matmul(psM[2][:, qs * 32:(qs + 1) * 32], lhsT=phi[:NCP, qs * 128:(qs + 1) * 128], rhs=movP[:NCP, :], start=(r == 0), stop=False)), reads=["phi", "movP"], writes=["psM2"])
                P.op("pe", (lambda e, qs=qs, r=r: e.matmul(psM[2][:, qs * 32:(qs + 1) * 32], lhsT=plo[:NCP, qs * 128:(qs + 1) * 128], rhs=movP[:NCP, :], start=False, stop=(r == 7))), reads=["plo", "movP"], writes=["psM2"])
        for qs in range(4):
            itq = qt * 4 + qs
            P.op("dve", (lambda e, qs=qs, itq=itq: e.tensor_tensor(out=i3[:], in0=psM[2][:, qs * 32:(qs + 1) * 32], in1=fmP[:, itq, :], op=ALU.max)), reads=["psM2", "fmP"], writes=["i3"])
            P.op("dve", (lambda e, itq=itq: e.tensor_tensor(out=i3[:], in0=i3[:], in1=vmP[:, itq, :], op=ALU.min)), reads=["i3", "vmP"], writes=["i3"])
            P.op("dve", (lambda e: e.max(out=m16[:, 0:8], in_=i3[:])), reads=["i3"], writes=["m16"])
            P.op("dve", (lambda e: e.match_replace(out=i4[:], in_to_replace=m16[:, 0:8], in_values=i3[:], imm_value=-3e38)), reads=["i3", "m16"], writes=["i4"])
            P.op("dve", (lambda e: e.max(out=m16[:, 8:16], in_=i4[:])), reads=["i4"], writes=["m16"])
            P.op("dve", (lambda e: e.tensor_scalar(out=i4[:], in0=i3[:], scalar1=m16[:, 15:16], scalar2=BIGM, op0=ALU.is_ge, op1=ALU.mult)), reads=["i3", "m16"], writes=["i4"])
            P.op("dve", (lambda e: e.tensor_scalar(out=negm[:], in0=i4[:], scalar1=-BIGM, scalar2=None, op0=ALU.add)), reads=["i4"], writes=["negm"])
            P.op("pe", (lambda e, qs=qs: e.transpose(out=psT[:32, qs * 128:(qs + 1) * 128], in_=negm[:, :], identity=ident_b[:, :])), reads=["negm", "ident_b"], writes=["psT"])
        P.op("act", (lambda e: e.activation(out=nmT[:32, :], in_=psT[:32, 0:512], func=AF.Copy)), reads=["psT"], writes=["nmT"])
        for r in range(8):
            hn = g * 8 + r
            load_gate(hn)
            P.dma("sp", (lambda e, hn=hn: e.dma_start(out=zt[:], in_=zs_s[hn, :, q0:q0 + 512])), reads=["zs_s"], writes=["zt"])
            kts = list(range(0, 4 * qt + 4))
            for kt in kts:
                ex = [(eone[:32, kt, :], nmT[:32, :], ["eone", "nmT"])]
                if kt >= 4 * qt:
                    ex.append((ident_b[:, :], maskP[:, kt - 4 * qt, :], ["ident_b", "maskP"]))
                attend(qa[:, r, :], qb[:71, r, :], 512, ksa[:, kt * 128:(kt + 1) * 128], ksb[:71, kt * 128:(kt + 1) * 128], 128, ex, vsr[:, kt, :], kt == kts[0], kt == kts[-1], ["ksa", "ksb", "vsr"])
            finish_branch(512, gbc[:, 1, :], oacc[:, r, :], True, "oacc")
            kts = list(range(max(0, 4 * qt - 4), 4 * qt + 4))
            for kt in kts:
                d_ = kt - 4 * qt
                mi = d_ if d_ >= 0 else 8 + d_
                ex = [(ident_b[:, :], maskP[:, mi, :], ["ident_b", "maskP"])]
                attend(qa[:, r, :], qb[:71, r, :], 512, kwa[:, kt * 128:(kt + 1) * 128], kwb[:71, kt * 128:(kt + 1) * 128], 128, ex, vwr[:, kt, :], kt == kts[0], kt == kts[-1], ["kwa", "kwb", "vwr"])
            finish_branch(512, gbc[:, 2, :], oacc[:, r, :], True, "oacc")
            P.op("dve", (lambda e, r=r: e.tensor_tensor(out=ogo[:], in0=oacc[:, r, :], in1=zt[:], op=ALU.mult)), reads=["oacc", "zt"], writes=["ogo"])
            P.dma("sp", (lambda e, hn=hn: e.dma_start(out=og_d[qt * 4:(qt + 1) * 4].rearrange("n p (h t) -> p n h t", t=128)[:, :, hn, :], in_=ogo[:].rearrange("p (n t) -> p n t", t=128))), reads=["ogo"], writes=["og_d"])

    for g_ in range(G):
        prompt_group(g_)

    P.barrier()
    CV.reset()
    rdenS = CV.alloc("rdenS", [128, 512])
    tmpoS = CV.alloc("tmpoS", [128, 512])
    WS = (rdenS, tmpoS)
    ptb2 = [CV.alloc("ptS%d" % i, [128, 512], BF16) for i in range(2)]
    ptb[0], ptb[1] = ptb2[0], ptb2[1]
    eoneS = CV.alloc("eoneS", [128, 64, 128], BF16)
    maskS = CV.alloc("maskS", [128, 3, 64], BF16)
    movS = CV.alloc("movS", [128, NCS // 128, NSB], BF16)
    fmS = CV.alloc("fmS", [TS, NSB])
    vmS = CV.alloc("vmS", [TS, NSB])
    qaS = CV.alloc("qaS", [128, 8, TS], BF16)
    qbS = CV.alloc("qbS", [128, 8, TS], BF16)
    gbcS = CV.alloc("gbcS", [128, 3, 64])
    ztS = CV.alloc("ztS", [128, 8, TS], BF16)
    oaccS = CV.alloc("oaccS", [128, 64])
    ogoS = CV.alloc("ogoS", [128, 8, TS], BF16)
    ckaS = CV.alloc("ckaS", [128, NCS], BF16)
    ckbS = CV.alloc("ckbS", [128, NCS], BF16)
    cvS = CV.alloc("cvS", [128, NCS // 128, 128], BF16)
    pfS = CV.alloc("pfS", [128, NCS // 128, 64])
    pnS = CV.alloc("pnS", [128, 64])
    phiS = CV.alloc("phiS", [128, 64], BF16)
    ploS = CV.alloc("ploS", [128, 64], BF16)
    i3S = CV.alloc("i3S", [TS, NSB])
    i4S = CV.alloc("i4S", [TS, NSB])
    m16S = CV.alloc("m16S", [TS, 16])
    negmS = CV.alloc("negmS", [TS, NCK * 128 + 128], BF16)
    nmTS = CV.alloc("nmTS", [128, NCK, 64], BF16)
    kcaS = [CV.alloc("kcaS%d" % i, [128, 1024], BF16) for i in range(2)]
    kcbS = [CV.alloc("kcbS%d" % i, [128, 1024], BF16) for i in range(2)]
    vcS = [CV.alloc("vcS%d" % i, [128, 8, 128], BF16) for i in range(2)]
    knaS = CV.alloc("knaS", [128, TS], BF16)
    knbS = CV.alloc("knbS", [128, TS], BF16)
    vnS = CV.alloc("vnS", [TS, 128], BF16)
    kwaS = CV.alloc("kwaS", [128, WLC + TS], BF16)
    kwbS = CV.alloc("kwbS", [128, WLC + TS], BF16)
    vwS = CV.alloc("vwS", [128, 5, 128], BF16)
    P.dma("sp", lambda e: e.dma_start(out=eoneS[:], in_=eone_d[:, :, :]), writes=["eoneS"])
    P.dma("sp", lambda e: e.dma_start(out=maskS[:], in_=maskS_d[:, :, :]), writes=["maskS"])
    P.dma("sp", lambda e: e.dma_start(out=movS[:], in_=movS_d[:, :, :]), writes=["movS"])
    P.dma("sp", lambda e: e.dma_start(out=fmS[:], in_=fmS_d[:, :]), writes=["fmS"])
    P.dma("sp", lambda e: e.dma_start(out=vmS[:], in_=vmS_d[:, :]), writes=["vmS"])
    QK_S = ("qaS", "qbS")
    NCT = NCS // 128

    def sample_group(g):
        hs = slice(g * 8, (g + 1) * 8)
        gc_ = slice(g * DVN, (g + 1) * DVN)
        qaf = qaS[:].rearrange("p a b -> p (a b)")
        qbf = qbS[:].rearrange("p a b -> p (a b)")
        P.dma("sp", (lambda e: e.dma_start(out=qaS[:], in_=qT_s[hs, 0:128, T:T + TS].rearrange("h p t -> p h t"))), reads=["qT_s"], writes=["qaS"])
        P.dma("sp", (lambda e: e.dma_start(out=qbS[:64], in_=qT_s[hs, 128:192, T:T + TS].rearrange("h p t -> p h t"))), reads=["qT_s"], writes=["qbS"])
        P.dma("sp", (lambda e: e.dma_start(out=qbS[64:71], in_=qaugS_d[:, hs, :])), writes=["qbS"])
        for x_ in range(3):
            P.dma("sp", (lambda e, x_=x_: e.dma_start(out=gbcS[:, x_, :].rearrange("p (r q) -> p r q", q=TS), in_=gT_s[g * 24:(g + 1) * 24, T:T + TS].rearrange("(r x) q -> x r q", x=3)[x_].partition_broadcast(128))), reads=["gT_s"], writes=["gbcS"])
        P.dma("sp", (lambda e: e.dma_start(out=ztS[:], in_=zs_s[hs, :, T:T + TS].rearrange("h p t -> p h t"))), reads=["zs_s"], writes=["ztS"])
        P.dma("sp", (lambda e: e.dma_start(out=ckaS[:], in_=ckTS_s[g, 0:128, :])), reads=["ck_s"], writes=["ckaS"])
        P.dma("sp", (lambda e: e.dma_start(out=ckbS[:64], in_=ckTS_s[g, 128:192, :])), reads=["ck_s"], writes=["ckbS"])
        P.dma("sp", (lambda e: e.dma_start(out=ckbS[64:71], in_=kaugCS_d[:, :])), writes=["ckbS"])
        P.dma("sp", (lambda e: e.dma_start(out=cvS[:], in_=cvS_s[g].rearrange("(n p) d -> p n d", p=128))), reads=["cv_s"], writes=["cvS"])
        for ct in range(NCT):
            ex = [(ident_b[:, :], maskS[:, 2, :], ["ident_b", "maskS"])] if ct == NCT - 1 else []
            pg, pt_ = attend(qaf, qbf[:71], 64, ckaS[:, ct * 128:(ct + 1) * 128], ckbS[:71, ct * 128:(ct + 1) * 128], 128, ex, cvS[:, ct, :], ct == 0, ct == NCT - 1, ["ckaS", "ckbS", "cvS"], QK_S)
            P.op("act", (lambda e, pg=pg, ct=ct: e.activation(out=pfS[:, ct, :], in_=pg[:, :64], func=AF.Exp)), reads=[pg.name], writes=["pfS"])
        finish_branch(64, gbcS[:, 0, :], oaccS[:, :], False, "oaccS", "gbcS", WS)
        for ct in range(NCT):
            P.op("dve", (lambda e, ct=ct: e.tensor_tensor(out=pnS[:], in0=pfS[:, ct, :], in1=rdenS[:, :64], op=ALU.mult)), reads=["pfS", "rdenS"], writes=["pnS"])
            P.op("dve", (lambda e: e.tensor_copy(out=phiS[:], in_=pnS[:])), reads=["pnS"], writes=["phiS"])
            P.op("dve", (lambda e: e.tensor_tensor(out=pnS[:], in0=pnS[:], in1=phiS[:], op=ALU.subtract)), reads=["pnS", "phiS"], writes=["pnS"])
            P.op("dve", (lambda e: e.tensor_copy(out=ploS[:], in_=pnS[:])), reads=["pnS"], writes=["ploS"])
            for r in range(8):
                first = (ct == 0 and r == 0)
                last = (ct == NCT - 1 and r == 7)
                P.op("pe", (lambda e, ct=ct, r=r, first=first: e.matmul(psM[2][:TS, 0:NSB], lhsT=phiS[:, r * TS:(r + 1) * TS], rhs=movS[:, ct, :], start=first, stop=False)), reads=["phiS", "movS"], writes=["psM2"])
                P.op("pe", (lambda e, ct=ct, r=r, last=last: e.matmul(psM[2][:TS, 0:NSB], lhsT=ploS[:, r * TS:(r + 1) * TS], rhs=movS[:, ct, :], start=False, stop=last)), reads=["ploS", "movS"], writes=["psM2"])
        P.op("dve", (lambda e: e.tensor_tensor(out=i3S[:], in0=psM[2][:TS, 0:NSB], in1=fmS[:], op=ALU.max)), reads=["psM2", "fmS"], writes=["i3S"])
        P.op("dve", (lambda e: e.tensor_tensor(out=i3S[:], in0=i3S[:], in1=vmS[:], op=ALU.min)), reads=["i3S", "vmS"], writes=["i3S"])
        P.op("dve", (lambda e: e.max(out=m16S[:, 0:8], in_=i3S[:])), reads=["i3S"], writes=["m16S"])
        P.op("dve", (lambda e: e.match_replace(out=i4S[:], in_to_replace=m16S[:, 0:8], in_values=i3S[:], imm_value=-3e38)), reads=["i3S", "m16S"], writes=["i4S"])
        P.op("dve", (lambda e: e.max(out=m16S[:, 8:16], in_=i4S[:])), reads=["i4S"], writes=["m16S"])
        P.op("dve", (lambda e: e.tensor_scalar(out=i4S[:], in0=i3S[:], scalar1=m16S[:, 15:16], scalar2=BIGM, op0=ALU.is_ge, op1=ALU.mult)), reads=["i3S", "m16S"], writes=["i4S"])
        P.op("dve", (lambda e: e.tensor_scalar(out=negmS[:, 0:NSB], in0=i4S[:], scalar1=-BIGM, scalar2=None, op0=ALU.add)), reads=["i4S"], writes=["negmS"])
        for c in range(NCK):
            cw = min(128, NBP - 128 * c)
            P.op("pe", (lambda e, c=c, cw=cw: e.transpose(out=psT[:cw, c * 128:c * 128 + TS], in_=negmS[:, 128 * c:128 * c + cw], identity=ident_b[:TS, :TS])), reads=["negmS", "ident_b"], writes=["psT"])
            for r in range(8):
                eng = "act" if r % 2 == 0 else "dve"
                if eng == "act":
                    P.op("act", (lambda e, c=c, cw=cw, r=r: e.activation(out=nmTS[:cw, c, r * TS:(r + 1) * TS], in_=psT[:cw, c * 128:c * 128 + TS], func=AF.Copy)), reads=["psT"], writes=["nmTS"])
                else:
                    P.op("dve", (lambda e, c=c, cw=cw, r=r: e.tensor_copy(out=nmTS[:cw, c, r * TS:(r + 1) * TS], in_=psT[:cw, c * 128:c * 128 + TS])), reads=["psT"], writes=["nmTS"])
        nchunk = (NPG + 7) // 8
        for ch in range(nchunk):
            i2 = ch % 2
            k0 = ch * 1024
            nt_ = min(8, NPG - ch * 8)
            ka_, kb_, v_ = kcaS[i2], kcbS[i2], vcS[i2]
            P.dma("sp", (lambda e, ka_=ka_, k0=k0, nt_=nt_: e.dma_start(out=ka_[:, :nt_ * 128], in_=sK["ks"][g, 0:128, k0:k0 + nt_ * 128])), reads=["sfm_ks"], writes=[ka_.name])
            P.dma("sp", (lambda e, kb_=kb_, k0=k0, nt_=nt_: e.dma_start(out=kb_[:64, :nt_ * 128], in_=sK["ks"][g, 128:192, k0:k0 + nt_ * 128])), reads=["sfm_ks"], writes=[kb_.name])
            P.dma("sp", (lambda e, kb_=kb_, k0=k0, nt_=nt_: e.dma_start(out=kb_[64:71, :nt_ * 128], in_=kaugS_d[:, k0:k0 + nt_ * 128])), writes=[kb_.name])
            P.dma("sp", (lambda e, v_=v_, k0=k0, nt_=nt_: e.dma_start(out=v_[:, :nt_, :], in_=sVs[k0:k0 + nt_ * 128, gc_].rearrange("(n p) d -> p n d", p=128))), reads=["s_svs"], writes=[v_.name])
            for j in range(nt_):
                kt = ch * 8 + j
                c = (2 * kt) // 128
                cw = min(128, NBP - 128 * c)
                ex = [(eoneS[:cw, kt % 64, :], nmTS[:cw, c, :], ["eoneS", "nmTS"])]
                attend(qaf, qbf[:71], 64, ka_[:, j * 128:(j + 1) * 128], kb_[:71, j * 128:(j + 1) * 128], 128, ex, v_[:, j, :], kt == 0, False, [ka_.name, kb_.name, v_.name], QK_S)
        P.dma("sp", (lambda e: e.dma_start(out=knaS[:], in_=sK["ks"][g, 0:128, P0:P0 + TS])), reads=["sfm_ks"], writes=["knaS"])
        P.dma("sp", (lambda e: e.dma_start(out=knbS[:64], in_=sK["ks"][g, 128:192, P0:P0 + TS])), reads=["sfm_ks"], writes=["knbS"])
        P.dma("sp", (lambda e: e.dma_start(out=knbS[64:71], in_=kaugS_d[:, P0:P0 + TS])), writes=["knbS"])
        P.dma("sp", (lambda e: e.dma_start(out=vnS[:], in_=sVs[P0:P0 + TS, gc_])), reads=["s_svs"], writes=["vnS"])
        attend(qaf, qbf[:71], 64, knaS[:, :], knbS[:71, :], TS, [(ident_b[:TS, :TS], maskS[:TS, 0, :], ["ident_b", "maskS"])], vnS[:, :], False, True, ["knaS", "knbS", "vnS"], QK_S)
        finish_branch(64, gbcS[:, 1, :], oaccS[:, :], True, "oaccS", "gbcS", WS)
        P.dma("sp", (lambda e: e.dma_start(out=kwaS[:], in_=sKw[g, 0:128, :])), reads=["sfm_kw"], writes=["kwaS"])
        P.dma("sp", (lambda e: e.dma_start(out=kwbS[:64], in_=sKw[g, 128:192, :])), reads=["sfm_kw"], writes=["kwbS"])
        P.dma("sp", (lambda e: e.dma_start(out=kwbS[64:71], in_=kaugS_d[:, P0 - WLC:P0 + TS])), writes=["kwbS"])
        P.dma("sp", (lambda e: e.dma_start(out=vwS[:, 0:4, :], in_=sVw[0:WLC, gc_].rearrange("(n p) d -> p n d", p=128))), reads=["s_svw"], writes=["vwS"])
        P.dma("sp", (lambda e: e.dma_start(out=vwS[:TS, 4, :], in_=sVw[WLC:WLC + TS, gc_])), reads=["s_svw"], writes=["vwS"])
        for j in range(5):
            nk = 128 if j < 4 else TS
            ex = []
            if j == 0:
                ex = [(ident_b[:, :], maskS[:, 1, :], ["ident_b", "maskS"])]
            if j == 4:
                ex = [(ident_b[:TS, :TS], maskS[:TS, 0, :], ["ident_b", "maskS"])]
            attend(qaf, qbf[:71], 64, kwaS[:, j * 128:j * 128 + nk], kwbS[:71, j * 128:j * 128 + nk], nk, ex, vwS[:nk, j, :], j == 0, j == 4, ["kwaS", "kwbS", "vwS"], QK_S)
        finish_branch(64, gbcS[:, 2, :], oaccS[:, :], True, "oaccS", "gbcS", WS)
        P.op("dve", (lambda e: e.tensor_tensor(out=ogoS[:].rearrange("p a b -> p (a b)"), in0=oaccS[:, :], in1=ztS[:].rearrange("p a b -> p (a b)"), op=ALU.mult)), reads=["oaccS", "ztS"], writes=["ogoS"])
        P.dma("sp", (lambda e: e.dma_start(out=og_d[NT].rearrange("p (h t) -> p h t", t=128)[:, hs, 0:TS], in_=ogoS[:])), reads=["ogoS"], writes=["og_d"])

    if not cfg.get("nosample"):
        for g_ in range(G):
            sample_group(g_)
    if stop <= 14:
        raise _Stop()
    P.barrier()
    outproj(wout2_d, o_x1, "x1src", o_y, "o_y")


def simulate_deadlock(P):
    sem = {}
    pc = {e: 0 for e in ENGS}
    progress = True
    while progress:
        progress = False
        for e in ENGS:
            while pc[e] < len(P.streams[e]):
                waits, fn, sk, inc = P.streams[e][pc[e]]
                if all(sem.get(k, 0) >= v for k, v in waits):
                    sem[sk] = sem.get(sk, 0) + inc
                    pc[e] += 1
                    progress = True
                else:
                    break
    stuck = {e: (pc[e], len(P.streams[e])) for e in ENGS if pc[e] < len(P.streams[e])}
    return stuck, sem


def _chan(a):
    return np.ascontiguousarray(a.T.reshape(-1, 128, a.shape[0]).transpose(1, 0, 2))


def _unchan(a):
    return np.ascontiguousarray(a.transpose(2, 1, 0).reshape(a.shape[2], -1))


def kernel(x_prompt, x_sample, state_delta, state_conv, cache_cmp_k, cache_cmp_v,
           cache_slc_k, cache_slc_v, cache_win_k, cache_win_v, page_table,
           norm_dn, w_in_dn, conv_w_dn, a_log_dn, dt_bias_dn, out_norm_dn, w_out_dn,
           norm_nsa, w_in_nsa, q_norm_nsa, k_norm_cmp, k_norm_slc, k_norm_win,
           cmp_pe_k, cmp_w1_k, cmp_w2_k, cmp_pe_v, cmp_w1_v, cmp_w2_v, w_out_nsa):
    f = lambda a: np.ascontiguousarray(np.asarray(a, dtype=np.float32))
    B, T, D = x_prompt.shape
    NS, TS, _ = x_sample.shape
    H = a_log_dn.shape[1]
    cfg = dict(D=D, H=H, T=T, TB=128, G=cache_cmp_k.shape[3], NPG=page_table.shape[1], NPOOL=cache_cmp_k.shape[1])
    nc = build_program(cfg)
    cst = host_consts()
    shared = {
        "norm_dn_b": np.ascontiguousarray(np.broadcast_to(f(norm_dn[0]), (128, D))),
        "w_in_dn": f(w_in_dn[0]), "conv_w": _chan(f(conv_w_dn[0])),
        "a_log_b": np.ascontiguousarray(np.broadcast_to(f(a_log_dn[0]), (128, H))),
        "dt_bias_b": np.ascontiguousarray(np.broadcast_to(f(dt_bias_dn[0]), (128, H))),
        "out_norm_c": f(out_norm_dn[0]).reshape(128, 1), "w_out_dn": f(w_out_dn[0]),
    }
    for k, v in cst.items():
        shared["c_" + k] = v
    shared["norm_nsa_b"] = np.ascontiguousarray(np.broadcast_to(f(norm_nsa[0]), (128, D)))
    shared["w_in_nsa"] = f(w_in_nsa[0])
    shared["knorm_b"] = np.ascontiguousarray(np.broadcast_to(np.stack([f(k_norm_slc[0]), f(k_norm_win[0])]), (128, 2, 192)))
    HNn = q_norm_nsa.shape[1] and (w_out_nsa.shape[1] // 128)
    shared.update(nsa_consts(T, HNn))
    qn = f(q_norm_nsa[0])
    qc = np.zeros((128, 2), np.float32)
    qc[:, 0] = qn[0:128]
    qc[:64, 1] = qn[128:192]
    shared["qnorm_c"] = qc
    shared["w1k_t"] = np.ascontiguousarray(f(cmp_w1_k[0]).transpose(1, 0, 2))
    shared["w1v_t"] = np.ascontiguousarray(f(cmp_w1_v[0]).transpose(1, 0, 2))
    shared["w2k"] = f(cmp_w2_k[0])
    shared["w2v"] = f(cmp_w2_v[0])
    shared["pek_t"] = np.ascontiguousarray(f(cmp_pe_k[0]).T)
    shared["pev_t"] = np.ascontiguousarray(f(cmp_pe_v[0]).T)
    shared["knc_b"] = np.ascontiguousarray(np.broadcast_to(f(k_norm_cmp[0]), (128, 192)))
    shared["w_out_nsa"] = f(w_out_nsa[0])
    NPGn = page_table.shape[1]
    NPOOLn = cache_cmp_k.shape[1]
    shared.update(nsa_consts_sample(HNn, NPGn))
    shared["pool_ck"] = f(cache_cmp_k[0]).reshape(NPOOLn * 128, -1)
    shared["pool_cv"] = f(cache_cmp_v[0]).reshape(NPOOLn * 128, -1)
    shared["pool_sk"] = f(cache_slc_k[0]).reshape(NPOOLn * 128, -1)
    shared["pool_sv"] = f(cache_slc_v[0]).reshape(NPOOLn * 128, -1)
    in_maps = []
    for c in range(8):
        m = dict(shared)
        m["x"] = np.concatenate([f(x_prompt[c % B]), f(x_sample[c])], 0)
        m["conv0_in"] = _chan(f(state_conv[0, c]))
        m["s0"] = f(state_delta[0, c])
        m["page_tab"] = np.ascontiguousarray(np.asarray(page_table[c], dtype=np.int32).reshape(1, -1))
        m["cwin_k"] = f(cache_win_k[0, c]).reshape(cache_win_k.shape[2], -1)
        m["cwin_v"] = f(cache_win_v[0, c]).reshape(cache_win_v.shape[2], -1)
        in_maps.append(m)
    res = run_bass_kernel_spmd(nc, in_maps, core_ids=list(range(8))).results
    y_prompt = np.stack([res[b]["o_y"][:T] for b in range(B)])
    y_sample = np.stack([res[c]["o_y"][T:] for c in range(NS)])
    p_sd = np.stack([res[b]["o_sdP"] for b in range(B)])[None]
    p_sc = np.stack([_unchan(res[b]["o_scP"]) for b in range(B)])[None]
    s_sd = np.stack([res[c]["o_sdS"] for c in range(NS)])[None]
    s_sc = np.stack([_unchan(res[c]["o_scS"]) for c in range(NS)])[None]
    G, DK, DV = cache_cmp_k.shape[3], cache_cmp_k.shape[4], cache_cmp_v.shape[4]
    WL = cache_win_k.shape[2]
    z = lambda *s: np.zeros(s, np.float32)
    wlp = min(512, T)
    gp = lambda key, n, dd: np.stack([res[b][key].reshape(n, G, dd) for b in range(B)])[None]
    gs = lambda key, n, dd: np.stack([res[c][key].reshape(n, G, dd) for c in range(NS)])[None]
    return (y_prompt.astype(np.float32), y_sample.astype(np.float32), p_sd, p_sc,
            gp("o_pck", T, DK), gp("o_pcv", T, DV), gp("o_psk", T, DK), gp("o_psv", T, DV),
            gp("o_pwk", wlp, DK), gp("o_pwv", wlp, DV),
            s_sd, s_sc,
            gs("o_sck", TS, DK), gs("o_scv", TS, DV), gs("o_ssk", TS, DK), gs("o_ssv", TS, DV),
            gs("o_swk", WL, DK), gs("o_swv", WL, DV))
```
